# Optimizing a Trainium2 kernel written in Bass

```python
import math
import jax, jax.numpy as jnp
from jax import lax
import numpy as np

D_MODEL = 2048
BATCH = 2
SEQ = 4096
DEPTH = 2

N_A_LAYERS = DEPTH // 2
N_B_LAYERS = DEPTH - N_A_LAYERS
CHUNK = 128
SGU_GROUPS = 16
SGU_GROUP_DIM = D_MODEL // SGU_GROUPS
D_FF = -(-8 * D_MODEL // (3 * 256)) * 256
HEAD_DIM = 128
DIFF_HEADS = D_MODEL // (2 * HEAD_DIM)
N_SUB = 2 * DIFF_HEADS
V_DIM = 2 * HEAD_DIM
ROPE_THETA = 10000.0
Q_BLOCK = 128
EPS = 1e-6

kernel_name = 'yoco_gmlp_diffattn_adaln'


def rms_norm(x, g):
    xf = x.astype(jnp.float32)
    y = xf * lax.rsqrt(jnp.mean(xf * xf, axis=-1, keepdims=True) + EPS)
    return (y * g.astype(jnp.float32)).astype(x.dtype)


def layer_norm(x, g, b):
    xf = x.astype(jnp.float32)
    mu = jnp.mean(xf, axis=-1, keepdims=True)
    var = jnp.mean(jnp.square(xf - mu), axis=-1, keepdims=True)
    y = (xf - mu) * lax.rsqrt(var + EPS)
    return (y * g.astype(jnp.float32) + b.astype(jnp.float32)).astype(x.dtype)


def modulate(h, shift, scale):
    return h * (1.0 + scale[:, None, :]) + shift[:, None, :]


def ada_mod(c, w, b, n):
    m = jax.nn.silu(c) @ w + b
    return jnp.split(m, n, axis=-1)


def swiglu(h, wi, wo):
    g, u = jnp.split(h @ wi, 2, axis=-1)
    return (jax.nn.silu(g) * u) @ wo


def rope_tables(positions):
    inv_freq = 1.0 / (ROPE_THETA ** (jnp.arange(0, HEAD_DIM, 2, dtype=jnp.float32) / HEAD_DIM))
    ang = positions.astype(jnp.float32)[..., None] * inv_freq
    return jnp.cos(ang), jnp.sin(ang)


def apply_rope(x, cos, sin):
    xf = x.astype(jnp.float32)
    x1, x2 = jnp.split(xf, 2, axis=-1)
    cs = cos[:, :, None, :]
    sn = sin[:, :, None, :]
    return jnp.concatenate([x1 * cs - x2 * sn, x2 * cs + x1 * sn], axis=-1).astype(x.dtype)


def gmlp_layer(x, c, ada_w, ada_b, norm1_g, in_w, in_b, ln_g, ln_b, sgu_w, sgu_b, out_w,
               norm2_g, ffn_wi, ffn_wo):
    sh1, sc1, g1, sh2, sc2, g2 = ada_mod(c, ada_w, ada_b, 6)
    bsz, seq, _ = x.shape
    h = modulate(rms_norm(x, norm1_g), sh1, sc1)
    z = jax.nn.gelu(h @ in_w + in_b, approximate=False)
    u, v = jnp.split(z, 2, axis=-1)
    v = layer_norm(v, ln_g, ln_b)
    v = v.reshape(bsz, seq // CHUNK, CHUNK, SGU_GROUPS, SGU_GROUP_DIM)
    w = sgu_w * jnp.tril(jnp.ones((CHUNK, CHUNK), sgu_w.dtype))
    sv = jnp.einsum('gts,bnsgc->bntgc', w, v) + sgu_b.T[None, None, :, :, None]
    y = (u * sv.reshape(bsz, seq, D_MODEL)) @ out_w
    x = x + g1[:, None, :] * y
    h = modulate(rms_norm(x, norm2_g), sh2, sc2)
    return x + g2[:, None, :] * swiglu(h, ffn_wi, ffn_wo)


def shared_kv(x, c, kv_ada_w, kv_ada_b, kv_norm_g, kv_w, k_norm_g, cos, sin):
    bsz, seq, _ = x.shape
    sh, sc = ada_mod(c, kv_ada_w, kv_ada_b, 2)
    h = modulate(rms_norm(x, kv_norm_g), sh, sc)
    kv = h @ kv_w
    k = kv[..., :D_MODEL].reshape(bsz, seq, N_SUB, HEAD_DIM)
    k = apply_rope(rms_norm(k, k_norm_g), cos, sin)
    v = kv[..., D_MODEL:].reshape(bsz, seq, DIFF_HEADS, V_DIM)
    return k, v


def diff_attention(q, k, v, lam):
    bsz, seq, _, _ = q.shape
    n_blk = seq // Q_BLOCK
    scale = HEAD_DIM ** -0.5
    kh = k.transpose(0, 2, 1, 3)
    vh = v.transpose(0, 2, 1, 3)
    qb = q.transpose(0, 2, 1, 3).reshape(bsz, N_SUB, n_blk, Q_BLOCK, HEAD_DIM)
    qb = qb.transpose(2, 0, 1, 3, 4)
    key_pos = jnp.arange(seq)
    neg = jnp.finfo(jnp.float32).min

    def one_block(args):
        q_blk, i = args
        s = jnp.einsum('bhqd,bhkd->bhqk', q_blk, kh).astype(jnp.float32) * scale
        q_pos = i * Q_BLOCK + jnp.arange(Q_BLOCK)
        s = jnp.where(key_pos[None, :] <= q_pos[:, None], s, neg)
        p = jax.nn.softmax(s, axis=-1).reshape(bsz, DIFF_HEADS, 2, Q_BLOCK, seq)
        p = p[:, :, 0] - lam * p[:, :, 1]
        return jnp.einsum('bhqk,bhkd->bhqd', p.astype(vh.dtype), vh)

    out = lax.map(one_block, (qb, jnp.arange(n_blk)))
    return out.transpose(1, 0, 3, 2, 4).reshape(bsz, seq, DIFF_HEADS, V_DIM)


def diff_layer(x, c, k, v, cos, sin, layer_idx, ada_w, ada_b, norm1_g, q_w, q_norm_g,
               lq1, lk1, lq2, lk2, subln_g, o_w, norm2_g, ffn_wi, ffn_wo):
    sh1, sc1, g1, sh2, sc2, g2 = ada_mod(c, ada_w, ada_b, 6)
    bsz, seq, _ = x.shape
    h = modulate(rms_norm(x, norm1_g), sh1, sc1)
    q = (h @ q_w).reshape(bsz, seq, N_SUB, HEAD_DIM)
    q = apply_rope(rms_norm(q, q_norm_g), cos, sin)
    lam_init = 0.8 - 0.6 * math.exp(-0.3 * layer_idx)
    f32 = jnp.float32
    lam = (jnp.exp(jnp.sum(lq1.astype(f32) * lk1.astype(f32)))
           - jnp.exp(jnp.sum(lq2.astype(f32) * lk2.astype(f32))) + lam_init)
    o = diff_attention(q, k, v, lam)
    o = rms_norm(o, subln_g) * (1.0 - lam_init)
    y = o.reshape(bsz, seq, D_MODEL) @ o_w
    x = x + g1[:, None, :] * y
    h = modulate(rms_norm(x, norm2_g), sh2, sc2)
    return x + g2[:, None, :] * swiglu(h, ffn_wi, ffn_wo)


def setup_inputs(seed: int = 0) -> dict:
    key = jax.random.key(seed)
    ks = iter(jax.random.split(key, 48))
    f32 = jnp.float32

    def nrm(shape, fan_in, mult=1.0):
        return jax.random.normal(next(ks), shape, f32) * (mult * fan_in ** -0.5)

    def gain(shape):
        return 1.0 + 0.05 * jax.random.normal(next(ks), shape, f32)

    def bias(shape, s=0.02):
        return s * jax.random.normal(next(ks), shape, f32)

    D, NA, NB = D_MODEL, N_A_LAYERS, N_B_LAYERS
    x = jax.random.normal(next(ks), (BATCH, SEQ, D), f32)
    c = jax.random.normal(next(ks), (BATCH, D), f32)
    offs = jax.random.randint(next(ks), (BATCH, 1), 0, 1024, dtype=jnp.int32)
    positions = jnp.arange(SEQ, dtype=jnp.int32)[None, :] + offs
    return {
        'x': x, 'c': c, 'positions': positions,
        'a_ada_w': nrm((NA, D, 6 * D), D, 0.5), 'a_ada_b': bias((NA, 6 * D)),
        'a_norm1_g': gain((NA, D)),
        'a_in_w': nrm((NA, D, 2 * D), D), 'a_in_b': bias((NA, 2 * D)),
        'a_sgu_ln_g': gain((NA, D)), 'a_sgu_ln_b': bias((NA, D)),
        'a_sgu_w': nrm((NA, SGU_GROUPS, CHUNK, CHUNK), CHUNK),
        'a_sgu_b': 1.0 + bias((NA, SGU_GROUPS, CHUNK), 0.05),
        'a_out_w': nrm((NA, D, D), D),
        'a_norm2_g': gain((NA, D)),
        'a_ffn_wi': nrm((NA, D, 2 * D_FF), D), 'a_ffn_wo': nrm((NA, D_FF, D), D_FF),
        'kv_ada_w': nrm((D, 2 * D), D, 0.5), 'kv_ada_b': bias((2 * D,)),
        'kv_norm_g': gain((D,)), 'kv_w': nrm((D, 2 * D), D), 'k_norm_g': gain((HEAD_DIM,)),
        'b_ada_w': nrm((NB, D, 6 * D), D, 0.5), 'b_ada_b': bias((NB, 6 * D)),
        'b_norm1_g': gain((NB, D)),
        'b_q_w': nrm((NB, D, D), D), 'b_q_norm_g': gain((NB, HEAD_DIM)),
        'b_lambda_q1': bias((NB, HEAD_DIM), 0.1), 'b_lambda_k1': bias((NB, HEAD_DIM), 0.1),
        'b_lambda_q2': bias((NB, HEAD_DIM), 0.1), 'b_lambda_k2': bias((NB, HEAD_DIM), 0.1),
        'b_subln_g': gain((NB, V_DIM)),
        'b_o_w': nrm((NB, D, D), D),
        'b_norm2_g': gain((NB, D)),
        'b_ffn_wi': nrm((NB, D, 2 * D_FF), D), 'b_ffn_wo': nrm((NB, D_FF, D), D_FF),
    }


def reference(x, c, positions,
              a_ada_w, a_ada_b, a_norm1_g, a_in_w, a_in_b, a_sgu_ln_g, a_sgu_ln_b,
              a_sgu_w, a_sgu_b, a_out_w, a_norm2_g, a_ffn_wi, a_ffn_wo,
              kv_ada_w, kv_ada_b, kv_norm_g, kv_w, k_norm_g,
              b_ada_w, b_ada_b, b_norm1_g, b_q_w, b_q_norm_g,
              b_lambda_q1, b_lambda_k1, b_lambda_q2, b_lambda_k2, b_subln_g, b_o_w,
              b_norm2_g, b_ffn_wi, b_ffn_wo):
    cos, sin = rope_tables(positions)
    k_sh = None
    v_sh = None
    for l in range(DEPTH):
        if l < N_A_LAYERS:
            x = gmlp_layer(x, c, a_ada_w[l], a_ada_b[l], a_norm1_g[l], a_in_w[l], a_in_b[l],
                           a_sgu_ln_g[l], a_sgu_ln_b[l], a_sgu_w[l], a_sgu_b[l], a_out_w[l],
                           a_norm2_g[l], a_ffn_wi[l], a_ffn_wo[l])
        else:
            if l == N_A_LAYERS:
                k_sh, v_sh = shared_kv(x, c, kv_ada_w, kv_ada_b, kv_norm_g, kv_w, k_norm_g,
                                       cos, sin)
            j = l - N_A_LAYERS
            x = diff_layer(x, c, k_sh, v_sh, cos, sin, l, b_ada_w[j], b_ada_b[j],
                           b_norm1_g[j], b_q_w[j], b_q_norm_g[j], b_lambda_q1[j],
                           b_lambda_k1[j], b_lambda_q2[j], b_lambda_k2[j], b_subln_g[j],
                           b_o_w[j], b_norm2_g[j], b_ffn_wi[j], b_ffn_wo[j])
    return x
```

```python
import math
import numpy as np
import ml_dtypes
import concourse.bass as bass
import concourse.mybir as mybir
from concourse.bass_utils import run_bass_kernel_spmd
from contextlib import ExitStack

F32 = mybir.dt.float32
BF16 = mybir.dt.bfloat16
I32 = mybir.dt.int32
ALU = mybir.AluOpType
AF = mybir.ActivationFunctionType

D = 2048
NCH = 16
T = 1024
DFF = 5632
NJ = 44
EPS = 1e-6
NSLOT = 5
ENGS = ("pe", "act", "dve", "pool", "sp")

COLS = {}
_o = 0
for _n, _w in [("a_norm1_g", 16), ("a_norm2_g", 16), ("a_in_b_u", 16), ("a_ln_g", 16), ("a_ln_b", 16), ("kv_norm_g", 16),
               ("b_norm1_g", 16), ("b_norm2_g", 16), ("a_ada_b", 96), ("kv_ada_b", 32), ("b_ada_b", 96),
               ("k_norm_g", 1), ("q_norm_g", 1), ("subln_g", 2), ("lam", 4), ("invf", 1), ("cT", 16)]:
    COLS[_n] = (_o, _w)
    _o += _w
NCOL = _o


class Buf:
    __slots__ = ("name", "w", "r")

    def __init__(self, name=""):
        self.name = name
        self.w = None
        self.r = {}


class Prog:
    def __init__(self, nc, es):
        self.nc = nc
        self.es = es
        self.q = {e: [] for e in ENGS}
        self.sem = {}
        self.cnt = {}
        self.seen = {e: {} for e in ENGS}
        for e in ENGS:
            self.new_sem("E_" + e)
        self.n_dma_sem = 0

    def new_sem(self, name):
        self.sem[name] = self.es.enter_context(self.nc.semaphore(name))
        self.cnt[name] = 0
        return name

    def dma_sem(self):
        self.n_dma_sem += 1
        return self.new_sem("D%d" % self.n_dma_sem)

    def op(self, eng, fn, reads=(), writes=(), dma=None, extra_waits=(), dma_inc=16):
        waits = {}

        def need(tok):
            if tok is None:
                return
            s, v = tok
            if waits.get(s, 0) < v:
                waits[s] = v

        for b in reads:
            need(b.w)
        for b in writes:
            need(b.w)
            for s, v in b.r.items():
                need((s, v))
        for t in extra_waits:
            need(t)
        if eng == "pe":
            waits.pop("E_pe", None)
        wl = []
        seen = self.seen[eng]
        for s, v in waits.items():
            if seen.get(s, 0) < v:
                seen[s] = v
                wl.append((s, v))
        if dma is not None:
            s = dma
            self.cnt[s] += dma_inc
            inc = (s, dma_inc)
        else:
            s = "E_" + eng
            self.cnt[s] += 1
            inc = (s, 1)
        tok = (s, self.cnt[s])
        self.q[eng].append((wl, fn, inc))
        for b in reads:
            if b.r.get(s, 0) < tok[1]:
                b.r[s] = tok[1]
        for b in writes:
            b.w = tok
            b.r = {}
        return tok

    def wait_only(self, eng, toks):
        wl = []
        seen = self.seen[eng]
        for s, v in toks:
            if seen.get(s, 0) < v:
                seen[s] = v
                wl.append((s, v))
        if wl:
            self.q[eng].append((wl, None, None))

    def barrier(self):
        toks = [(s, c) for s, c in self.cnt.items() if c > 0]
        for e in ENGS:
            self.wait_only(e, toks)

    def chain(self, eng, fns, reads=(), writes=()):
        for fn in fns:
            tok = self.op(eng, fn, reads=reads, writes=writes)
        return tok

    def emit(self):
        nc = self.nc
        block = self.es.enter_context(nc.Block())
        sem = self.sem

        def replay(engobj, items):
            for wl, fn, inc in items:
                for s, v in wl:
                    engobj.wait_ge(sem[s], v)
                if fn is not None:
                    ins = fn(engobj)
                    ins.then_inc(sem[inc[0]], inc[1])

        q = self.q

        @block.tensor
        def _(e):
            replay(e, q["pe"])

        @block.scalar
        def _(e):
            replay(e, q["act"])

        @block.vector
        def _(e):
            replay(e, q["dve"])

        @block.gpsimd
        def _(e):
            replay(e, q["pool"])

        @block.sync
        def _(e):
            replay(e, q["sp"])


class WStream:
    def __init__(self, P, nc, es, nslot):
        self.P = P
        self.nslot = nslot
        self.slots = [es.enter_context(nc.sbuf_tensor("s_wslot%d" % i, [128, 4096], BF16)) for i in range(nslot)]
        self.bufs = [Buf("wslot%d" % i) for i in range(nslot)]
        self.sems = [P.dma_sem() for _ in range(nslot)]
        self.plan = []
        self.issued = 0
        self.taken = 0
        self.closed = 0

    def add(self, key, src, shape):
        self.plan.append((key, src, shape))

    def pump(self):
        while self.issued < len(self.plan) and self.issued - self.nslot < self.closed:
            i = self.issued
            key, src, shape = self.plan[i]
            s = i % self.nslot
            dst = self.view(s, shape)
            self.P.op("pool", (lambda dst, src: (lambda e: e.dma_start(out=dst, in_=src)))(dst, src),
                      writes=[self.bufs[s]], dma=self.sems[s])
            self.issued += 1

    def view(self, s, shape):
        if shape == "k":
            return self.slots[s][:].rearrange("p (k f) -> p k f", k=16)
        else:
            return self.slots[s][:].rearrange("p (j f) -> p j f", j=2)

    def take(self, key):
        i = self.taken
        assert self.plan[i][0] == key, (self.plan[i][0], key)
        assert i - self.closed < self.nslot, "too many open weight units"
        self.pump()
        assert self.issued > i
        self.taken += 1
        s = i % self.nslot
        return self.view(s, self.plan[i][2]), self.bufs[s]

    def release(self):
        self.closed = self.taken
        self.pump()


def tile_w(W):
    K, Fd = W.shape
    return np.ascontiguousarray(W.reshape(K // 128, 128, Fd // 256, 256).transpose(2, 1, 0, 3))


def tile_wo(W):
    K, Fd = W.shape
    return np.ascontiguousarray(W.reshape(K // 256, 2, 128, Fd).transpose(0, 2, 1, 3))


def colv(v):
    v = np.asarray(v, np.float32).reshape(-1, 128)
    return np.ascontiguousarray(v.T)


def tok_index(j):
    idx = []
    for s in range(8):
        p = 8 * (s // 2) + (j if s % 2 == 0 else 7 - j)
        idx.append(np.arange(p * 128, (p + 1) * 128))
    return np.concatenate(idx)


LAM_INIT = 0.8 - 0.6 * math.exp(-0.3 * 1)


import os
STOP = int(os.environ.get("KSTOP", "99"))


class _Stop(Exception):
    pass


def stage(n):
    if STOP <= n:
        raise _Stop()


class Builder:
    def __init__(self, mode):
        self.mode = mode
        self.nc = bass.Bass("TRN2", target_bir_lowering=False)

    def sbt(self, name, shape, dt):
        self._uid = getattr(self, "_uid", 0) + 1
        return self.nc.sbuf_tensor("s_%s_%d" % (name, self._uid), list(shape), dt)

    def dram_in(self, name, shape, dt=F32):
        return self.nc.dram_tensor(name, list(shape), dt, kind="ExternalInput").ap()

    def dram_out(self, name, shape, dt=F32):
        return self.nc.dram_tensor(name, list(shape), dt, kind="ExternalOutput").ap()

    def build(self):
        nc = self.nc
        mode = self.mode
        with ExitStack() as es:
            self.es = es
            P = self.P = Prog(nc, es)
            sb = lambda n, s, d: es.enter_context(self.sbt("" + n, list(s), d))
            self.sb = sb
            xT_d = self.dram_in("xT", [D, T])
            cols_d = self.dram_in("cols", [128, NCOL])
            pos_d = self.dram_in("pos", [1, T], I32)
            out_d = self.dram_out("outT", [D, T])
            self.DBG = bool(int(os.environ.get("KDEBUG", "0")))
            if self.DBG:
                self.dbgK = self.dram_out("dbgK", [128, 4096], BF16)
                self.dbgV = self.dram_out("dbgV", [128, 4096], BF16)
                self.dbgA = self.dram_out("dbgA", [3, 1024, 1024], BF16)
                self.dbgL = self.dram_out("dbgL", [256, 1024], BF16)
            W = {}
            if mode in ("A", "F"):
                rows_d = self.dram_in("rows", [3, D])
                sguw_d = self.dram_in("sgu_w", [16, 128, 128])
                W["a_ada"] = self.dram_in("a_ada_w", [48, 128, 16, 256])
                W["a_in"] = self.dram_in("a_in_w", [16, 128, 16, 256])
                W["a_out"] = self.dram_in("a_out_w", [8, 128, 16, 256])
                W["a_wi"] = self.dram_in("a_ffn_wi", [44, 128, 16, 256])
                W["a_wo"] = self.dram_in("a_ffn_wo", [22, 128, 2, 2048])
                W["kv_ada"] = self.dram_in("kv_ada_w", [16, 128, 16, 256])
                W["kv"] = self.dram_in("kv_w", [16, 128, 16, 256])
            if mode == "A":
                KT_d = self.dram_out("KT", [16, 128, T], BF16)
                V_d = self.dram_out("V", [8, T, 256], BF16)
            if mode == "F":
                self.kt_loc = [nc.dram_tensor("kt_loc%d" % i, [256, T], BF16, kind="Internal").ap() for i in range(8)]
                self.kt_all = [nc.dram_tensor("kt_all%d" % i, [4 * 256, T], BF16, kind="Internal").ap() for i in range(8)]
                self.v_loc = [nc.dram_tensor("v_loc%d" % i, [256, T], BF16, kind="Internal").ap() for i in range(8)]
                self.v_all = [nc.dram_tensor("v_all%d" % i, [4 * 256, T], BF16, kind="Internal").ap() for i in range(8)]
                self.dm_loc = nc.dram_tensor("dm_loc", [128, 128], BF16, kind="Internal").ap()
                self.dm_all = nc.dram_tensor("dm_all", [4 * 128, 128], BF16, kind="Internal").ap()
                KT_d = [self.kt_loc[hh // 2][(hh % 2) * 128:(hh % 2) * 128 + 128, :] for hh in range(16)]
                V_d = [self.v_loc[u].rearrange("r (q d) -> (r q) d", d=256) for u in range(8)]
            if mode in ("B", "F"):
                W["b_ada"] = self.dram_in("b_ada_w", [48, 128, 16, 256])
                W["b_q"] = self.dram_in("b_q_w", [8, 128, 16, 256])
                W["b_o"] = self.dram_in("b_o_w", [8, 128, 16, 256])
                W["b_wi"] = self.dram_in("b_ffn_wi", [44, 128, 16, 256])
                W["b_wo"] = self.dram_in("b_ffn_wo", [22, 128, 2, 2048])
                mask_d = self.dram_in("mask", [128, 8, 128])
            if mode == "B":
                KTf_d = self.dram_in("KTf", [16, 128, 4096], BF16)
                Vf_d = self.dram_in("Vf", [8, 2, 128, 16, 256], BF16)
            if mode == "F":
                KTf_d = [self.kt_all[hh // 2].rearrange("(j s d) t -> s d j t", j=4, s=2, d=128)[hh % 2] for hh in range(16)]
                Vf_d = [[self.v_all[u].rearrange("(j r) (q d) -> j (r q) d", j=4, d=256)[j].rearrange("(s p) d -> p s d", p=128)
                         for j in range(4)] for u in range(8)]
            self.W = W

            self.xT = sb("xT", [128, NCH, T], F32)
            self.hT = sb("hT", [128, NCH, T], BF16)
            self.cols = sb("cols", [128, NCOL], F32)
            self.rstd = sb("rstd", [128, T], F32)
            self.ones = sb("ones", [128, 128], BF16)
            self.ones_f = sb("ones_f", [128, 128], F32)
            self.sc_bf = sb("sc_bf", [128, NCH], BF16)
            self.adaA = sb("adaA", [128, 96], F32)
            self.adaK = sb("adaK", [128, 32], F32)
            self.mods = sb("mods", [128, 4, 16], F32)
            self.tmpf = [sb("tmpf%d" % i, [128, 512], F32) for i in range(3)]
            self.sqb = [sb("sqb%d" % i, [128, 2, 512], BF16) for i in range(2)]
            self.eps_col = sb("eps_col", [128, 4], F32)
            B = self.B = {}
            for n in ["xT0", "xT1", "hT0", "hT1", "cols", "rstd0", "rstd1", "ones", "sc_bf", "adaA", "adaK", "mods",
                      "tmpf0", "tmpf1", "tmpf2", "sqb0", "sqb1", "out", "consts"]:
                B[n] = Buf(n)
            self.ps = [es.enter_context(nc.psum_tensor("ps%d" % i, [128, 512], F32)) for i in range(7)]
            self.psT = es.enter_context(nc.psum_tensor("psT", [128, 1024], BF16))
            self.Bps = [Buf("ps%d" % i) for i in range(7)]
            self.BpsT = Buf("psT")
            self.ws = WStream(P, nc, es, NSLOT)
            self.dl = P.dma_sem()
            self.dx = P.dma_sem()
            self.do = P.dma_sem()
            xT, cols, ones, ones_f, sc_bf = self.xT, self.cols, self.ones, self.ones_f, self.sc_bf

            ws = self.ws
            if mode in ("A", "F"):
                self.plan_ada("a_ada", 0)
                self.plan_ada("a_ada", 1)
                for u in range(8, 16):
                    ws.add(("a_in", u), W["a_in"][u], "k")
                for u in range(0, 8):
                    ws.add(("a_in", u), W["a_in"][u], "k")
                self.plan_ada("a_ada", 2)
                for u in range(8):
                    ws.add(("a_out", u), W["a_out"][u], "k")
                for g in (3, 4, 5):
                    self.plan_ada("a_ada", g)
                self.plan_ffn("a")
                self.plan_ada("kv_ada", 0)
                self.plan_ada("kv_ada", 1)
                for u in range(16):
                    ws.add(("kv", u), W["kv"][u], "k")
            if mode in ("B", "F"):
                self.plan_ada("b_ada", 0)
                self.plan_ada("b_ada", 1)
                for u in range(8):
                    ws.add(("b_q", u), W["b_q"][u], "k")
                for g in (2, 3, 4, 5):
                    self.plan_ada("b_ada", g)
                for u in range(8):
                    ws.add(("b_o", u), W["b_o"][u], "k")
                self.plan_ffn("b")

            P.op("sp", lambda e: e.dma_start(out=cols[:], in_=cols_d), writes=[B["cols"]], dma=P.dma_sem())
            for th in range(2):
                for cq in range(4):
                    src = xT_d.rearrange("(c p) t -> p c t", p=128)[:, 4 * cq:4 * cq + 4, th * 512:(th + 1) * 512]
                    dst = xT[:, 4 * cq:4 * cq + 4, th * 512:(th + 1) * 512]
                    P.op("sp", (lambda dst, src: (lambda e: e.dma_start(out=dst, in_=src)))(dst, src), dma=self.dx)
            B["xT0"].w = (self.dx, P.cnt[self.dx])
            B["xT1"].w = (self.dx, P.cnt[self.dx])
            P.op("dve", lambda e: e.memset(ones[:], 1.0), writes=[B["ones"]])
            P.op("dve", lambda e: e.memset(ones_f[:], 1.0), writes=[B["ones"]])
            P.op("dve", lambda e: e.memset(self.eps_col[:, 0:1], EPS), writes=[B["consts"]])
            P.op("dve", lambda e: e.memset(self.eps_col[:, 1:2], -3.1415920), writes=[B["consts"]])
            c0 = COLS["cT"][0]
            P.op("act", lambda e: e.activation(out=sc_bf[:], in_=cols[:, c0:c0 + 16], func=AF.Silu),
                 reads=[B["cols"]], writes=[B["sc_bf"]])

            self.Bktall = [Buf("ktall%d" % i) for i in range(8)]
            self.Bvall = [Buf("vall%d" % i) for i in range(8)]
            self.Bdummy = Buf("dummy")
            try:
                if mode in ("A", "F"):
                    self.layer_a(rows_d, sguw_d, pos_d, KT_d, V_d)
                if mode == "F" and self.DBG:
                    P.op("sp", lambda e: e.dma_start(out=self.dbgA[0], in_=self.kt_all[0]), reads=[self.Bktall[0]], dma=self.do)
                if mode in ("B", "F"):
                    self.layer_b(KTf_d, Vf_d, mask_d, pos_d)
                if mode == "F" and self.DBG:
                    P.op("sp", lambda e: e.dma_start(out=self.dbgA[2], in_=self.kt_all[0]), reads=[self.Bktall[0]], dma=self.do)
                    P.op("sp", lambda e: e.dma_start(out=self.dbgL, in_=self.kt_loc[0]), dma=self.do)
            except _Stop:
                ws.taken = len(ws.plan)

            for th in range(2):
                for cq in range(4):
                    dst = out_d.rearrange("(c p) t -> p c t", p=128)[:, 4 * cq:4 * cq + 4, th * 512:(th + 1) * 512]
                    src = xT[:, 4 * cq:4 * cq + 4, th * 512:(th + 1) * 512]
                    P.op("sp", (lambda dst, src: (lambda e: e.dma_start(out=dst, in_=src)))(dst, src),
                         reads=[B["xT%d" % th]], writes=[B["out"]], dma=self.do)
            P.wait_only("sp", [(self.do, P.cnt[self.do])] + [(d_, P.cnt[d_]) for d_ in getattr(self, "kv_out_sems", [])])
            assert ws.taken == len(ws.plan), (ws.taken, len(ws.plan))
            P.emit()
        return nc

    def col(self, name, i=0, n=1):
        o, w = COLS[name]
        return self.cols[:, o + i:o + i + n]

    def plan_ada(self, wname, g):
        for u in range(8):
            self.ws.add((wname, g * 8 + u), self.W[wname][g * 8 + u], "k")

    def plan_ffn(self, L):
        for grp in range(11):
            for qq in range(2):
                q = grp * 2 + qq
                self.ws.add((L + "_wi", q), self.W[L + "_wi"][q], "k")
                self.ws.add((L + "_wi", 22 + q), self.W[L + "_wi"][22 + q], "k")
            for qq in range(2):
                q = grp * 2 + qq
                self.ws.add((L + "_wo", q), self.W[L + "_wo"][q], "j")

    def ada_unit(self, wname, idx, dst, dbuf, bias_name):
        P, B = self.P, self.B
        bank = 6
        ps = self.ps[bank]
        g, u = idx // 8, idx % 8
        wv, wb = self.ws.take((wname, idx))

        def fn(e, wv=wv, u=u):
            for fc in range(2):
                c = u * 2 + fc
                for k in range(NCH):
                    ins = e.matmul(ps[:, c:c + 1], lhsT=wv[:, k, fc * 128:(fc + 1) * 128],
                                   rhs=self.sc_bf[:, k:k + 1], start=(k == 0), stop=(k == NCH - 1))
            return ins
        P.op("pe", fn, reads=[wb, B["sc_bf"]], writes=[self.Bps[bank]])
        self.ws.release()
        if u == 7:
            bo = COLS[bias_name][0] + g * 16
            P.op("dve", lambda e: e.tensor_tensor(out=dst[:, g * 16:(g + 1) * 16], in0=ps[:, 0:16],
                                                  in1=self.cols[:, bo:bo + 16], op=ALU.add),
                 reads=[self.Bps[bank], B["cols"]], writes=[dbuf])

    def ada_group(self, wname, g, dst, dbuf, bias_name):
        for u in range(8):
            self.ada_unit(wname, g * 8 + u, dst, dbuf, bias_name)

    def mk_gsc(self, slot, gname, sc_ap, src_buf):
        P, B = self.P, self.B
        o = COLS[gname][0]
        P.op("dve", lambda e: e.scalar_tensor_tensor(out=self.mods[:, slot, :], in0=sc_ap, scalar=1.0,
                                                     in1=self.cols[:, o:o + 16], op0=ALU.add, op1=ALU.mult),
             reads=[src_buf, B["cols"]], writes=[B["mods"]])

    def norm_stats(self):
        P, B = self.P, self.B
        xT = self.xT
        for th in range(2):
            bank = th
            for cq in range(8):
                sq = self.sqb[cq % 2]
                sqB = B["sqb%d" % (cq % 2)]
                P.op("act", lambda e, sq=sq, cq=cq, th=th: e.activation(
                    out=sq[:], in_=xT[:, 2 * cq:2 * cq + 2, th * 512:(th + 1) * 512], func=AF.Square),
                    reads=[B["xT%d" % th]], writes=[sqB])

                def fn(e, sq=sq, cq=cq, bank=bank):
                    for c in range(2):
                        ins = e.matmul(self.ps[bank][:], lhsT=self.ones[:], rhs=sq[:, c, :],
                                       start=(cq == 0 and c == 0), stop=(cq == 7 and c == 1))
                    return ins
                P.op("pe", fn, reads=[sqB, B["ones"]], writes=[self.Bps[bank]])
            tf = self.tmpf[2]
            P.op("act", lambda e, bank=bank, tf=tf: e.activation(out=tf[:], in_=self.ps[bank][:], func=AF.Sqrt,
                                                              bias=self.eps_col[:, 0:1], scale=1.0 / D),
                 reads=[self.Bps[bank], B["consts"]], writes=[B["tmpf2"]])
            P.op("dve", lambda e, th=th, tf=tf: e.reciprocal(out=self.rstd[:, th * 512:(th + 1) * 512], in_=tf[:]),
                 reads=[B["tmpf2"]], writes=[B["rstd%d" % th]])

    def modulate(self, gsc_slot, sh_ap_fn, sh_buf):
        P, B = self.P, self.B
        for th in range(2):
            for c in range(NCH):
                i = c % 2
                tf = self.tmpf[i]
                P.op("dve", lambda e, tf=tf, c=c, th=th: e.scalar_tensor_tensor(
                    out=tf[:], in0=self.xT[:, c, th * 512:(th + 1) * 512], scalar=self.mods[:, gsc_slot, c:c + 1],
                    in1=self.rstd[:, th * 512:(th + 1) * 512], op0=ALU.mult, op1=ALU.mult),
                    reads=[B["xT%d" % th], B["mods"], B["rstd%d" % th]], writes=[B["tmpf%d" % i]])
                P.op("act", lambda e, tf=tf, c=c, th=th: e.activation(
                    out=self.hT[:, c, th * 512:(th + 1) * 512], in_=tf[:], func=AF.Identity,
                    bias=sh_ap_fn(c), scale=1.0),
                    reads=[B["tmpf%d" % i], sh_buf], writes=[B["hT%d" % th]])

    def proj_residual(self, wname, rhs_fn, rhs_bufs, gate_ap_fn, gate_buf):
        P, B = self.P, self.B
        n = 0
        for u in range(8):
            wv, wb = self.ws.take((wname, u))
            for fc in range(2):
                m = u * 2 + fc
                for th in range(2):
                    bank = n % 4
                    n += 1

                    def fn(e, wv=wv, fc=fc, th=th, bank=bank):
                        for k in range(NCH):
                            ins = e.matmul(self.ps[bank][:], lhsT=wv[:, k, fc * 128:(fc + 1) * 128],
                                           rhs=rhs_fn(k, th), start=(k == 0), stop=(k == NCH - 1))
                        return ins
                    P.op("pe", fn, reads=[wb] + rhs_bufs(th), writes=[self.Bps[bank]])
                    P.op("dve", lambda e, m=m, th=th, bank=bank: e.scalar_tensor_tensor(
                        out=self.xT[:, m, th * 512:(th + 1) * 512], in0=self.ps[bank][:], scalar=gate_ap_fn(m),
                        in1=self.xT[:, m, th * 512:(th + 1) * 512], op0=ALU.mult, op1=ALU.add),
                        reads=[self.Bps[bank], gate_buf, B["xT%d" % th]], writes=[B["xT%d" % th]])
            self.ws.release()

    def ffn(self, L, gate_ap_fn, gate_buf, between=None):
        P, B, nc = self.P, self.B, self.nc
        with ExitStack() as ph:
            aT = [ph.enter_context(self.sbt("aT%d" % i, [128, 4, T], BF16)) for i in range(2)]
            BaT = [[Buf("aT%d_%d" % (i, th)) for th in range(2)] for i in range(2)]
            sg = [ph.enter_context(self.sbt("sg%d" % i, [128, 512], F32)) for i in range(2)]
            Bsg = [Buf("sg0"), Buf("sg1")]
            ny = 0
            nsg = 0
            for grp in range(11):
                ab = grp % 2
                for qq in range(2):
                    q = grp * 2 + qq
                    wg, wgb = self.ws.take((L + "_wi", q))
                    wu, wub = self.ws.take((L + "_wi", 22 + q))
                    for fc in range(2):
                        jj = qq * 2 + fc
                        for th in range(2):
                            bg = th
                            bu = 2 + th

                            def fg(e, wg=wg, fc=fc, th=th, bg=bg):
                                for k in range(NCH):
                                    ins = e.matmul(self.ps[bg][:], lhsT=wg[:, k, fc * 128:(fc + 1) * 128],
                                                   rhs=self.hT[:, k, th * 512:(th + 1) * 512], start=(k == 0), stop=(k == NCH - 1))
                                return ins
                            P.op("pe", fg, reads=[wgb, B["hT%d" % th]], writes=[self.Bps[bg]])

                            def fu(e, wu=wu, fc=fc, th=th, bu=bu):
                                for k in range(NCH):
                                    ins = e.matmul(self.ps[bu][:], lhsT=wu[:, k, fc * 128:(fc + 1) * 128],
                                                   rhs=self.hT[:, k, th * 512:(th + 1) * 512], start=(k == 0), stop=(k == NCH - 1))
                                return ins
                            P.op("pe", fu, reads=[wub, B["hT%d" % th]], writes=[self.Bps[bu]])
                            si = nsg % 2
                            nsg += 1
                            P.op("act", lambda e, si=si, bg=bg: e.activation(out=sg[si][:], in_=self.ps[bg][:], func=AF.Silu),
                                 reads=[self.Bps[bg]], writes=[Bsg[si]])
                            P.op("dve", lambda e, si=si, bu=bu, ab=ab, jj=jj, th=th: e.tensor_tensor(
                                out=aT[ab][:, jj, th * 512:(th + 1) * 512], in0=self.ps[bu][:], in1=sg[si][:], op=ALU.mult),
                                reads=[self.Bps[bu], Bsg[si]], writes=[BaT[ab][th]])
                    self.ws.release()
                wo0, wob0 = self.ws.take((L + "_wo", grp * 2))
                wo1, wob1 = self.ws.take((L + "_wo", grp * 2 + 1))
                wos = (wo0, wo1)
                for m in range(NCH):
                    for th in range(2):
                        bank = 4 + ny % 2
                        ny += 1

                        def fy(e, m=m, th=th, bank=bank, wos=wos, ab=ab):
                            for jj in range(4):
                                ins = e.matmul(self.ps[bank][:], lhsT=wos[jj // 2][:, jj % 2, m * 128:(m + 1) * 128],
                                               rhs=aT[ab][:, jj, th * 512:(th + 1) * 512], start=(jj == 0), stop=(jj == 3))
                            return ins
                        P.op("pe", fy, reads=[wob0, wob1, BaT[ab][th]], writes=[self.Bps[bank]])
                        P.op("dve", lambda e, m=m, th=th, bank=bank: e.scalar_tensor_tensor(
                            out=self.xT[:, m, th * 512:(th + 1) * 512], in0=self.ps[bank][:], scalar=gate_ap_fn(m),
                            in1=self.xT[:, m, th * 512:(th + 1) * 512], op0=ALU.mult, op1=ALU.add),
                            reads=[self.Bps[bank], gate_buf, B["xT%d" % th]], writes=[B["xT%d" % th]])
                self.ws.release()
        P.barrier()

    def rope_tables(self, pos_d):
        P, B, nc = self.P, self.B, self.nc
        B["rope"] = Buf("rope")
        with ExitStack() as ph:
            posi = ph.enter_context(self.sbt("posi", [128, T], I32))
            ua = ph.enter_context(self.sbt("rope_u", [128, T], F32))
            ub = ph.enter_context(self.sbt("rope_k", [128, T], F32))
            Bp, Ba, Bb = Buf("posi"), Buf("ua"), Buf("ub")
            io = COLS["invf"][0]
            for off, dst in ((0.5, self.sinS), (0.75, self.cosF)):
                P.op("sp", lambda e: e.dma_start(out=posi[:], in_=pos_d.broadcast_to([128, T])), writes=[Bp], dma=P.dma_sem())
                P.op("dve", lambda e: e.tensor_copy(out=ua[:], in_=posi[:]), reads=[Bp], writes=[Ba])
                P.op("dve", lambda e, off=off: e.tensor_scalar(out=ua[:], in0=ua[:], scalar1=self.cols[:, io:io + 1], scalar2=off,
                                                               op0=ALU.mult, op1=ALU.add),
                     reads=[Ba, B["cols"]], writes=[Ba])
                P.op("dve", lambda e: e.tensor_copy(out=posi[:], in_=ua[:]), reads=[Ba], writes=[Bp])
                P.op("dve", lambda e: e.tensor_copy(out=ub[:], in_=posi[:]), reads=[Bp], writes=[Bb])
                P.op("dve", lambda e: e.tensor_tensor(out=ua[:], in0=ua[:], in1=ub[:], op=ALU.subtract),
                     reads=[Ba, Bb], writes=[Ba])
                P.op("dve", lambda e: e.tensor_single_scalar(out=ub[:], in_=ua[:], scalar=0.0, op=ALU.is_lt),
                     reads=[Ba], writes=[Bb])
                P.op("dve", lambda e: e.tensor_tensor(out=ua[:], in0=ua[:], in1=ub[:], op=ALU.add),
                     reads=[Ba, Bb], writes=[Ba])
                P.op("act", lambda e, dst=dst: e.activation(out=dst[:], in_=ua[:], func=AF.Sin, bias=self.eps_col[:, 1:2],
                                                            scale=6.2831845),
                     reads=[Ba, B["consts"]], writes=[B["rope"]])
            sinS_ = self.sinS
            P.op("dve", lambda e: e.tensor_scalar(out=sinS_[64:128, :], in0=sinS_[64:128, :], scalar1=-1.0, scalar2=None,
                                                  op0=ALU.mult),
                 reads=[B["rope"]], writes=[B["rope"]])
        P.barrier()

    def qk_head(self, ps_bank, th, gcol_ap, dst_ap, dst_buf):
        P, B = self.P, self.B
        ps = self.ps[ps_bank]
        bss = 2 + th
        sq = self.sqb[0]
        P.op("act", lambda e: e.activation(out=sq[:, 0, :], in_=ps[:], func=AF.Square),
             reads=[self.Bps[ps_bank]], writes=[B["sqb0"]])
        P.op("pe", lambda e: e.matmul(self.ps[bss][:], lhsT=self.ones[:], rhs=sq[:, 0, :], start=True, stop=True),
             reads=[B["sqb0"], B["ones"]], writes=[self.Bps[bss]])
        t0, t1, t2 = self.tmpf
        P.op("act", lambda e: e.activation(out=t2[:], in_=self.ps[bss][:], func=AF.Sqrt, bias=self.eps_col[:, 0:1],
                                           scale=1.0 / 128),
             reads=[self.Bps[bss], B["consts"]], writes=[B["tmpf2"]])
        P.op("dve", lambda e: e.reciprocal(out=t2[:], in_=t2[:]), reads=[B["tmpf2"]], writes=[B["tmpf2"]])
        qn = self.qn
        P.op("dve", lambda e: e.scalar_tensor_tensor(out=qn[:], in0=ps[:], scalar=gcol_ap, in1=t2[:],
                                                     op0=ALU.mult, op1=ALU.mult),
             reads=[self.Bps[ps_bank], B["tmpf2"], B["cols"]], writes=[B["qn"]])
        sl = slice(th * 512, (th + 1) * 512)
        cosF, sinS = self.cosF, self.sinS
        P.op("dve", lambda e: e.tensor_tensor(out=t0[:], in0=qn[:], in1=cosF[:, sl], op=ALU.mult),
             reads=[B["qn"], B["rope"]], writes=[B["tmpf0"]])
        P.op("dve", lambda e: e.tensor_tensor(out=t1[0:64, :], in0=qn[64:128, :], in1=sinS[64:128, sl], op=ALU.mult),
             reads=[B["qn"], B["rope"]], writes=[B["tmpf1"]])
        P.op("dve", lambda e: e.tensor_tensor(out=t1[64:128, :], in0=qn[0:64, :], in1=sinS[0:64, sl], op=ALU.mult),
             reads=[B["qn"], B["rope"]], writes=[B["tmpf1"]])
        P.op("dve", lambda e: e.tensor_tensor(out=dst_ap, in0=t0[:], in1=t1[:], op=ALU.add),
             reads=[B["tmpf0"], B["tmpf1"]], writes=[dst_buf])

    def layer_a(self, rows_d, sguw_d, pos_d, KT_d, V_d):
        P, B, nc, sb = self.P, self.B, self.nc, self.sb
        adaA, adaK = self.adaA, self.adaK
        stage(0)
        phA = ExitStack()
        wmT = phA.enter_context(self.sbt("wmT", [128, 16, 128], BF16))
        B2 = phA.enter_context(self.sbt("B2", [128, 16, 128], F32))
        inbv = phA.enter_context(self.sbt("inbv", [1, D], BF16))
        for n in ["wmT", "B2", "inbv"]:
            B[n] = Buf(n)
        P.op("pool", lambda e: e.dma_start(out=inbv[:], in_=rows_d[0:1, :]), writes=[B["inbv"]], dma=P.dma_sem())
        with ExitStack() as ph:
            sguw = ph.enter_context(self.sbt("sguw", [128, 16, 128], F32))
            sguwb = ph.enter_context(self.sbt("sguwb", [128, 16, 128], BF16))
            sgub = ph.enter_context(self.sbt("sgub", [128, 16, 128], F32))
            identb = ph.enter_context(self.sbt("identb0", [128, 128], BF16))
            tri = ph.enter_context(self.sbt("tri", [128, 128], BF16))
            Bsw, Bswb, Bsb, Bid, Btri = Buf("sguw"), Buf("sguwb"), Buf("sgub"), Buf("ident"), Buf("tri")
            P.op("sp", lambda e: e.dma_start(out=sguw[:], in_=sguw_d.rearrange("g t s -> t g s")), writes=[Bsw], dma=P.dma_sem())
            P.op("sp", lambda e: e.dma_start(out=sgub[:].rearrange("p g t -> p (g t)"), in_=rows_d[2:3, :].broadcast_to([128, D])),
                 writes=[Bsb], dma=P.dma_sem())
            P.op("pool", lambda e: e.memset(identb[:], 1.0), writes=[Bid])
            P.op("pool", lambda e: e.affine_select(out=identb[:], in_=identb[:], pattern=[[-1, 128]], compare_op=ALU.is_equal,
                                                   fill=0.0, base=0, channel_multiplier=1), reads=[Bid], writes=[Bid])
            P.op("pool", lambda e: e.memset(tri[:], 1.0), writes=[Btri])
            P.op("pool", lambda e: e.affine_select(out=tri[:], in_=tri[:], pattern=[[1, 128]], compare_op=ALU.is_ge,
                                                   fill=0.0, base=0, channel_multiplier=-1), reads=[Btri], writes=[Btri])
            P.op("dve", lambda e: e.tensor_copy(out=sguwb[:], in_=sguw[:]), reads=[Bsw], writes=[Bswb])
            for g4 in range(4):
                def ft(e, g4=g4):
                    for gi in range(4):
                        g = g4 * 4 + gi
                        ins = e.transpose(out=self.psT[:, gi * 128:(gi + 1) * 128], in_=sguwb[:, g, :], identity=identb[:])
                    return ins
                P.op("pe", ft, reads=[Bswb, Bid], writes=[self.BpsT])
                for gi in range(4):
                    g = g4 * 4 + gi
                    P.op("dve", lambda e, g=g, gi=gi: e.tensor_tensor(
                        out=wmT[:, g, :], in0=self.psT[:, gi * 128:(gi + 1) * 128], in1=tri[:], op=ALU.mult),
                        reads=[self.BpsT, Btri], writes=[B["wmT"]])
            lb = COLS["a_ln_b"][0]
            for q in range(4):
                bank = q % 2
                P.op("pe", lambda e, q=q, bank=bank: e.matmul(
                    self.ps[bank][:], lhsT=self.ones[:], rhs=wmT[:, 4 * q:4 * q + 4, :], start=True, stop=True),
                    reads=[B["wmT"], B["ones"]], writes=[self.Bps[bank]])
                for gi in range(4):
                    g = 4 * q + gi
                    P.op("dve", lambda e, g=g, gi=gi, bank=bank: e.scalar_tensor_tensor(
                        out=B2[:, g, :], in0=self.ps[bank][:, gi * 128:(gi + 1) * 128], scalar=self.cols[:, lb + g:lb + g + 1],
                        in1=sgub[:, g, :], op0=ALU.mult, op1=ALU.add),
                        reads=[self.Bps[bank], Bsb, B["cols"]], writes=[B["B2"]])
        P.barrier()
        stage(1)

        self.norm_stats()
        stage(2)
        self.ada_group("a_ada", 0, adaA, B["adaA"], "a_ada_b")
        self.ada_group("a_ada", 1, adaA, B["adaA"], "a_ada_b")
        self.mk_gsc(0, "a_norm1_g", adaA[:, 16:32], B["adaA"])
        self.modulate(0, lambda c: adaA[:, c:c + 1], B["adaA"])
        stage(3)

        with ExitStack() as ph:
            VS = ph.enter_context(self.sbt("VS", [128, 8, 16, 128], BF16))
            BVS = [Buf("VS%d" % tb) for tb in range(8)]
            st1 = ph.enter_context(self.sbt("st1", [128, 8, 8], F32))
            st2 = ph.enter_context(self.sbt("st2", [128, 8, 8], F32))
            stt_ = ph.enter_context(self.sbt("stt", [128, 8, 4], F32))
            junk = ph.enter_context(self.sbt("junk", [128, 256], BF16))
            ug = [ph.enter_context(self.sbt("ug%d" % i, [128, 512], BF16)) for i in range(2)]
            Bst = [Buf("st%d" % tb) for tb in range(8)]
            Bjunk = Buf("junk")
            Bug = [Buf("ug0"), Buf("ug1")]
            P.op("dve", lambda e: e.memset(st1[:], 0.0), writes=Bst)
            P.op("dve", lambda e: e.memset(st2[:], 0.0), writes=Bst)
            nb = 0
            for cb in range(8):
                wv, wb = self.ws.take(("a_in", 8 + cb))
                for tb in range(8):
                    bank = nb % 4
                    nb += 1

                    def fv(e, wv=wv, tb=tb, bank=bank, cb=cb):
                        o = self.ps[bank][:, 0:256]
                        for k in range(NCH):
                            e.matmul(o, lhsT=self.hT[:, k, tb * 128:(tb + 1) * 128], rhs=wv[:, k, :], start=(k == 0), stop=False)
                        return e.matmul(o, lhsT=self.ones[0:1, :], rhs=inbv[0:1, cb * 256:(cb + 1) * 256], start=False, stop=True)
                    P.op("pe", fv, reads=[wb, B["hT%d" % (tb // 4)], B["inbv"], B["ones"]], writes=[self.Bps[bank]])
                    vdst = VS[:, tb, 2 * cb:2 * cb + 2, :]
                    P.op("act", lambda e, vdst=vdst, bank=bank, tb=tb, cb=cb: e.activation(
                        out=vdst, in_=self.ps[bank][:, 0:256].rearrange("p (a b) -> p a b", a=2), func=AF.Gelu,
                        accum_out=st1[:, tb, cb:cb + 1]),
                        reads=[self.Bps[bank]], writes=[BVS[tb], Bst[tb]])
                    P.op("act", lambda e, vdst=vdst, tb=tb, cb=cb: e.activation(
                        out=junk[:].rearrange("p (a b) -> p a b", a=2), in_=vdst, func=AF.Square,
                        accum_out=st2[:, tb, cb:cb + 1]),
                        reads=[BVS[tb]], writes=[Bjunk, Bst[tb]])
                self.ws.release()
            stage(4)
            X = mybir.AxisListType.X
            for tb in range(8):
                P.chain("dve", [
                    lambda e, tb=tb: e.tensor_reduce(out=stt_[:, tb, 0:1], in_=st1[:, tb, :], axis=X, op=ALU.add),
                    lambda e, tb=tb: e.tensor_reduce(out=stt_[:, tb, 1:2], in_=st2[:, tb, :], axis=X, op=ALU.add),
                    lambda e, tb=tb: e.tensor_scalar(out=stt_[:, tb, 0:1], in0=stt_[:, tb, 0:1], scalar1=1.0 / D, scalar2=None, op0=ALU.mult),
                    lambda e, tb=tb: e.tensor_tensor(out=stt_[:, tb, 2:3], in0=stt_[:, tb, 0:1], in1=stt_[:, tb, 0:1], op=ALU.mult),
                    lambda e, tb=tb: e.scalar_tensor_tensor(out=stt_[:, tb, 1:2], in0=stt_[:, tb, 1:2], scalar=1.0 / D,
                                                            in1=stt_[:, tb, 2:3], op0=ALU.mult, op1=ALU.subtract),
                ], reads=[Bst[tb]], writes=[Bst[tb]])
                P.op("act", lambda e, tb=tb: e.activation(out=stt_[:, tb, 2:3], in_=stt_[:, tb, 1:2], func=AF.Sqrt,
                                                          bias=self.eps_col[:, 0:1], scale=1.0),
                     reads=[Bst[tb], B["consts"]], writes=[Bst[tb]])
                P.chain("dve", [
                    lambda e, tb=tb: e.reciprocal(out=stt_[:, tb, 2:3], in_=stt_[:, tb, 2:3]),
                    lambda e, tb=tb: e.scalar_tensor_tensor(out=stt_[:, tb, 3:4], in0=stt_[:, tb, 0:1], scalar=-1.0,
                                                            in1=stt_[:, tb, 2:3], op0=ALU.mult, op1=ALU.mult),
                ], reads=[Bst[tb]], writes=[Bst[tb]])
                P.op("dve", lambda e, tb=tb: e.tensor_scalar(
                    out=VS[:, tb, :, :], in0=VS[:, tb, :, :], scalar1=stt_[:, tb, 2:3], scalar2=stt_[:, tb, 3:4],
                    op0=ALU.mult, op1=ALU.add),
                    reads=[BVS[tb], Bst[tb]], writes=[BVS[tb]])
            lg = COLS["a_ln_g"][0]
            ns = 0
            for tb in range(8):
                for g4 in range(4):
                    bank = 4 + ns % 2
                    ns += 1

                    def fsg(e, tb=tb, g4=g4, bank=bank):
                        for gi in range(4):
                            g = g4 * 4 + gi
                            ins = e.matmul(self.ps[bank][:, gi * 128:(gi + 1) * 128], lhsT=VS[:, tb, g, :], rhs=wmT[:, g, :],
                                           start=True, stop=True)
                        return ins
                    P.op("pe", fsg, reads=[BVS[tb], B["wmT"]], writes=[self.Bps[bank]])
                    for gi in range(4):
                        g = g4 * 4 + gi
                        P.op("dve", lambda e, tb=tb, g=g, gi=gi, bank=bank: e.scalar_tensor_tensor(
                            out=VS[:, tb, g, :], in0=self.ps[bank][:, gi * 128:(gi + 1) * 128],
                            scalar=self.cols[:, lg + g:lg + g + 1], in1=B2[:, g, :], op0=ALU.mult, op1=ALU.add),
                            reads=[self.Bps[bank], B["B2"], B["cols"]], writes=[BVS[tb]])
            stage(5)
            bo = COLS["a_in_b_u"][0]
            nu = 0
            for u in range(8):
                wv, wb = self.ws.take(("a_in", u))
                for fc in range(2):
                    g = u * 2 + fc
                    for th in range(2):
                        bank = nu % 4
                        i = nu % 2
                        nu += 1

                        def fu(e, wv=wv, fc=fc, th=th, bank=bank):
                            for k in range(NCH):
                                ins = e.matmul(self.ps[bank][:], lhsT=wv[:, k, fc * 128:(fc + 1) * 128],
                                               rhs=self.hT[:, k, th * 512:(th + 1) * 512], start=(k == 0), stop=(k == NCH - 1))
                            return ins
                        P.op("pe", fu, reads=[wb, B["hT%d" % th]], writes=[self.Bps[bank]])
                        P.op("act", lambda e, i=i, bank=bank, g=g: e.activation(
                            out=ug[i][:], in_=self.ps[bank][:], func=AF.Gelu, bias=self.cols[:, bo + g:bo + g + 1], scale=1.0),
                            reads=[self.Bps[bank], B["cols"]], writes=[Bug[i]])
                        P.op("dve", lambda e, i=i, g=g, th=th: e.tensor_tensor(
                            out=VS[:, 4 * th:4 * th + 4, g, :], in0=VS[:, 4 * th:4 * th + 4, g, :],
                            in1=ug[i][:].rearrange("p (a b) -> p a b", a=4), op=ALU.mult),
                            reads=[Bug[i]] + BVS[4 * th:4 * th + 4], writes=BVS[4 * th:4 * th + 4])
                self.ws.release()
            self.ada_group("a_ada", 2, adaA, B["adaA"], "a_ada_b")
            self.proj_residual("a_out", lambda k, th: VS[:, 4 * th:4 * th + 4, k, :],
                               lambda th: BVS[4 * th:4 * th + 4], lambda m: adaA[:, 32 + m:33 + m], B["adaA"])
        phA.close()
        P.barrier()
        stage(6)
        for g in (3, 4, 5):
            self.ada_group("a_ada", g, adaA, B["adaA"], "a_ada_b")
        self.norm_stats()
        self.mk_gsc(1, "a_norm2_g", adaA[:, 64:80], B["adaA"])
        self.modulate(1, lambda c: adaA[:, 48 + c:49 + c], B["adaA"])
        self.ffn("a", lambda m: adaA[:, 80 + m:81 + m], B["adaA"])
        stage(7)
        self.ada_group("kv_ada", 0, adaK, B["adaK"], "kv_ada_b")
        self.ada_group("kv_ada", 1, adaK, B["adaK"], "kv_ada_b")
        self.norm_stats()
        self.mk_gsc(2, "kv_norm_g", adaK[:, 16:32], B["adaK"])
        self.modulate(2, lambda c: adaK[:, c:c + 1], B["adaK"])
        with ExitStack() as ph:
            self.cosF = ph.enter_context(self.sbt("cosF", [128, T], F32))
            self.sinS = ph.enter_context(self.sbt("sinS", [128, T], F32))
            self.rope_tables(pos_d)
            self.qn = ph.enter_context(self.sbt("qn", [128, 512], F32))
            B["qn"] = Buf("qn")
            kt = [ph.enter_context(self.sbt("kt%d" % i, [128, 512], BF16)) for i in range(2)]
            Bkt = [Buf("kt0"), Buf("kt1")]
            vt = [ph.enter_context(self.sbt("vt%d" % i, [128, 256], BF16)) for i in range(2)]
            Bvt = [Buf("vt0"), Buf("vt1")]
            kg = self.col("k_norm_g")
            dks = [P.dma_sem(), P.dma_sem()]
            dvs = [P.dma_sem(), P.dma_sem()]
            self.kv_out_sems = dks + dvs
            nk = 0
            for u in range(8):
                wv, wb = self.ws.take(("kv", u))
                for fc in range(2):
                    hh = u * 2 + fc
                    for th in range(2):
                        bank = th
                        i = nk % 2
                        nk += 1

                        def fk(e, wv=wv, fc=fc, th=th, bank=bank):
                            for k in range(NCH):
                                ins = e.matmul(self.ps[bank][:], lhsT=wv[:, k, fc * 128:(fc + 1) * 128],
                                               rhs=self.hT[:, k, th * 512:(th + 1) * 512], start=(k == 0), stop=(k == NCH - 1))
                            return ins
                        P.op("pe", fk, reads=[wb, B["hT%d" % th]], writes=[self.Bps[bank]])
                        self.qk_head(bank, th, kg, kt[i][:], Bkt[i])
                        P.op("sp", lambda e, i=i, hh=hh, th=th: e.dma_start(out=KT_d[hh][:, th * 512:(th + 1) * 512], in_=kt[i][:]),
                             reads=[Bkt[i]], dma=dks[i])
                self.ws.release()
                if self.mode == "F":
                    P.op("pool", lambda e, u=u: e.collective_compute("AllGather", ALU.bypass, replica_groups=[[0, 1, 2, 3], [4, 5, 6, 7]],
                                                                     ins=[self.kt_loc[u]], outs=[self.kt_all[u]]),
                         extra_waits=[(d_, P.cnt[d_]) for d_ in dks], writes=[self.Bktall[u]], dma=P.dma_sem(), dma_inc=1)
            nv = 0
            for u in range(8):
                wv, wb = self.ws.take(("kv", 8 + u))
                for tb in range(8):
                    bank = 4 + nv % 3
                    i = nv % 2
                    nv += 1

                    def fvv(e, wv=wv, tb=tb, bank=bank):
                        for k in range(NCH):
                            ins = e.matmul(self.ps[bank][:, 0:256], lhsT=self.hT[:, k, tb * 128:(tb + 1) * 128], rhs=wv[:, k, :],
                                           start=(k == 0), stop=(k == NCH - 1))
                        return ins
                    P.op("pe", fvv, reads=[wb, B["hT%d" % (tb // 4)]], writes=[self.Bps[bank]])
                    P.op("act", lambda e, i=i, bank=bank: e.activation(out=vt[i][:], in_=self.ps[bank][:, 0:256], func=AF.Copy),
                         reads=[self.Bps[bank]], writes=[Bvt[i]])
                    P.op("sp", lambda e, i=i, u=u, tb=tb: e.dma_start(out=V_d[u][tb * 128:(tb + 1) * 128, :], in_=vt[i][:]),
                         reads=[Bvt[i]], dma=dvs[i])
                self.ws.release()
                if self.mode == "F":
                    P.op("pool", lambda e, u=u: e.collective_compute("AllGather", ALU.bypass, replica_groups=[[0, 1, 2, 3], [4, 5, 6, 7]],
                                                                     ins=[self.v_loc[u]], outs=[self.v_all[u]]),
                         extra_waits=[(d_, P.cnt[d_]) for d_ in dvs], writes=[self.Bvall[u]], dma=P.dma_sem(), dma_inc=1)
            if self.mode == "F":
                Bdl = Buf("dmloc")
                P.op("sp", lambda e: e.dma_start(out=self.dm_loc, in_=self.ones[:]), reads=[B["ones"]], writes=[Bdl], dma=P.dma_sem())
                P.op("pool", lambda e: e.collective_compute("AllGather", ALU.bypass, replica_groups=[[0, 1, 2, 3], [4, 5, 6, 7]],
                                                            ins=[self.dm_loc], outs=[self.dm_all]),
                     reads=[Bdl], writes=[self.Bdummy], dma=P.dma_sem(), dma_inc=1)
        P.barrier()

    def layer_b(self, KTf_d, Vf_d, mask_d, pos_d):
        P, B, nc, sb = self.P, self.B, self.nc, self.sb
        adaA = self.adaA
        QT = sb("QT", [128, NCH, T], BF16)
        BQ = [[Buf("QT%d_%d" % (hd, m)) for m in range(4)] for hd in range(8)]
        mask = sb("mask", [128, 8, 128], BF16)
        lamt = sb("lamt", [128, 8], F32)
        lamb = sb("lamb", [128, 2], BF16)
        B["mask"] = Buf("mask")
        B["lamt"] = Buf("lamt")
        P.op("pool", lambda e: e.dma_start(out=mask[:], in_=mask_d), writes=[B["mask"]], dma=P.dma_sem())
        stage(10)
        lo = COLS["lam"][0]
        so = COLS["subln_g"][0]
        P.op("dve", lambda e: e.tensor_tensor(out=lamb[:, 0:1], in0=self.cols[:, lo:lo + 1], in1=self.cols[:, lo + 1:lo + 2], op=ALU.mult),
             reads=[B["cols"]], writes=[B["lamt"]])
        P.op("dve", lambda e: e.tensor_tensor(out=lamb[:, 1:2], in0=self.cols[:, lo + 2:lo + 3], in1=self.cols[:, lo + 3:lo + 4], op=ALU.mult),
             reads=[B["cols"]], writes=[B["lamt"]])
        P.op("pe", lambda e: e.matmul(self.ps[5][:, 0:2], lhsT=self.ones[:], rhs=lamb[:, 0:2], start=True, stop=True),
             reads=[B["lamt"], B["ones"]], writes=[self.Bps[5]])
        P.op("act", lambda e: e.activation(out=lamt[:, 2:4], in_=self.ps[5][:, 0:2], func=AF.Exp),
             reads=[self.Bps[5]], writes=[B["lamt"]])
        P.chain("dve", [
            lambda e: e.scalar_tensor_tensor(out=lamt[:, 4:5], in0=lamt[:, 2:3], scalar=LAM_INIT, in1=lamt[:, 3:4],
                                             op0=ALU.add, op1=ALU.subtract),
            lambda e: e.tensor_scalar(out=lamt[:, 5:6], in0=lamt[:, 4:5], scalar1=-1.0, scalar2=None, op0=ALU.mult),
            lambda e: e.tensor_scalar(out=lamt[:, 6:8], in0=self.cols[:, so:so + 2], scalar1=1.0 - LAM_INIT, scalar2=None, op0=ALU.mult),
        ], reads=[B["lamt"], B["cols"]], writes=[B["lamt"]])

        stage(11)
        self.norm_stats()
        self.ada_group("b_ada", 0, adaA, B["adaA"], "b_ada_b")
        self.ada_group("b_ada", 1, adaA, B["adaA"], "b_ada_b")
        self.mk_gsc(0, "b_norm1_g", adaA[:, 16:32], B["adaA"])
        self.modulate(0, lambda c: adaA[:, c:c + 1], B["adaA"])
        stage(12)
        with ExitStack() as ph:
            self.cosF = ph.enter_context(self.sbt("cosF", [128, T], F32))
            self.sinS = ph.enter_context(self.sbt("sinS", [128, T], F32))
            self.rope_tables(pos_d)
            self.qn = ph.enter_context(self.sbt("qn", [128, 512], F32))
            B["qn"] = Buf("qn")
            qg = self.col("q_norm_g")
            for u in range(8):
                wv, wb = self.ws.take(("b_q", u))
                for fc in range(2):
                    hh = u * 2 + fc
                    for th in range(2):
                        bank = th

                        def fk(e, wv=wv, fc=fc, th=th, bank=bank):
                            for k in range(NCH):
                                ins = e.matmul(self.ps[bank][:], lhsT=wv[:, k, fc * 128:(fc + 1) * 128],
                                               rhs=self.hT[:, k, th * 512:(th + 1) * 512], start=(k == 0), stop=(k == NCH - 1))
                            return ins
                        P.op("pe", fk, reads=[wb, B["hT%d" % th]], writes=[self.Bps[bank]])
                        self.qk_head(bank, th, qg, QT[:, hh, th * 512:(th + 1) * 512], BQ[u][2 * th])
                        BQ[u][2 * th + 1].w = BQ[u][2 * th].w
                self.ws.release()
        P.barrier()
        stage(13)

        with ExitStack() as ph:
            ex = [ph.enter_context(self.sbt("kvx%d" % i, [128, 4096], BF16)) for i in range(1)]
            slots = [self.hT[:, 4 * i:4 * i + 4, :].rearrange("p c t -> p (c t)") for i in range(4)] + [e_[:] for e_ in ex]
            NS = len(slots)
            sbufs = [Buf("kvs%d" % i) for i in range(NS)]
            ssems = [P.dma_sem() for _ in range(NS)]
            plan = []
            fused = self.mode == "F"
            for hd in range(8):
                if fused:
                    plan.append(("ktf", KTf_d[2 * hd], hd))
                    plan.append(("ktf", KTf_d[2 * hd + 1], hd))
                    plan.append(("vf", (Vf_d[hd][0], Vf_d[hd][1]), hd))
                    plan.append(("vf", (Vf_d[hd][2], Vf_d[hd][3]), hd))
                else:
                    plan.append(("kt", KTf_d[2 * hd], hd))
                    plan.append(("kt", KTf_d[2 * hd + 1], hd))
                    plan.append(("v", Vf_d[hd, 0], hd))
                    plan.append(("v", Vf_d[hd, 1], hd))

            def kmap(kb):
                if not fused:
                    return kb * 128, kb // 16, kb % 16
                m8, r8 = kb // 8, kb % 8
                if r8 < 4:
                    j, sl_ = r8, 2 * m8
                else:
                    j, sl_ = 7 - r8, 2 * m8 + 1
                return j * 1024 + sl_ * 128, j // 2, (j % 2) * 8 + sl_
            st = {"issued": 0, "closed": 0}

            def kview(s, kind):
                if kind == "kt":
                    return slots[s]
                if kind == "ktf":
                    return slots[s].rearrange("p (j t) -> p j t", j=4)
                if kind == "vf":
                    return slots[s].rearrange("p (j s f) -> p j s f", j=2, s=8)
                return slots[s].rearrange("p (k f) -> p k f", k=16)

            def pump():
                while st["issued"] < len(plan) and st["issued"] - NS < st["closed"]:
                    i = st["issued"]
                    kind, src, phd = plan[i]
                    s = i % NS
                    dst = kview(s, kind)
                    if kind == "vf":
                        for jj in range(2):
                            P.op("sp", (lambda dst, src: (lambda e: e.dma_start(out=dst, in_=src)))(dst[:, jj], src[jj]),
                                 reads=[self.Bvall[phd], self.Bdummy], writes=[sbufs[s]], dma=ssems[s])
                    else:
                        P.op("sp", (lambda dst, src: (lambda e: e.dma_start(out=dst, in_=src)))(dst, src),
                             reads=([self.Bktall[phd], self.Bdummy] if kind == "ktf" else []),
                             writes=[sbufs[s]], dma=ssems[s])
                    st["issued"] += 1

            pT = [self.sqb[0][:, 0, :], self.sqb[0][:, 1, :], self.sqb[1][:, 0, :], self.sqb[1][:, 1, :]]
            BpT = [Buf("pT%d" % i) for i in range(4)]
            accs = [ph.enter_context(self.sbt("accs%d" % i, [128, 260], F32)) for i in range(4)]
            Baccs = [Buf("accs%d" % i) for i in range(4)]
            fin = ph.enter_context(self.sbt("fin", [128, 8], F32))
            Bfin = Buf("fin")
            onb = ph.enter_context(self.sbt("onb", [128, 256], BF16))
            Bonb = Buf("onb")
            identb = ph.enter_context(self.sbt("identb", [128, 128], BF16))
            Bidb = Buf("identb")
            P.op("pool", lambda e: e.memset(identb[:], 1.0), writes=[Bidb])
            P.op("pool", lambda e: e.affine_select(out=identb[:], in_=identb[:], pattern=[[-1, 128]], compare_op=ALU.is_equal,
                                                   fill=0.0, base=0, channel_multiplier=1), reads=[Bidb], writes=[Bidb])
            t0, t1, t2 = self.tmpf
            scale = 128.0 ** -0.5
            npt = 0
            nst = 0
            ada_next = 16
            for hd in range(8):
                base = hd * 4
                pump()
                if hd == 0 and self.DBG:
                    if fused:
                        P.op("sp", lambda e: e.dma_start(out=self.dbgA[1], in_=self.kt_all[0]), reads=[self.Bktall[0]], dma=self.do)
                    P.op("sp", lambda e: e.dma_start(out=self.dbgK, in_=slots[0]), reads=[sbufs[0]], dma=self.do)
                    P.op("sp", lambda e: e.dma_start(out=self.dbgV, in_=slots[2]), reads=[sbufs[2]], dma=self.do)
                kts = [kview((base + i) % NS, "kt") for i in range(2)]
                vs = [kview((base + 2 + i) % NS, "v") for i in range(2)]
                kb_ = [sbufs[(base + i) % NS] for i in range(4)]
                for m in range(4):
                    nkb = 8 * m + 8
                    for kb in range(nkb):
                        kc, vh, vb = kmap(kb)
                        full = kb < 8 * m + 4
                        off = 0 if full else 128
                        nq = 256 - off
                        sbank = nst % 2
                        nst += 1
                        pi = npt % 4
                        npt += 1
                        psS = self.ps[sbank]

                        def fs(e, kb=kb, off=off, nq=nq, psS=psS, m=m, hd=hd, kts=kts, kc=kc):
                            for sub in range(2):
                                ins = e.matmul(psS[:, sub * 256 + off:sub * 256 + 256],
                                               lhsT=kts[sub][:, kc:kc + 128],
                                               rhs=QT[:, 2 * hd + sub, 256 * m + off:256 * m + 256], start=True, stop=True)
                            return ins
                        P.op("pe", fs, reads=[kb_[0], kb_[1], BQ[hd][m]], writes=[self.Bps[sbank]])
                        src3 = psS[:].rearrange("p (s q) -> p s q", s=2)[:, :, off:256]
                        dst3 = pT[pi].rearrange("p (s q) -> p s q", s=2)[:, :, off:256]
                        P.op("act", lambda e, src3=src3, dst3=dst3: e.activation(out=dst3, in_=src3, func=AF.Exp, scale=scale),
                             reads=[self.Bps[sbank]], writes=[BpT[pi]])
                        if kb >= 8 * m:
                            par = 0 if full else 1
                            r = kb - 8 * m - 4 * par
                            moff = 0 if par == 0 else 128
                            for sub in range(2):
                                dm = pT[pi][:, sub * 256 + moff:sub * 256 + moff + 128]
                                P.op("dve", lambda e, dm=dm, par=par, r=r: e.tensor_tensor(
                                    out=dm, in0=dm, in1=mask[:, par * 4 + r, :], op=ALU.mult),
                                    reads=[BpT[pi], B["mask"]], writes=[BpT[pi]])
                        vv = vs[vh][:, vb, :]

                        def fpv(e, kb=kb, full=full, pi=pi, vv=vv, m=m):
                            for sl in ((0, 1) if full else (1,)):
                                last = (8 * m + 3) if sl == 0 else (8 * m + 7)
                                for sub in range(2):
                                    acc = self.ps[2 + sl * 2 + sub]
                                    lh = pT[pi][:, sub * 256 + sl * 128:sub * 256 + sl * 128 + 128]
                                    e.matmul(acc[:, 0:256], lhsT=lh, rhs=vv, start=(kb == 0), stop=(kb == last),
                                             skip_group_check=True)
                                    ins = e.matmul(acc[:, 256:257], lhsT=lh, rhs=self.ones[:, 0:1], start=False, stop=(kb == last),
                                                   skip_group_check=True)
                            return ins
                        P.op("pe", fpv, reads=[BpT[pi], kb_[2 + vh], B["ones"]],
                             writes=[self.Bps[2 + sl * 2 + sub] for sl in ((0, 1) if full else (1,)) for sub in range(2)])
                    for a in range(4):
                        P.op("act", lambda e, a=a: e.activation(out=accs[a][:, 0:257], in_=self.ps[2 + a][:, 0:257], func=AF.Copy),
                             reads=[self.Bps[2 + a]], writes=[Baccs[a]])
                    for sl in range(2):
                        aA, aB = accs[2 * sl], accs[2 * sl + 1]
                        s_loc = 2 * m + sl
                        P.chain("dve", [
                            lambda e, aA=aA: e.reciprocal(out=fin[:, 0:1], in_=aA[:, 256:257]),
                            lambda e, aB=aB: e.reciprocal(out=fin[:, 1:2], in_=aB[:, 256:257]),
                            lambda e: e.tensor_tensor(out=fin[:, 2:3], in0=fin[:, 1:2], in1=lamt[:, 5:6], op=ALU.mult),
                        ], reads=[Baccs[2 * sl], Baccs[2 * sl + 1], Bfin, B["lamt"]], writes=[Bfin])
                        P.op("dve", lambda e, aB=aB: e.tensor_scalar(out=t0[:, 0:256], in0=aB[:, 0:256], scalar1=fin[:, 2:3], scalar2=None,
                                                                     op0=ALU.mult),
                             reads=[Baccs[2 * sl + 1], Bfin], writes=[B["tmpf0"]])
                        P.op("dve", lambda e, aA=aA: e.scalar_tensor_tensor(out=t1[:, 0:256], in0=aA[:, 0:256], scalar=fin[:, 0:1],
                                                                            in1=t0[:, 0:256], op0=ALU.mult, op1=ALU.add),
                             reads=[Baccs[2 * sl], Bfin, B["tmpf0"]], writes=[B["tmpf1"]])
                        P.op("act", lambda e: e.activation(out=t2[:, 0:256], in_=t1[:, 0:256], func=AF.Square, accum_out=fin[:, 3:4]),
                             reads=[B["tmpf1"], Bfin], writes=[B["tmpf2"], Bfin])
                        P.op("act", lambda e: e.activation(out=fin[:, 4:5], in_=fin[:, 3:4], func=AF.Sqrt, bias=self.eps_col[:, 0:1],
                                                           scale=1.0 / 256),
                             reads=[Bfin, B["consts"]], writes=[Bfin])
                        P.op("dve", lambda e: e.reciprocal(out=fin[:, 5:6], in_=fin[:, 4:5]), reads=[Bfin], writes=[Bfin])
                        P.op("dve", lambda e: e.tensor_scalar(out=onb[:], in0=t1[:, 0:256], scalar1=fin[:, 5:6], scalar2=None, op0=ALU.mult),
                             reads=[B["tmpf1"], Bfin], writes=[Bonb])

                        def ftr(e):
                            e.transpose(out=self.psT[:, 0:128], in_=onb[:, 0:128], identity=identb[:])
                            return e.transpose(out=self.psT[:, 128:256], in_=onb[:, 128:256], identity=identb[:])
                        P.op("pe", ftr, reads=[Bonb, Bidb], writes=[self.BpsT])
                        for c2 in range(2):
                            P.op("dve", lambda e, c2=c2, hd=hd, s_loc=s_loc: e.tensor_scalar(
                                out=QT[:, 2 * hd + c2, s_loc * 128:(s_loc + 1) * 128], in0=self.psT[:, c2 * 128:(c2 + 1) * 128],
                                scalar1=lamt[:, 6 + c2:7 + c2], scalar2=None, op0=ALU.mult),
                                reads=[self.BpsT, B["lamt"]], writes=[BQ[hd][m]])
                st["closed"] = base + 4
                if STOP <= 14 + hd:
                    ada_next = 48
                    break
                for _ in range(4):
                    self.ada_unit("b_ada", ada_next, adaA, B["adaA"], "b_ada_b")
                    ada_next += 1
            assert ada_next == 48
        P.barrier()
        stage(21)
        allQ = [BQ[hd][m] for hd in range(8) for m in range(4)]
        self.proj_residual("b_o", lambda k, th: QT[:, k, th * 512:(th + 1) * 512], lambda th: allQ,
                           lambda m: adaA[:, 32 + m:33 + m], B["adaA"])
        stage(22)
        self.norm_stats()
        self.mk_gsc(1, "b_norm2_g", adaA[:, 64:80], B["adaA"])
        self.modulate(1, lambda c: adaA[:, 48 + c:49 + c], B["adaA"])
        self.ffn("b", lambda m: adaA[:, 80 + m:81 + m], B["adaA"])


_NC_CACHE = {}


def get_nc(mode):
    if mode not in _NC_CACHE:
        _NC_CACHE[mode] = Builder(mode).build()
    return _NC_CACHE[mode]


def host_cols(inp, b):
    cols = np.zeros((128, NCOL), np.float32)

    def put(name, arr):
        o, w = COLS[name]
        cols[:, o:o + w] = arr
    put("a_norm1_g", colv(inp["a_norm1_g"][0]))
    put("a_norm2_g", colv(inp["a_norm2_g"][0]))
    put("a_in_b_u", colv(inp["a_in_b"][0][:D]))
    put("a_ln_g", colv(inp["a_sgu_ln_g"][0]))
    put("a_ln_b", colv(inp["a_sgu_ln_b"][0]))
    put("kv_norm_g", colv(inp["kv_norm_g"]))
    put("b_norm1_g", colv(inp["b_norm1_g"][0]))
    put("b_norm2_g", colv(inp["b_norm2_g"][0]))
    put("a_ada_b", colv(inp["a_ada_b"][0]))
    put("kv_ada_b", colv(inp["kv_ada_b"]))
    put("b_ada_b", colv(inp["b_ada_b"][0]))
    put("k_norm_g", colv(inp["k_norm_g"]))
    put("q_norm_g", colv(inp["b_q_norm_g"][0]))
    put("subln_g", colv(inp["b_subln_g"][0]))
    put("lam", np.stack([inp["b_lambda_q1"][0], inp["b_lambda_k1"][0], inp["b_lambda_q2"][0], inp["b_lambda_k2"][0]], 1))
    inv_freq = 1.0 / (10000.0 ** (np.arange(0, 128, 2, dtype=np.float32) / 128.0))
    put("invf", (np.concatenate([inv_freq, inv_freq]) / (2 * np.pi)).astype(np.float32)[:, None])
    put("cT", colv(inp["c"][b]))
    return cols


def host_mask(j):
    m = np.zeros((128, 8, 128), np.float32)
    tri = (np.arange(128)[:, None] <= np.arange(128)[None, :]).astype(np.float32)
    for par in range(2):
        p = j if par == 0 else 7 - j
        for r in range(4):
            kbk = r if par == 0 else 4 + r
            if kbk < p:
                m[:, par * 4 + r, :] = 1.0
            elif kbk == p:
                m[:, par * 4 + r, :] = tri
    return m


def run_a(inp, ncores=8):
    nc = get_nc("A")
    shared = {
        "a_ada_w": tile_w(inp["a_ada_w"][0]), "a_in_w": tile_w(inp["a_in_w"][0]), "a_out_w": tile_w(inp["a_out_w"][0]),
        "a_ffn_wi": tile_w(inp["a_ffn_wi"][0]), "a_ffn_wo": tile_wo(inp["a_ffn_wo"][0]),
        "kv_ada_w": tile_w(inp["kv_ada_w"]), "kv_w": tile_w(inp["kv_w"]),
        "sgu_w": np.ascontiguousarray(inp["a_sgu_w"][0]),
        "rows": np.ascontiguousarray(np.stack([inp["a_in_b"][0][D:], inp["a_sgu_ln_b"][0], inp["a_sgu_b"][0].reshape(-1)]).astype(np.float32)),
    }
    in_maps = []
    for i in range(8):
        b, j = i // 4, i % 4
        idx = tok_index(j)
        m = dict(shared)
        m["xT"] = np.ascontiguousarray(inp["x"][b][idx].T)
        m["pos"] = np.ascontiguousarray(inp["positions"][b][idx][None, :]).astype(np.int32)
        m["cols"] = host_cols(inp, b)
        in_maps.append(m)
    in_maps = in_maps[:ncores]
    res = run_bass_kernel_spmd(nc, in_maps, core_ids=list(range(ncores)))
    return res.results


def run_b(inp, ra, ncores=8):
    nc = get_nc("B")
    shared = {
        "b_ada_w": tile_w(inp["b_ada_w"][0]), "b_q_w": tile_w(inp["b_q_w"][0]), "b_o_w": tile_w(inp["b_o_w"][0]),
        "b_ffn_wi": tile_w(inp["b_ffn_wi"][0]), "b_ffn_wo": tile_wo(inp["b_ffn_wo"][0]),
    }
    KTf, Vf = [], []
    for b in range(2):
        kt = np.zeros((16, 128, 4096), ml_dtypes.bfloat16)
        v = np.zeros((8, 4096, 256), ml_dtypes.bfloat16)
        for j in range(4):
            idx = tok_index(j)
            kt[:, :, idx] = ra[4 * b + j]["KT"]
            v[:, idx, :] = ra[4 * b + j]["V"]
        KTf.append(kt)
        Vf.append(np.ascontiguousarray(v.reshape(8, 2, 16, 128, 256).transpose(0, 1, 3, 2, 4)))
    in_maps = []
    for i in range(8):
        b, j = i // 4, i % 4
        idx = tok_index(j)
        m = dict(shared)
        m["xT"] = np.ascontiguousarray(ra[i]["outT"])
        m["pos"] = np.ascontiguousarray(inp["positions"][b][idx][None, :]).astype(np.int32)
        m["cols"] = host_cols(inp, b)
        m["KTf"] = KTf[b]
        m["Vf"] = Vf[b]
        m["mask"] = host_mask(j)
        in_maps.append(m)
    in_maps = in_maps[:ncores]
    res = run_bass_kernel_spmd(nc, in_maps, core_ids=list(range(ncores)))
    return res.results


def run_f(inp, ncores=8):
    nc = get_nc("F")
    shared = {
        "a_ada_w": tile_w(inp["a_ada_w"][0]), "a_in_w": tile_w(inp["a_in_w"][0]), "a_out_w": tile_w(inp["a_out_w"][0]),
        "a_ffn_wi": tile_w(inp["a_ffn_wi"][0]), "a_ffn_wo": tile_wo(inp["a_ffn_wo"][0]),
        "kv_ada_w": tile_w(inp["kv_ada_w"]), "kv_w": tile_w(inp["kv_w"]),
        "sgu_w": np.ascontiguousarray(inp["a_sgu_w"][0]),
        "rows": np.ascontiguousarray(np.stack([inp["a_in_b"][0][D:], inp["a_sgu_ln_b"][0], inp["a_sgu_b"][0].reshape(-1)]).astype(np.float32)),
        "b_ada_w": tile_w(inp["b_ada_w"][0]), "b_q_w": tile_w(inp["b_q_w"][0]), "b_o_w": tile_w(inp["b_o_w"][0]),
        "b_ffn_wi": tile_w(inp["b_ffn_wi"][0]), "b_ffn_wo": tile_wo(inp["b_ffn_wo"][0]),
    }
    in_maps = []
    for i in range(8):
        b, j = i // 4, i % 4
        idx = tok_index(j)
        m = dict(shared)
        m["xT"] = np.ascontiguousarray(inp["x"][b][idx].T)
        m["pos"] = np.ascontiguousarray(inp["positions"][b][idx][None, :]).astype(np.int32)
        m["cols"] = host_cols(inp, b)
        m["mask"] = host_mask(j)
        in_maps.append(m)
    in_maps = in_maps[:ncores]
    res = run_bass_kernel_spmd(nc, in_maps, core_ids=list(range(ncores)))
    return res.results


def kernel(**inp):
    inp = {k: np.asarray(v) for k, v in inp.items()}
    rf = run_f(inp)
    out = np.zeros((2, 4096, D), np.float32)
    for i in range(8):
        b, j = i // 4, i % 4
        out[b, tok_index(j), :] = rf[i]["outT"].T
    return out
```

```python
import math
import numpy as np
import ml_dtypes
import concourse.bass as bass
import concourse.mybir as mybir
from concourse.bass_utils import run_bass_kernel_spmd
from contextlib import ExitStack

F32 = mybir.dt.float32
BF16 = mybir.dt.bfloat16
I32 = mybir.dt.int32
ALU = mybir.AluOpType
AF = mybir.ActivationFunctionType

D = 2048
NCH = 16
T = 1024
DFF = 5632
NJ = 44
EPS = 1e-6
NSLOT = 5
ENGS = ("pe", "act", "dve", "pool", "sp")

COLS = {}
_o = 0
for _n, _w in [("a_norm1_g", 16), ("a_norm2_g", 16), ("a_in_b_u", 16), ("a_ln_g", 16), ("a_ln_b", 16), ("kv_norm_g", 16),
               ("b_norm1_g", 16), ("b_norm2_g", 16), ("a_ada_b", 96), ("kv_ada_b", 32), ("b_ada_b", 96),
               ("k_norm_g", 1), ("q_norm_g", 1), ("subln_g", 2), ("lam", 4), ("invf", 1), ("cT", 16)]:
    COLS[_n] = (_o, _w)
    _o += _w
NCOL = _o


class Buf:
    __slots__ = ("name", "w", "r")

    def __init__(self, name=""):
        self.name = name
        self.w = None
        self.r = {}


class Prog:
    def __init__(self, nc, es):
        self.nc = nc
        self.es = es
        self.q = {e: [] for e in ENGS}
        self.sem = {}
        self.cnt = {}
        self.seen = {e: {} for e in ENGS}
        for e in ENGS:
            self.new_sem("E_" + e)
        self.n_dma_sem = 0

    def new_sem(self, name):
        self.sem[name] = self.es.enter_context(self.nc.semaphore(name))
        self.cnt[name] = 0
        return name

    def dma_sem(self):
        self.n_dma_sem += 1
        return self.new_sem("D%d" % self.n_dma_sem)

    def op(self, eng, fn, reads=(), writes=(), dma=None, extra_waits=(), dma_inc=16):
        waits = {}

        def need(tok):
            if tok is None:
                return
            s, v = tok
            if waits.get(s, 0) < v:
                waits[s] = v

        for b in reads:
            need(b.w)
        for b in writes:
            need(b.w)
            for s, v in b.r.items():
                need((s, v))
        for t in extra_waits:
            need(t)
        if eng == "pe":
            waits.pop("E_pe", None)
        wl = []
        seen = self.seen[eng]
        for s, v in waits.items():
            if seen.get(s, 0) < v:
                seen[s] = v
                wl.append((s, v))
        if dma is not None:
            s = dma
            self.cnt[s] += dma_inc
            inc = (s, dma_inc)
        else:
            s = "E_" + eng
            self.cnt[s] += 1
            inc = (s, 1)
        tok = (s, self.cnt[s])
        self.q[eng].append((wl, fn, inc))
        for b in reads:
            if b.r.get(s, 0) < tok[1]:
                b.r[s] = tok[1]
        for b in writes:
            b.w = tok
            b.r = {}
        return tok

    def wait_only(self, eng, toks):
        wl = []
        seen = self.seen[eng]
        for s, v in toks:
            if seen.get(s, 0) < v:
                seen[s] = v
                wl.append((s, v))
        if wl:
            self.q[eng].append((wl, None, None))

    def barrier(self):
        toks = [(s, c) for s, c in self.cnt.items() if c > 0]
        for e in ENGS:
            self.wait_only(e, toks)

    def chain(self, eng, fns, reads=(), writes=()):
        for fn in fns:
            tok = self.op(eng, fn, reads=reads, writes=writes)
        return tok

    def emit(self):
        nc = self.nc
        block = self.es.enter_context(nc.Block())
        sem = self.sem

        def replay(engobj, items):
            for wl, fn, inc in items:
                for s, v in wl:
                    engobj.wait_ge(sem[s], v)
                if fn is not None:
                    ins = fn(engobj)
                    ins.then_inc(sem[inc[0]], inc[1])

        q = self.q

        @block.tensor
        def _(e):
            replay(e, q["pe"])

        @block.scalar
        def _(e):
            replay(e, q["act"])

        @block.vector
        def _(e):
            replay(e, q["dve"])

        @block.gpsimd
        def _(e):
            replay(e, q["pool"])

        @block.sync
        def _(e):
            replay(e, q["sp"])


class WStream:
    def __init__(self, P, nc, es, nslot):
        self.P = P
        self.nslot = nslot
        self.slots = [es.enter_context(nc.sbuf_tensor("s_wslot%d" % i, [128, 4096], BF16)) for i in range(nslot)]
        self.bufs = [Buf("wslot%d" % i) for i in range(nslot)]
        self.sems = [P.dma_sem() for _ in range(nslot)]
        self.plan = []
        self.issued = 0
        self.taken = 0
        self.closed = 0

    def add(self, key, src, shape):
        self.plan.append((key, src, shape))

    def pump(self):
        while self.issued < len(self.plan) and self.issued - self.nslot < self.closed:
            i = self.issued
            key, src, shape = self.plan[i]
            s = i % self.nslot
            dst = self.view(s, shape)
            self.P.op("pool", (lambda dst, src: (lambda e: e.dma_start(out=dst, in_=src)))(dst, src),
                      writes=[self.bufs[s]], dma=self.sems[s])
            self.issued += 1

    def view(self, s, shape):
        if shape == "k":
            return self.slots[s][:].rearrange("p (k f) -> p k f", k=16)
        else:
            return self.slots[s][:].rearrange("p (j f) -> p j f", j=2)

    def take(self, key):
        i = self.taken
        assert self.plan[i][0] == key, (self.plan[i][0], key)
        assert i - self.closed < self.nslot, "too many open weight units"
        self.pump()
        assert self.issued > i
        self.taken += 1
        s = i % self.nslot
        return self.view(s, self.plan[i][2]), self.bufs[s]

    def release(self):
        self.closed = self.taken
        self.pump()


def tile_w(W):
    K, Fd = W.shape
    return np.ascontiguousarray(W.reshape(K // 128, 128, Fd // 256, 256).transpose(2, 1, 0, 3))


def tile_wo(W):
    K, Fd = W.shape
    return np.ascontiguousarray(W.reshape(K // 256, 2, 128, Fd).transpose(0, 2, 1, 3))


def colv(v):
    v = np.asarray(v, np.float32).reshape(-1, 128)
    return np.ascontiguousarray(v.T)


def tok_index(j):
    idx = []
    for s in range(8):
        p = 8 * (s // 2) + (j if s % 2 == 0 else 7 - j)
        idx.append(np.arange(p * 128, (p + 1) * 128))
    return np.concatenate(idx)


LAM_INIT = 0.8 - 0.6 * math.exp(-0.3 * 1)


import os
STOP = int(os.environ.get("KSTOP", "99"))


class _Stop(Exception):
    pass


def stage(n):
    if STOP <= n:
        raise _Stop()


class Builder:
    def __init__(self, mode):
        self.mode = mode
        self.nc = bass.Bass("TRN2", target_bir_lowering=False)

    def sbt(self, name, shape, dt):
        self._uid = getattr(self, "_uid", 0) + 1
        return self.nc.sbuf_tensor("s_%s_%d" % (name, self._uid), list(shape), dt)

    def dram_in(self, name, shape, dt=F32):
        return self.nc.dram_tensor(name, list(shape), dt, kind="ExternalInput").ap()

    def dram_out(self, name, shape, dt=F32):
        return self.nc.dram_tensor(name, list(shape), dt, kind="ExternalOutput").ap()

    def build(self):
        nc = self.nc
        mode = self.mode
        with ExitStack() as es:
            self.es = es
            P = self.P = Prog(nc, es)
            sb = lambda n, s, d: es.enter_context(self.sbt("" + n, list(s), d))
            self.sb = sb
            xT_d = self.dram_in("xT", [D, T])
            cols_d = self.dram_in("cols", [128, NCOL])
            pos_d = self.dram_in("pos", [1, T], I32)
            out_d = self.dram_out("outT", [D, T])
            self.DBG = bool(int(os.environ.get("KDEBUG", "0")))
            if self.DBG:
                self.dbgK = self.dram_out("dbgK", [128, 4096], BF16)
                self.dbgV = self.dram_out("dbgV", [128, 4096], BF16)
                self.dbgA = self.dram_out("dbgA", [3, 1024, 1024], BF16)
                self.dbgL = self.dram_out("dbgL", [256, 1024], BF16)
            W = {}
            if mode in ("A", "F"):
                rows_d = self.dram_in("rows", [3, D])
                sguw_d = self.dram_in("sgu_w", [16, 128, 128])
                W["a_ada"] = self.dram_in("a_ada_w", [48, 128, 16, 256])
                W["a_in"] = self.dram_in("a_in_w", [16, 128, 16, 256])
                W["a_out"] = self.dram_in("a_out_w", [8, 128, 16, 256])
                W["a_wi"] = self.dram_in("a_ffn_wi", [44, 128, 16, 256])
                W["a_wo"] = self.dram_in("a_ffn_wo", [22, 128, 2, 2048])
                W["kv_ada"] = self.dram_in("kv_ada_w", [16, 128, 16, 256])
                W["kv"] = self.dram_in("kv_w", [16, 128, 16, 256])
            if mode == "A":
                KT_d = self.dram_out("KT", [16, 128, T], BF16)
                V_d = self.dram_out("V", [8, T, 256], BF16)
            if mode == "F":
                self.kt_loc = [nc.dram_tensor("kt_loc%d" % i, [256, T], BF16, kind="Internal").ap() for i in range(8)]
                self.kt_all = [nc.dram_tensor("kt_all%d" % i, [4 * 256, T], BF16, kind="Internal").ap() for i in range(8)]
                self.v_loc = [nc.dram_tensor("v_loc%d" % i, [256, T], BF16, kind="Internal").ap() for i in range(8)]
                self.v_all = [nc.dram_tensor("v_all%d" % i, [4 * 256, T], BF16, kind="Internal").ap() for i in range(8)]
                self.dm_loc = nc.dram_tensor("dm_loc", [128, 128], BF16, kind="Internal").ap()
                self.dm_all = nc.dram_tensor("dm_all", [4 * 128, 128], BF16, kind="Internal").ap()
                KT_d = [self.kt_loc[hh // 2][(hh % 2) * 128:(hh % 2) * 128 + 128, :] for hh in range(16)]
                V_d = [self.v_loc[u].rearrange("r (q d) -> (r q) d", d=256) for u in range(8)]
            if mode in ("B", "F"):
                W["b_ada"] = self.dram_in("b_ada_w", [48, 128, 16, 256])
                W["b_q"] = self.dram_in("b_q_w", [8, 128, 16, 256])
                W["b_o"] = self.dram_in("b_o_w", [8, 128, 16, 256])
                W["b_wi"] = self.dram_in("b_ffn_wi", [44, 128, 16, 256])
                W["b_wo"] = self.dram_in("b_ffn_wo", [22, 128, 2, 2048])
                mask_d = self.dram_in("mask", [128, 8, 128])
            if mode == "B":
                KTf_d = self.dram_in("KTf", [16, 128, 4096], BF16)
                Vf_d = self.dram_in("Vf", [8, 2, 128, 16, 256], BF16)
            if mode == "F":
                KTf_d = [self.kt_all[hh // 2].rearrange("(j s d) t -> s d j t", j=4, s=2, d=128)[hh % 2] for hh in range(16)]
                Vf_d = [[self.v_all[u].rearrange("(j r) (q d) -> j (r q) d", j=4, d=256)[j].rearrange("(s p) d -> p s d", p=128)
                         for j in range(4)] for u in range(8)]
            self.W = W

            self.xT = sb("xT", [128, NCH, T], F32)
            self.hT = sb("hT", [128, NCH, T], BF16)
            self.cols = sb("cols", [128, NCOL], F32)
            self.rstd = sb("rstd", [128, T], F32)
            self.ones = sb("ones", [128, 128], BF16)
            self.ones_f = sb("ones_f", [128, 128], F32)
            self.sc_bf = sb("sc_bf", [128, NCH], BF16)
            self.adaA = sb("adaA", [128, 96], F32)
            self.adaK = sb("adaK", [128, 32], F32)
            self.mods = sb("mods", [128, 4, 16], F32)
            self.tmpf = [sb("tmpf%d" % i, [128, 512], F32) for i in range(3)]
            self.sqb = [sb("sqb%d" % i, [128, 2, 512], BF16) for i in range(2)]
            self.eps_col = sb("eps_col", [128, 4], F32)
            B = self.B = {}
            for n in ["xT0", "xT1", "hT0", "hT1", "cols", "rstd0", "rstd1", "ones", "sc_bf", "adaA", "adaK", "mods",
                      "tmpf0", "tmpf1", "tmpf2", "sqb0", "sqb1", "out", "consts"]:
                B[n] = Buf(n)
            self.ps = [es.enter_context(nc.psum_tensor("ps%d" % i, [128, 512], F32)) for i in range(7)]
            self.psT = es.enter_context(nc.psum_tensor("psT", [128, 1024], BF16))
            self.Bps = [Buf("ps%d" % i) for i in range(7)]
            self.BpsT = Buf("psT")
            self.ws = WStream(P, nc, es, NSLOT)
            self.dl = P.dma_sem()
            self.dx = P.dma_sem()
            self.do = P.dma_sem()
            xT, cols, ones, ones_f, sc_bf = self.xT, self.cols, self.ones, self.ones_f, self.sc_bf

            ws = self.ws
            if mode in ("A", "F"):
                self.plan_ada("a_ada", 0)
                self.plan_ada("a_ada", 1)
                for u in range(8, 16):
                    ws.add(("a_in", u), W["a_in"][u], "k")
                for u in range(0, 8):
                    ws.add(("a_in", u), W["a_in"][u], "k")
                self.plan_ada("a_ada", 2)
                for u in range(8):
                    ws.add(("a_out", u), W["a_out"][u], "k")
                for g in (3, 4, 5):
                    self.plan_ada("a_ada", g)
                self.plan_ffn("a")
                self.plan_ada("kv_ada", 0)
                self.plan_ada("kv_ada", 1)
                for u in range(16):
                    ws.add(("kv", u), W["kv"][u], "k")
            if mode in ("B", "F"):
                self.plan_ada("b_ada", 0)
                self.plan_ada("b_ada", 1)
                for u in range(8):
                    ws.add(("b_q", u), W["b_q"][u], "k")
                for g in (2, 3, 4, 5):
                    self.plan_ada("b_ada", g)
                for u in range(8):
                    ws.add(("b_o", u), W["b_o"][u], "k")
                self.plan_ffn("b")

            P.op("sp", lambda e: e.dma_start(out=cols[:], in_=cols_d), writes=[B["cols"]], dma=P.dma_sem())
            for th in range(2):
                for cq in range(4):
                    src = xT_d.rearrange("(c p) t -> p c t", p=128)[:, 4 * cq:4 * cq + 4, th * 512:(th + 1) * 512]
                    dst = xT[:, 4 * cq:4 * cq + 4, th * 512:(th + 1) * 512]
                    P.op("sp", (lambda dst, src: (lambda e: e.dma_start(out=dst, in_=src)))(dst, src), dma=self.dx)
            B["xT0"].w = (self.dx, P.cnt[self.dx])
            B["xT1"].w = (self.dx, P.cnt[self.dx])
            P.op("dve", lambda e: e.memset(ones[:], 1.0), writes=[B["ones"]])
            P.op("dve", lambda e: e.memset(ones_f[:], 1.0), writes=[B["ones"]])
            P.op("dve", lambda e: e.memset(self.eps_col[:, 0:1], EPS), writes=[B["consts"]])
            P.op("dve", lambda e: e.memset(self.eps_col[:, 1:2], -3.1415920), writes=[B["consts"]])
            c0 = COLS["cT"][0]
            P.op("act", lambda e: e.activation(out=sc_bf[:], in_=cols[:, c0:c0 + 16], func=AF.Silu),
                 reads=[B["cols"]], writes=[B["sc_bf"]])

            self.Bktall = [Buf("ktall%d" % i) for i in range(8)]
            self.Bvall = [Buf("vall%d" % i) for i in range(8)]
            self.Bdummy = Buf("dummy")
            try:
                if mode in ("A", "F"):
                    self.layer_a(rows_d, sguw_d, pos_d, KT_d, V_d)
                if mode == "F" and self.DBG:
                    P.op("sp", lambda e: e.dma_start(out=self.dbgA[0], in_=self.kt_all[0]), reads=[self.Bktall[0]], dma=self.do)
                if mode in ("B", "F"):
                    self.layer_b(KTf_d, Vf_d, mask_d, pos_d)
                if mode == "F" and self.DBG:
                    P.op("sp", lambda e: e.dma_start(out=self.dbgA[2], in_=self.kt_all[0]), reads=[self.Bktall[0]], dma=self.do)
                    P.op("sp", lambda e: e.dma_start(out=self.dbgL, in_=self.kt_loc[0]), dma=self.do)
            except _Stop:
                ws.taken = len(ws.plan)

            for th in range(2):
                for cq in range(4):
                    dst = out_d.rearrange("(c p) t -> p c t", p=128)[:, 4 * cq:4 * cq + 4, th * 512:(th + 1) * 512]
                    src = xT[:, 4 * cq:4 * cq + 4, th * 512:(th + 1) * 512]
                    P.op("sp", (lambda dst, src: (lambda e: e.dma_start(out=dst, in_=src)))(dst, src),
                         reads=[B["xT%d" % th]], writes=[B["out"]], dma=self.do)
            P.wait_only("sp", [(self.do, P.cnt[self.do])] + [(d_, P.cnt[d_]) for d_ in getattr(self, "kv_out_sems", [])])
            assert ws.taken == len(ws.plan), (ws.taken, len(ws.plan))
            P.emit()
        return nc

    def col(self, name, i=0, n=1):
        o, w = COLS[name]
        return self.cols[:, o + i:o + i + n]

    def plan_ada(self, wname, g):
        for u in range(8):
            self.ws.add((wname, g * 8 + u), self.W[wname][g * 8 + u], "k")

    def plan_ffn(self, L):
        for grp in range(11):
            for qq in range(2):
                q = grp * 2 + qq
                self.ws.add((L + "_wi", q), self.W[L + "_wi"][q], "k")
                self.ws.add((L + "_wi", 22 + q), self.W[L + "_wi"][22 + q], "k")
            for qq in range(2):
                q = grp * 2 + qq
                self.ws.add((L + "_wo", q), self.W[L + "_wo"][q], "j")

    def ada_unit(self, wname, idx, dst, dbuf, bias_name):
        P, B = self.P, self.B
        bank = 6
        ps = self.ps[bank]
        g, u = idx // 8, idx % 8
        wv, wb = self.ws.take((wname, idx))

        def fn(e, wv=wv, u=u):
            for fc in range(2):
                c = u * 2 + fc
                for k in range(NCH):
                    ins = e.matmul(ps[:, c:c + 1], lhsT=wv[:, k, fc * 128:(fc + 1) * 128],
                                   rhs=self.sc_bf[:, k:k + 1], start=(k == 0), stop=(k == NCH - 1))
            return ins
        P.op("pe", fn, reads=[wb, B["sc_bf"]], writes=[self.Bps[bank]])
        self.ws.release()
        if u == 7:
            bo = COLS[bias_name][0] + g * 16
            P.op("dve", lambda e: e.tensor_tensor(out=dst[:, g * 16:(g + 1) * 16], in0=ps[:, 0:16],
                                                  in1=self.cols[:, bo:bo + 16], op=ALU.add),
                 reads=[self.Bps[bank], B["cols"]], writes=[dbuf])

    def ada_group(self, wname, g, dst, dbuf, bias_name):
        for u in range(8):
            self.ada_unit(wname, g * 8 + u, dst, dbuf, bias_name)

    def mk_gsc(self, slot, gname, sc_ap, src_buf):
        P, B = self.P, self.B
        o = COLS[gname][0]
        P.op("dve", lambda e: e.scalar_tensor_tensor(out=self.mods[:, slot, :], in0=sc_ap, scalar=1.0,
                                                     in1=self.cols[:, o:o + 16], op0=ALU.add, op1=ALU.mult),
             reads=[src_buf, B["cols"]], writes=[B["mods"]])

    def norm_stats(self):
        P, B = self.P, self.B
        xT = self.xT
        for th in range(2):
            bank = th
            for cq in range(8):
                sq = self.sqb[cq % 2]
                sqB = B["sqb%d" % (cq % 2)]
                P.op("act", lambda e, sq=sq, cq=cq, th=th: e.activation(
                    out=sq[:], in_=xT[:, 2 * cq:2 * cq + 2, th * 512:(th + 1) * 512], func=AF.Square),
                    reads=[B["xT%d" % th]], writes=[sqB])

                def fn(e, sq=sq, cq=cq, bank=bank):
                    for c in range(2):
                        ins = e.matmul(self.ps[bank][:], lhsT=self.ones[:], rhs=sq[:, c, :],
                                       start=(cq == 0 and c == 0), stop=(cq == 7 and c == 1))
                    return ins
                P.op("pe", fn, reads=[sqB, B["ones"]], writes=[self.Bps[bank]])
            tf = self.tmpf[2]
            P.op("act", lambda e, bank=bank, tf=tf: e.activation(out=tf[:], in_=self.ps[bank][:], func=AF.Sqrt,
                                                              bias=self.eps_col[:, 0:1], scale=1.0 / D),
                 reads=[self.Bps[bank], B["consts"]], writes=[B["tmpf2"]])
            P.op("dve", lambda e, th=th, tf=tf: e.reciprocal(out=self.rstd[:, th * 512:(th + 1) * 512], in_=tf[:]),
                 reads=[B["tmpf2"]], writes=[B["rstd%d" % th]])

    def modulate(self, gsc_slot, sh_ap_fn, sh_buf):
        P, B = self.P, self.B
        for th in range(2):
            for c in range(NCH):
                i = c % 2
                tf = self.tmpf[i]
                P.op("dve", lambda e, tf=tf, c=c, th=th: e.scalar_tensor_tensor(
                    out=tf[:], in0=self.xT[:, c, th * 512:(th + 1) * 512], scalar=self.mods[:, gsc_slot, c:c + 1],
                    in1=self.rstd[:, th * 512:(th + 1) * 512], op0=ALU.mult, op1=ALU.mult),
                    reads=[B["xT%d" % th], B["mods"], B["rstd%d" % th]], writes=[B["tmpf%d" % i]])
                P.op("act", lambda e, tf=tf, c=c, th=th: e.activation(
                    out=self.hT[:, c, th * 512:(th + 1) * 512], in_=tf[:], func=AF.Identity,
                    bias=sh_ap_fn(c), scale=1.0),
                    reads=[B["tmpf%d" % i], sh_buf], writes=[B["hT%d" % th]])

    def proj_residual(self, wname, rhs_fn, rhs_bufs, gate_ap_fn, gate_buf):
        P, B = self.P, self.B
        n = 0
        for u in range(8):
            wv, wb = self.ws.take((wname, u))
            for fc in range(2):
                m = u * 2 + fc
                for th in range(2):
                    bank = n % 4
                    n += 1

                    def fn(e, wv=wv, fc=fc, th=th, bank=bank):
                        for k in range(NCH):
                            ins = e.matmul(self.ps[bank][:], lhsT=wv[:, k, fc * 128:(fc + 1) * 128],
                                           rhs=rhs_fn(k, th), start=(k == 0), stop=(k == NCH - 1))
                        return ins
                    P.op("pe", fn, reads=[wb] + rhs_bufs(th), writes=[self.Bps[bank]])
                    P.op("dve", lambda e, m=m, th=th, bank=bank: e.scalar_tensor_tensor(
                        out=self.xT[:, m, th * 512:(th + 1) * 512], in0=self.ps[bank][:], scalar=gate_ap_fn(m),
                        in1=self.xT[:, m, th * 512:(th + 1) * 512], op0=ALU.mult, op1=ALU.add),
                        reads=[self.Bps[bank], gate_buf, B["xT%d" % th]], writes=[B["xT%d" % th]])
            self.ws.release()

    def ffn(self, L, gate_ap_fn, gate_buf, between=None):
        P, B, nc = self.P, self.B, self.nc
        with ExitStack() as ph:
            aT = [ph.enter_context(self.sbt("aT%d" % i, [128, 4, T], BF16)) for i in range(2)]
            BaT = [[Buf("aT%d_%d" % (i, th)) for th in range(2)] for i in range(2)]
            sg = [ph.enter_context(self.sbt("sg%d" % i, [128, 512], F32)) for i in range(2)]
            Bsg = [Buf("sg0"), Buf("sg1")]
            ny = 0
            nsg = 0
            for grp in range(11):
                ab = grp % 2
                for qq in range(2):
                    q = grp * 2 + qq
                    wg, wgb = self.ws.take((L + "_wi", q))
                    wu, wub = self.ws.take((L + "_wi", 22 + q))
                    for fc in range(2):
                        jj = qq * 2 + fc
                        for th in range(2):
                            bg = th
                            bu = 2 + th

                            def fg(e, wg=wg, fc=fc, th=th, bg=bg):
                                for k in range(NCH):
                                    ins = e.matmul(self.ps[bg][:], lhsT=wg[:, k, fc * 128:(fc + 1) * 128],
                                                   rhs=self.hT[:, k, th * 512:(th + 1) * 512], start=(k == 0), stop=(k == NCH - 1))
                                return ins
                            P.op("pe", fg, reads=[wgb, B["hT%d" % th]], writes=[self.Bps[bg]])

                            def fu(e, wu=wu, fc=fc, th=th, bu=bu):
                                for k in range(NCH):
                                    ins = e.matmul(self.ps[bu][:], lhsT=wu[:, k, fc * 128:(fc + 1) * 128],
                                                   rhs=self.hT[:, k, th * 512:(th + 1) * 512], start=(k == 0), stop=(k == NCH - 1))
                                return ins
                            P.op("pe", fu, reads=[wub, B["hT%d" % th]], writes=[self.Bps[bu]])
                            si = nsg % 2
                            nsg += 1
                            P.op("act", lambda e, si=si, bg=bg: e.activation(out=sg[si][:], in_=self.ps[bg][:], func=AF.Silu),
                                 reads=[self.Bps[bg]], writes=[Bsg[si]])
                            P.op("dve", lambda e, si=si, bu=bu, ab=ab, jj=jj, th=th: e.tensor_tensor(
                                out=aT[ab][:, jj, th * 512:(th + 1) * 512], in0=self.ps[bu][:], in1=sg[si][:], op=ALU.mult),
                                reads=[self.Bps[bu], Bsg[si]], writes=[BaT[ab][th]])
                    self.ws.release()
                wo0, wob0 = self.ws.take((L + "_wo", grp * 2))
                wo1, wob1 = self.ws.take((L + "_wo", grp * 2 + 1))
                wos = (wo0, wo1)
                for m in range(NCH):
                    for th in range(2):
                        bank = 4 + ny % 2
                        ny += 1

                        def fy(e, m=m, th=th, bank=bank, wos=wos, ab=ab):
                            for jj in range(4):
                                ins = e.matmul(self.ps[bank][:], lhsT=wos[jj // 2][:, jj % 2, m * 128:(m + 1) * 128],
                                               rhs=aT[ab][:, jj, th * 512:(th + 1) * 512], start=(jj == 0), stop=(jj == 3))
                            return ins
                        P.op("pe", fy, reads=[wob0, wob1, BaT[ab][th]], writes=[self.Bps[bank]])
                        P.op("dve", lambda e, m=m, th=th, bank=bank: e.scalar_tensor_tensor(
                            out=self.xT[:, m, th * 512:(th + 1) * 512], in0=self.ps[bank][:], scalar=gate_ap_fn(m),
                            in1=self.xT[:, m, th * 512:(th + 1) * 512], op0=ALU.mult, op1=ALU.add),
                            reads=[self.Bps[bank], gate_buf, B["xT%d" % th]], writes=[B["xT%d" % th]])
                self.ws.release()
        P.barrier()

    def rope_tables(self, pos_d):
        P, B, nc = self.P, self.B, self.nc
        B["rope"] = Buf("rope")
        with ExitStack() as ph:
            posi = ph.enter_context(self.sbt("posi", [128, T], I32))
            ua = ph.enter_context(self.sbt("rope_u", [128, T], F32))
            ub = ph.enter_context(self.sbt("rope_k", [128, T], F32))
            Bp, Ba, Bb = Buf("posi"), Buf("ua"), Buf("ub")
            io = COLS["invf"][0]
            for off, dst in ((0.5, self.sinS), (0.75, self.cosF)):
                P.op("sp", lambda e: e.dma_start(out=posi[:], in_=pos_d.broadcast_to([128, T])), writes=[Bp], dma=P.dma_sem())
                P.op("dve", lambda e: e.tensor_copy(out=ua[:], in_=posi[:]), reads=[Bp], writes=[Ba])
                P.op("dve", lambda e, off=off: e.tensor_scalar(out=ua[:], in0=ua[:], scalar1=self.cols[:, io:io + 1], scalar2=off,
                                                               op0=ALU.mult, op1=ALU.add),
                     reads=[Ba, B["cols"]], writes=[Ba])
                P.op("dve", lambda e: e.tensor_copy(out=posi[:], in_=ua[:]), reads=[Ba], writes=[Bp])
                P.op("dve", lambda e: e.tensor_copy(out=ub[:], in_=posi[:]), reads=[Bp], writes=[Bb])
                P.op("dve", lambda e: e.tensor_tensor(out=ua[:], in0=ua[:], in1=ub[:], op=ALU.subtract),
                     reads=[Ba, Bb], writes=[Ba])
                P.op("dve", lambda e: e.tensor_single_scalar(out=ub[:], in_=ua[:], scalar=0.0, op=ALU.is_lt),
                     reads=[Ba], writes=[Bb])
                P.op("dve", lambda e: e.tensor_tensor(out=ua[:], in0=ua[:], in1=ub[:], op=ALU.add),
                     reads=[Ba, Bb], writes=[Ba])
                P.op("act", lambda e, dst=dst: e.activation(out=dst[:], in_=ua[:], func=AF.Sin, bias=self.eps_col[:, 1:2],
                                                            scale=6.2831845),
                     reads=[Ba, B["consts"]], writes=[B["rope"]])
            sinS_ = self.sinS
            P.op("dve", lambda e: e.tensor_scalar(out=sinS_[64:128, :], in0=sinS_[64:128, :], scalar1=-1.0, scalar2=None,
                                                  op0=ALU.mult),
                 reads=[B["rope"]], writes=[B["rope"]])
        P.barrier()

    def qk_head(self, ps_bank, th, gcol_ap, dst_ap, dst_buf):
        P, B = self.P, self.B
        ps = self.ps[ps_bank]
        bss = 2 + th
        sq = self.sqb[0]
        P.op("act", lambda e: e.activation(out=sq[:, 0, :], in_=ps[:], func=AF.Square),
             reads=[self.Bps[ps_bank]], writes=[B["sqb0"]])
        P.op("pe", lambda e: e.matmul(self.ps[bss][:], lhsT=self.ones[:], rhs=sq[:, 0, :], start=True, stop=True),
             reads=[B["sqb0"], B["ones"]], writes=[self.Bps[bss]])
        t0, t1, t2 = self.tmpf
        P.op("act", lambda e: e.activation(out=t2[:], in_=self.ps[bss][:], func=AF.Sqrt, bias=self.eps_col[:, 0:1],
                                           scale=1.0 / 128),
             reads=[self.Bps[bss], B["consts"]], writes=[B["tmpf2"]])
        P.op("dve", lambda e: e.reciprocal(out=t2[:], in_=t2[:]), reads=[B["tmpf2"]], writes=[B["tmpf2"]])
        qn = self.qn
        P.op("dve", lambda e: e.scalar_tensor_tensor(out=qn[:], in0=ps[:], scalar=gcol_ap, in1=t2[:],
                                                     op0=ALU.mult, op1=ALU.mult),
             reads=[self.Bps[ps_bank], B["tmpf2"], B["cols"]], writes=[B["qn"]])
        sl = slice(th * 512, (th + 1) * 512)
        cosF, sinS = self.cosF, self.sinS
        P.op("dve", lambda e: e.tensor_tensor(out=t0[:], in0=qn[:], in1=cosF[:, sl], op=ALU.mult),
             reads=[B["qn"], B["rope"]], writes=[B["tmpf0"]])
        P.op("dve", lambda e: e.tensor_tensor(out=t1[0:64, :], in0=qn[64:128, :], in1=sinS[64:128, sl], op=ALU.mult),
             reads=[B["qn"], B["rope"]], writes=[B["tmpf1"]])
        P.op("dve", lambda e: e.tensor_tensor(out=t1[64:128, :], in0=qn[0:64, :], in1=sinS[0:64, sl], op=ALU.mult),
             reads=[B["qn"], B["rope"]], writes=[B["tmpf1"]])
        P.op("dve", lambda e: e.tensor_tensor(out=dst_ap, in0=t0[:], in1=t1[:], op=ALU.add),
             reads=[B["tmpf0"], B["tmpf1"]], writes=[dst_buf])

    def layer_a(self, rows_d, sguw_d, pos_d, KT_d, V_d):
        P, B, nc, sb = self.P, self.B, self.nc, self.sb
        adaA, adaK = self.adaA, self.adaK
        stage(0)
        phA = ExitStack()
        wmT = phA.enter_context(self.sbt("wmT", [128, 16, 128], BF16))
        B2 = phA.enter_context(self.sbt("B2", [128, 16, 128], F32))
        inbv = phA.enter_context(self.sbt("inbv", [1, D], BF16))
        for n in ["wmT", "B2", "inbv"]:
            B[n] = Buf(n)
        P.op("pool", lambda e: e.dma_start(out=inbv[:], in_=rows_d[0:1, :]), writes=[B["inbv"]], dma=P.dma_sem())
        with ExitStack() as ph:
            sguw = ph.enter_context(self.sbt("sguw", [128, 16, 128], F32))
            sguwb = ph.enter_context(self.sbt("sguwb", [128, 16, 128], BF16))
            sgub = ph.enter_context(self.sbt("sgub", [128, 16, 128], F32))
            identb = ph.enter_context(self.sbt("identb0", [128, 128], BF16))
            tri = ph.enter_context(self.sbt("tri", [128, 128], BF16))
            Bsw, Bswb, Bsb, Bid, Btri = Buf("sguw"), Buf("sguwb"), Buf("sgub"), Buf("ident"), Buf("tri")
            P.op("sp", lambda e: e.dma_start(out=sguw[:], in_=sguw_d.rearrange("g t s -> t g s")), writes=[Bsw], dma=P.dma_sem())
            P.op("sp", lambda e: e.dma_start(out=sgub[:].rearrange("p g t -> p (g t)"), in_=rows_d[2:3, :].broadcast_to([128, D])),
                 writes=[Bsb], dma=P.dma_sem())
            P.op("pool", lambda e: e.memset(identb[:], 1.0), writes=[Bid])
            P.op("pool", lambda e: e.affine_select(out=identb[:], in_=identb[:], pattern=[[-1, 128]], compare_op=ALU.is_equal,
                                                   fill=0.0, base=0, channel_multiplier=1), reads=[Bid], writes=[Bid])
            P.op("pool", lambda e: e.memset(tri[:], 1.0), writes=[Btri])
            P.op("pool", lambda e: e.affine_select(out=tri[:], in_=tri[:], pattern=[[1, 128]], compare_op=ALU.is_ge,
                                                   fill=0.0, base=0, channel_multiplier=-1), reads=[Btri], writes=[Btri])
            P.op("dve", lambda e: e.tensor_copy(out=sguwb[:], in_=sguw[:]), reads=[Bsw], writes=[Bswb])
            for g4 in range(4):
                def ft(e, g4=g4):
                    for gi in range(4):
                        g = g4 * 4 + gi
                        ins = e.transpose(out=self.psT[:, gi * 128:(gi + 1) * 128], in_=sguwb[:, g, :], identity=identb[:])
                    return ins
                P.op("pe", ft, reads=[Bswb, Bid], writes=[self.BpsT])
                for gi in range(4):
                    g = g4 * 4 + gi
                    P.op("dve", lambda e, g=g, gi=gi: e.tensor_tensor(
                        out=wmT[:, g, :], in0=self.psT[:, gi * 128:(gi + 1) * 128], in1=tri[:], op=ALU.mult),
                        reads=[self.BpsT, Btri], writes=[B["wmT"]])
            lb = COLS["a_ln_b"][0]
            for q in range(4):
                bank = q % 2
                P.op("pe", lambda e, q=q, bank=bank: e.matmul(
                    self.ps[bank][:], lhsT=self.ones[:], rhs=wmT[:, 4 * q:4 * q + 4, :], start=True, stop=True),
                    reads=[B["wmT"], B["ones"]], writes=[self.Bps[bank]])
                for gi in range(4):
                    g = 4 * q + gi
                    P.op("dve", lambda e, g=g, gi=gi, bank=bank: e.scalar_tensor_tensor(
                        out=B2[:, g, :], in0=self.ps[bank][:, gi * 128:(gi + 1) * 128], scalar=self.cols[:, lb + g:lb + g + 1],
                        in1=sgub[:, g, :], op0=ALU.mult, op1=ALU.add),
                        reads=[self.Bps[bank], Bsb, B["cols"]], writes=[B["B2"]])
        P.barrier()
        stage(1)

        self.norm_stats()
        stage(2)
        self.ada_group("a_ada", 0, adaA, B["adaA"], "a_ada_b")
        self.ada_group("a_ada", 1, adaA, B["adaA"], "a_ada_b")
        self.mk_gsc(0, "a_norm1_g", adaA[:, 16:32], B["adaA"])
        self.modulate(0, lambda c: adaA[:, c:c + 1], B["adaA"])
        stage(3)

        with ExitStack() as ph:
            VS = ph.enter_context(self.sbt("VS", [128, 8, 16, 128], BF16))
            BVS = [Buf("VS%d" % tb) for tb in range(8)]
            st1 = ph.enter_context(self.sbt("st1", [128, 8, 8], F32))
            st2 = ph.enter_context(self.sbt("st2", [128, 8, 8], F32))
            stt_ = ph.enter_context(self.sbt("stt", [128, 8, 4], F32))
            junk = ph.enter_context(self.sbt("junk", [128, 256], BF16))
            ug = [ph.enter_context(self.sbt("ug%d" % i, [128, 512], BF16)) for i in range(2)]
            Bst = [Buf("st%d" % tb) for tb in range(8)]
            Bjunk = Buf("junk")
            Bug = [Buf("ug0"), Buf("ug1")]
            P.op("dve", lambda e: e.memset(st1[:], 0.0), writes=Bst)
            P.op("dve", lambda e: e.memset(st2[:], 0.0), writes=Bst)
            nb = 0
            for cb in range(8):
                wv, wb = self.ws.take(("a_in", 8 + cb))
                for tb in range(8):
                    bank = nb % 4
                    nb += 1

                    def fv(e, wv=wv, tb=tb, bank=bank, cb=cb):
                        o = self.ps[bank][:, 0:256]
                        for k in range(NCH):
                            e.matmul(o, lhsT=self.hT[:, k, tb * 128:(tb + 1) * 128], rhs=wv[:, k, :], start=(k == 0), stop=False)
                        return e.matmul(o, lhsT=self.ones[0:1, :], rhs=inbv[0:1, cb * 256:(cb + 1) * 256], start=False, stop=True)
                    P.op("pe", fv, reads=[wb, B["hT%d" % (tb // 4)], B["inbv"], B["ones"]], writes=[self.Bps[bank]])
                    vdst = VS[:, tb, 2 * cb:2 * cb + 2, :]
                    P.op("act", lambda e, vdst=vdst, bank=bank, tb=tb, cb=cb: e.activation(
                        out=vdst, in_=self.ps[bank][:, 0:256].rearrange("p (a b) -> p a b", a=2), func=AF.Gelu,
                        accum_out=st1[:, tb, cb:cb + 1]),
                        reads=[self.Bps[bank]], writes=[BVS[tb], Bst[tb]])
                    P.op("act", lambda e, vdst=vdst, tb=tb, cb=cb: e.activation(
                        out=junk[:].rearrange("p (a b) -> p a b", a=2), in_=vdst, func=AF.Square,
                        accum_out=st2[:, tb, cb:cb + 1]),
                        reads=[BVS[tb]], writes=[Bjunk, Bst[tb]])
                self.ws.release()
            stage(4)
            X = mybir.AxisListType.X
            for tb in range(8):
                P.chain("dve", [
                    lambda e, tb=tb: e.tensor_reduce(out=stt_[:, tb, 0:1], in_=st1[:, tb, :], axis=X, op=ALU.add),
                    lambda e, tb=tb: e.tensor_reduce(out=stt_[:, tb, 1:2], in_=st2[:, tb, :], axis=X, op=ALU.add),
                    lambda e, tb=tb: e.tensor_scalar(out=stt_[:, tb, 0:1], in0=stt_[:, tb, 0:1], scalar1=1.0 / D, scalar2=None, op0=ALU.mult),
                    lambda e, tb=tb: e.tensor_tensor(out=stt_[:, tb, 2:3], in0=stt_[:, tb, 0:1], in1=stt_[:, tb, 0:1], op=ALU.mult),
                    lambda e, tb=tb: e.scalar_tensor_tensor(out=stt_[:, tb, 1:2], in0=stt_[:, tb, 1:2], scalar=1.0 / D,
                                                            in1=stt_[:, tb, 2:3], op0=ALU.mult, op1=ALU.subtract),
                ], reads=[Bst[tb]], writes=[Bst[tb]])
                P.op("act", lambda e, tb=tb: e.activation(out=stt_[:, tb, 2:3], in_=stt_[:, tb, 1:2], func=AF.Sqrt,
                                                          bias=self.eps_col[:, 0:1], scale=1.0),
                     reads=[Bst[tb], B["consts"]], writes=[Bst[tb]])
                P.chain("dve", [
                    lambda e, tb=tb: e.reciprocal(out=stt_[:, tb, 2:3], in_=stt_[:, tb, 2:3]),
                    lambda e, tb=tb: e.scalar_tensor_tensor(out=stt_[:, tb, 3:4], in0=stt_[:, tb, 0:1], scalar=-1.0,
                                                            in1=stt_[:, tb, 2:3], op0=ALU.mult, op1=ALU.mult),
                ], reads=[Bst[tb]], writes=[Bst[tb]])
                P.op("dve", lambda e, tb=tb: e.tensor_scalar(
                    out=VS[:, tb, :, :], in0=VS[:, tb, :, :], scalar1=stt_[:, tb, 2:3], scalar2=stt_[:, tb, 3:4],
                    op0=ALU.mult, op1=ALU.add),
                    reads=[BVS[tb], Bst[tb]], writes=[BVS[tb]])
            lg = COLS["a_ln_g"][0]
            ns = 0
            for tb in range(8):
                for g4 in range(4):
                    bank = 4 + ns % 2
                    ns += 1

                    def fsg(e, tb=tb, g4=g4, bank=bank):
                        for gi in range(4):
                            g = g4 * 4 + gi
                            ins = e.matmul(self.ps[bank][:, gi * 128:(gi + 1) * 128], lhsT=VS[:, tb, g, :], rhs=wmT[:, g, :],
                                           start=True, stop=True)
                        return ins
                    P.op("pe", fsg, reads=[BVS[tb], B["wmT"]], writes=[self.Bps[bank]])
                    for gi in range(4):
                        g = g4 * 4 + gi
                        P.op("dve", lambda e, tb=tb, g=g, gi=gi, bank=bank: e.scalar_tensor_tensor(
                            out=VS[:, tb, g, :], in0=self.ps[bank][:, gi * 128:(gi + 1) * 128],
                            scalar=self.cols[:, lg + g:lg + g + 1], in1=B2[:, g, :], op0=ALU.mult, op1=ALU.add),
                            reads=[self.Bps[bank], B["B2"], B["cols"]], writes=[BVS[tb]])
            stage(5)
            bo = COLS["a_in_b_u"][0]
            nu = 0
            for u in range(8):
                wv, wb = self.ws.take(("a_in", u))
                for fc in range(2):
                    g = u * 2 + fc
                    for th in range(2):
                        bank = nu % 4
                        i = nu % 2
                        nu += 1

                        def fu(e, wv=wv, fc=fc, th=th, bank=bank):
                            for k in range(NCH):
                                ins = e.matmul(self.ps[bank][:], lhsT=wv[:, k, fc * 128:(fc + 1) * 128],
                                               rhs=self.hT[:, k, th * 512:(th + 1) * 512], start=(k == 0), stop=(k == NCH - 1))
                            return ins
                        P.op("pe", fu, reads=[wb, B["hT%d" % th]], writes=[self.Bps[bank]])
                        P.op("act", lambda e, i=i, bank=bank, g=g: e.activation(
                            out=ug[i][:], in_=self.ps[bank][:], func=AF.Gelu, bias=self.cols[:, bo + g:bo + g + 1], scale=1.0),
                            reads=[self.Bps[bank], B["cols"]], writes=[Bug[i]])
                        P.op("dve", lambda e, i=i, g=g, th=th: e.tensor_tensor(
                            out=VS[:, 4 * th:4 * th + 4, g, :], in0=VS[:, 4 * th:4 * th + 4, g, :],
                            in1=ug[i][:].rearrange("p (a b) -> p a b", a=4), op=ALU.mult),
                            reads=[Bug[i]] + BVS[4 * th:4 * th + 4], writes=BVS[4 * th:4 * th + 4])
                self.ws.release()
            self.ada_group("a_ada", 2, adaA, B["adaA"], "a_ada_b")
            self.proj_residual("a_out", lambda k, th: VS[:, 4 * th:4 * th + 4, k, :],
                               lambda th: BVS[4 * th:4 * th + 4], lambda m: adaA[:, 32 + m:33 + m], B["adaA"])
        phA.close()
        P.barrier()
        stage(6)
        for g in (3, 4, 5):
            self.ada_group("a_ada", g, adaA, B["adaA"], "a_ada_b")
        self.norm_stats()
        self.mk_gsc(1, "a_norm2_g", adaA[:, 64:80], B["adaA"])
        self.modulate(1, lambda c: adaA[:, 48 + c:49 + c], B["adaA"])
        self.ffn("a", lambda m: adaA[:, 80 + m:81 + m], B["adaA"])
        stage(7)
        self.ada_group("kv_ada", 0, adaK, B["adaK"], "kv_ada_b")
        self.ada_group("kv_ada", 1, adaK, B["adaK"], "kv_ada_b")
        self.norm_stats()
        self.mk_gsc(2, "kv_norm_g", adaK[:, 16:32], B["adaK"])
        self.modulate(2, lambda c: adaK[:, c:c + 1], B["adaK"])
        with ExitStack() as ph:
            self.cosF = ph.enter_context(self.sbt("cosF", [128, T], F32))
            self.sinS = ph.enter_context(self.sbt("sinS", [128, T], F32))
            self.rope_tables(pos_d)
            self.qn = ph.enter_context(self.sbt("qn", [128, 512], F32))
            B["qn"] = Buf("qn")
            kt = [ph.enter_context(self.sbt("kt%d" % i, [128, 512], BF16)) for i in range(2)]
            Bkt = [Buf("kt0"), Buf("kt1")]
            vt = [ph.enter_context(self.sbt("vt%d" % i, [128, 256], BF16)) for i in range(2)]
            Bvt = [Buf("vt0"), Buf("vt1")]
            kg = self.col("k_norm_g")
            dks = [P.dma_sem(), P.dma_sem()]
            dvs = [P.dma_sem(), P.dma_sem()]
            self.kv_out_sems = dks + dvs
            nk = 0
            for u in range(8):
                wv, wb = self.ws.take(("kv", u))
                for fc in range(2):
                    hh = u * 2 + fc
                    for th in range(2):
                        bank = th
                        i = nk % 2
                        nk += 1

                        def fk(e, wv=wv, fc=fc, th=th, bank=bank):
                            for k in range(NCH):
                                ins = e.matmul(self.ps[bank][:], lhsT=wv[:, k, fc * 128:(fc + 1) * 128],
                                               rhs=self.hT[:, k, th * 512:(th + 1) * 512], start=(k == 0), stop=(k == NCH - 1))
                            return ins
                        P.op("pe", fk, reads=[wb, B["hT%d" % th]], writes=[self.Bps[bank]])
                        self.qk_head(bank, th, kg, kt[i][:], Bkt[i])
                        P.op("sp", lambda e, i=i, hh=hh, th=th: e.dma_start(out=KT_d[hh][:, th * 512:(th + 1) * 512], in_=kt[i][:]),
                             reads=[Bkt[i]], dma=dks[i])
                self.ws.release()
                if self.mode == "F":
                    P.op("pool", lambda e, u=u: e.collective_compute("AllGather", ALU.bypass, replica_groups=[[0, 1, 2, 3], [4, 5, 6, 7]],
                                                                     ins=[self.kt_loc[u]], outs=[self.kt_all[u]]),
                         extra_waits=[(d_, P.cnt[d_]) for d_ in dks], writes=[self.Bktall[u]], dma=P.dma_sem(), dma_inc=1)
            nv = 0
            for u in range(8):
                wv, wb = self.ws.take(("kv", 8 + u))
                for tb in range(8):
                    bank = 4 + nv % 3
                    i = nv % 2
                    nv += 1

                    def fvv(e, wv=wv, tb=tb, bank=bank):
                        for k in range(NCH):
                            ins = e.matmul(self.ps[bank][:, 0:256], lhsT=self.hT[:, k, tb * 128:(tb + 1) * 128], rhs=wv[:, k, :],
                                           start=(k == 0), stop=(k == NCH - 1))
                        return ins
                    P.op("pe", fvv, reads=[wb, B["hT%d" % (tb // 4)]], writes=[self.Bps[bank]])
                    P.op("act", lambda e, i=i, bank=bank: e.activation(out=vt[i][:], in_=self.ps[bank][:, 0:256], func=AF.Copy),
                         reads=[self.Bps[bank]], writes=[Bvt[i]])
                    P.op("sp", lambda e, i=i, u=u, tb=tb: e.dma_start(out=V_d[u][tb * 128:(tb + 1) * 128, :], in_=vt[i][:]),
                         reads=[Bvt[i]], dma=dvs[i])
                self.ws.release()
                if self.mode == "F":
                    P.op("pool", lambda e, u=u: e.collective_compute("AllGather", ALU.bypass, replica_groups=[[0, 1, 2, 3], [4, 5, 6, 7]],
                                                                     ins=[self.v_loc[u]], outs=[self.v_all[u]]),
                         extra_waits=[(d_, P.cnt[d_]) for d_ in dvs], writes=[self.Bvall[u]], dma=P.dma_sem(), dma_inc=1)
            if self.mode == "F":
                Bdl = Buf("dmloc")
                P.op("sp", lambda e: e.dma_start(out=self.dm_loc, in_=self.ones[:]), reads=[B["ones"]], writes=[Bdl], dma=P.dma_sem())
                P.op("pool", lambda e: e.collective_compute("AllGather", ALU.bypass, replica_groups=[[0, 1, 2, 3], [4, 5, 6, 7]],
                                                            ins=[self.dm_loc], outs=[self.dm_all]),
                     reads=[Bdl], writes=[self.Bdummy], dma=P.dma_sem(), dma_inc=1)
        P.barrier()

    def layer_b(self, KTf_d, Vf_d, mask_d, pos_d):
        P, B, nc, sb = self.P, self.B, self.nc, self.sb
        adaA = self.adaA
        QT = sb("QT", [128, NCH, T], BF16)
        BQ = [[Buf("QT%d_%d" % (hd, m)) for m in range(4)] for hd in range(8)]
        mask = sb("mask", [128, 8, 128], BF16)
        lamt = sb("lamt", [128, 8], F32)
        lamb = sb("lamb", [128, 2], BF16)
        B["mask"] = Buf("mask")
        B["lamt"] = Buf("lamt")
        P.op("pool", lambda e: e.dma_start(out=mask[:], in_=mask_d), writes=[B["mask"]], dma=P.dma_sem())
        stage(10)
        lo = COLS["lam"][0]
        so = COLS["subln_g"][0]
        P.op("dve", lambda e: e.tensor_tensor(out=lamb[:, 0:1], in0=self.cols[:, lo:lo + 1], in1=self.cols[:, lo + 1:lo + 2], op=ALU.mult),
             reads=[B["cols"]], writes=[B["lamt"]])
        P.op("dve", lambda e: e.tensor_tensor(out=lamb[:, 1:2], in0=self.cols[:, lo + 2:lo + 3], in1=self.cols[:, lo + 3:lo + 4], op=ALU.mult),
             reads=[B["cols"]], writes=[B["lamt"]])
        P.op("pe", lambda e: e.matmul(self.ps[5][:, 0:2], lhsT=self.ones[:], rhs=lamb[:, 0:2], start=True, stop=True),
             reads=[B["lamt"], B["ones"]], writes=[self.Bps[5]])
        P.op("act", lambda e: e.activation(out=lamt[:, 2:4], in_=self.ps[5][:, 0:2], func=AF.Exp),
             reads=[self.Bps[5]], writes=[B["lamt"]])
        P.chain("dve", [
            lambda e: e.scalar_tensor_tensor(out=lamt[:, 4:5], in0=lamt[:, 2:3], scalar=LAM_INIT, in1=lamt[:, 3:4],
                                             op0=ALU.add, op1=ALU.subtract),
            lambda e: e.tensor_scalar(out=lamt[:, 5:6], in0=lamt[:, 4:5], scalar1=-1.0, scalar2=None, op0=ALU.mult),
            lambda e: e.tensor_scalar(out=lamt[:, 6:8], in0=self.cols[:, so:so + 2], scalar1=1.0 - LAM_INIT, scalar2=None, op0=ALU.mult),
        ], reads=[B["lamt"], B["cols"]], writes=[B["lamt"]])

        stage(11)
        self.norm_stats()
        self.ada_group("b_ada", 0, adaA, B["adaA"], "b_ada_b")
        self.ada_group("b_ada", 1, adaA, B["adaA"], "b_ada_b")
        self.mk_gsc(0, "b_norm1_g", adaA[:, 16:32], B["adaA"])
        self.modulate(0, lambda c: adaA[:, c:c + 1], B["adaA"])
        stage(12)
        with ExitStack() as ph:
            self.cosF = ph.enter_context(self.sbt("cosF", [128, T], F32))
            self.sinS = ph.enter_context(self.sbt("sinS", [128, T], F32))
            self.rope_tables(pos_d)
            self.qn = ph.enter_context(self.sbt("qn", [128, 512], F32))
            B["qn"] = Buf("qn")
            qg = self.col("q_norm_g")
            for u in range(8):
                wv, wb = self.ws.take(("b_q", u))
                for fc in range(2):
                    hh = u * 2 + fc
                    for th in range(2):
                        bank = th

                        def fk(e, wv=wv, fc=fc, th=th, bank=bank):
                            for k in range(NCH):
                                ins = e.matmul(self.ps[bank][:], lhsT=wv[:, k, fc * 128:(fc + 1) * 128],
                                               rhs=self.hT[:, k, th * 512:(th + 1) * 512], start=(k == 0), stop=(k == NCH - 1))
                            return ins
                        P.op("pe", fk, reads=[wb, B["hT%d" % th]], writes=[self.Bps[bank]])
                        self.qk_head(bank, th, qg, QT[:, hh, th * 512:(th + 1) * 512], BQ[u][2 * th])
                        BQ[u][2 * th + 1].w = BQ[u][2 * th].w
                self.ws.release()
        P.barrier()
        stage(13)

        with ExitStack() as ph:
            ex = [ph.enter_context(self.sbt("kvx%d" % i, [128, 4096], BF16)) for i in range(1)]
            slots = [self.hT[:, 4 * i:4 * i + 4, :].rearrange("p c t -> p (c t)") for i in range(4)] + [e_[:] for e_ in ex]
            NS = len(slots)
            sbufs = [Buf("kvs%d" % i) for i in range(NS)]
            ssems = [P.dma_sem() for _ in range(NS)]
            plan = []
            fused = self.mode == "F"
            for hd in range(8):
                if fused:
                    plan.append(("ktf", KTf_d[2 * hd], hd))
                    plan.append(("ktf", KTf_d[2 * hd + 1], hd))
                    plan.append(("vf", (Vf_d[hd][0], Vf_d[hd][1]), hd))
                    plan.append(("vf", (Vf_d[hd][2], Vf_d[hd][3]), hd))
                else:
                    plan.append(("kt", KTf_d[2 * hd], hd))
                    plan.append(("kt", KTf_d[2 * hd + 1], hd))
                    plan.append(("v", Vf_d[hd, 0], hd))
                    plan.append(("v", Vf_d[hd, 1], hd))

            def kmap(kb):
                if not fused:
                    return kb * 128, kb // 16, kb % 16
                m8, r8 = kb // 8, kb % 8
                if r8 < 4:
                    j, sl_ = r8, 2 * m8
                else:
                    j, sl_ = 7 - r8, 2 * m8 + 1
                return j * 1024 + sl_ * 128, j // 2, (j % 2) * 8 + sl_
            st = {"issued": 0, "closed": 0}

            def kview(s, kind):
                if kind == "kt":
                    return slots[s]
                if kind == "ktf":
                    return slots[s].rearrange("p (j t) -> p j t", j=4)
                if kind == "vf":
                    return slots[s].rearrange("p (j s f) -> p j s f", j=2, s=8)
                return slots[s].rearrange("p (k f) -> p k f", k=16)

            def pump():
                while st["issued"] < len(plan) and st["issued"] - NS < st["closed"]:
                    i = st["issued"]
                    kind, src, phd = plan[i]
                    s = i % NS
                    dst = kview(s, kind)
                    if kind == "vf":
                        for jj in range(2):
                            P.op("sp", (lambda dst, src: (lambda e: e.dma_start(out=dst, in_=src)))(dst[:, jj], src[jj]),
                                 reads=[self.Bvall[phd], self.Bdummy], writes=[sbufs[s]], dma=ssems[s])
                    else:
                        P.op("sp", (lambda dst, src: (lambda e: e.dma_start(out=dst, in_=src)))(dst, src),
                             reads=([self.Bktall[phd], self.Bdummy] if kind == "ktf" else []),
                             writes=[sbufs[s]], dma=ssems[s])
                    st["issued"] += 1

            pT = [self.sqb[0][:, 0, :], self.sqb[0][:, 1, :], self.sqb[1][:, 0, :], self.sqb[1][:, 1, :]]
            BpT = [Buf("pT%d" % i) for i in range(4)]
            accs = [ph.enter_context(self.sbt("accs%d" % i, [128, 260], F32)) for i in range(4)]
            Baccs = [Buf("accs%d" % i) for i in range(4)]
            fin = ph.enter_context(self.sbt("fin", [128, 8], F32))
            Bfin = Buf("fin")
            onbs = [ph.enter_context(self.sbt("onb%d" % i, [128, 256], BF16)) for i in range(2)]
            Bonbs = [Buf("onb0"), Buf("onb1")]
            identb = ph.enter_context(self.sbt("identb", [128, 128], BF16))
            Bidb = Buf("identb")
            P.op("pool", lambda e: e.memset(identb[:], 1.0), writes=[Bidb])
            P.op("pool", lambda e: e.affine_select(out=identb[:], in_=identb[:], pattern=[[-1, 128]], compare_op=ALU.is_equal,
                                                   fill=0.0, base=0, channel_multiplier=1), reads=[Bidb], writes=[Bidb])
            t0, t1, t2 = self.tmpf
            scale = 128.0 ** -0.5
            npt = 0
            nst = 0
            ada_next = 16
            for hd in range(8):
                base = hd * 4
                pump()
                if hd == 0 and self.DBG:
                    if fused:
                        P.op("sp", lambda e: e.dma_start(out=self.dbgA[1], in_=self.kt_all[0]), reads=[self.Bktall[0]], dma=self.do)
                    P.op("sp", lambda e: e.dma_start(out=self.dbgK, in_=slots[0]), reads=[sbufs[0]], dma=self.do)
                    P.op("sp", lambda e: e.dma_start(out=self.dbgV, in_=slots[2]), reads=[sbufs[2]], dma=self.do)
                kts = [kview((base + i) % NS, "kt") for i in range(2)]
                vs = [kview((base + 2 + i) % NS, "v") for i in range(2)]
                kb_ = [sbufs[(base + i) % NS] for i in range(4)]
                steps = [(m, kb) for m in range(4) for kb in range(8 * m + 8)]
                info = {}
                SB = (0, 1)

                def emit_S(idx, hd=hd, kts=kts, kb_=kb_):
                    nonlocal nst, npt
                    m, kb = steps[idx]
                    kc, vh, vb = kmap(kb)
                    full = kb < 8 * m + 4
                    off = 0 if full else 128
                    sbank = SB[nst % 2]
                    nst += 1
                    pi = npt % 4
                    npt += 1
                    info[idx] = (pi, full, vh, vb)
                    psS = self.ps[sbank]

                    def fs(e):
                        for sub in range(2):
                            ins = e.matmul(psS[:, sub * 256 + off:sub * 256 + 256],
                                           lhsT=kts[sub][:, kc:kc + 128],
                                           rhs=QT[:, 2 * hd + sub, 256 * m + off:256 * m + 256], start=True, stop=True)
                        return ins
                    P.op("pe", fs, reads=[kb_[0], kb_[1], BQ[hd][m]], writes=[self.Bps[sbank]])
                    src3 = psS[:].rearrange("p (s q) -> p s q", s=2)[:, :, off:256]
                    dst3 = pT[pi].rearrange("p (s q) -> p s q", s=2)[:, :, off:256]
                    P.op("act", lambda e: e.activation(out=dst3, in_=src3, func=AF.Exp, scale=scale),
                         reads=[self.Bps[sbank]], writes=[BpT[pi]])
                    if kb >= 8 * m:
                        par = 0 if full else 1
                        r = kb - 8 * m - 4 * par
                        moff = 0 if par == 0 else 128
                        for sub in range(2):
                            dm = pT[pi][:, sub * 256 + moff:sub * 256 + moff + 128]
                            P.op("dve", lambda e, dm=dm: e.tensor_tensor(
                                out=dm, in0=dm, in1=mask[:, par * 4 + r, :], op=ALU.mult),
                                reads=[BpT[pi], B["mask"]], writes=[BpT[pi]])

                def emit_PV(idx, hd=hd, vs=vs, kb_=kb_):
                    m, kb = steps[idx]
                    pi, full, vh, vb = info[idx]
                    vv = vs[vh][:, vb, :]

                    def fpv(e):
                        for sl in ((0, 1) if full else (1,)):
                            last = (8 * m + 3) if sl == 0 else (8 * m + 7)
                            for sub in range(2):
                                acc = self.ps[2 + sl * 2 + sub]
                                lh = pT[pi][:, sub * 256 + sl * 128:sub * 256 + sl * 128 + 128]
                                e.matmul(acc[:, 0:256], lhsT=lh, rhs=vv, start=(kb == 0), stop=(kb == last),
                                         skip_group_check=True)
                                ins = e.matmul(acc[:, 256:257], lhsT=lh, rhs=self.ones[:, 0:1], start=False, stop=(kb == last),
                                               skip_group_check=True)
                        return ins
                    P.op("pe", fpv, reads=[BpT[pi], kb_[2 + vh], B["ones"]],
                         writes=[self.Bps[2 + sl * 2 + sub] for sl in ((0, 1) if full else (1,)) for sub in range(2)])
                    if kb == 8 * m + 7:
                        finalize(m)
                        pending.append((idx + 6, m))

                def finalize(m, hd=hd):
                    for a in range(4):
                        P.op("act", lambda e, a=a: e.activation(out=accs[a][:, 0:257], in_=self.ps[2 + a][:, 0:257], func=AF.Copy),
                             reads=[self.Bps[2 + a]], writes=[Baccs[a]])
                    for sl in range(2):
                        aA, aB = accs[2 * sl], accs[2 * sl + 1]
                        onb, Bonb = onbs[sl], Bonbs[sl]
                        P.chain("dve", [
                            lambda e, aA=aA: e.reciprocal(out=fin[:, 0:1], in_=aA[:, 256:257]),
                            lambda e, aB=aB: e.reciprocal(out=fin[:, 1:2], in_=aB[:, 256:257]),
                            lambda e: e.tensor_tensor(out=fin[:, 2:3], in0=fin[:, 1:2], in1=lamt[:, 5:6], op=ALU.mult),
                        ], reads=[Baccs[2 * sl], Baccs[2 * sl + 1], Bfin, B["lamt"]], writes=[Bfin])
                        P.op("dve", lambda e, aB=aB: e.tensor_scalar(out=t0[:, 0:256], in0=aB[:, 0:256], scalar1=fin[:, 2:3], scalar2=None,
                                                                     op0=ALU.mult),
                             reads=[Baccs[2 * sl + 1], Bfin], writes=[B["tmpf0"]])
                        P.op("dve", lambda e, aA=aA: e.scalar_tensor_tensor(out=t1[:, 0:256], in0=aA[:, 0:256], scalar=fin[:, 0:1],
                                                                            in1=t0[:, 0:256], op0=ALU.mult, op1=ALU.add),
                             reads=[Baccs[2 * sl], Bfin, B["tmpf0"]], writes=[B["tmpf1"]])
                        P.op("act", lambda e: e.activation(out=t2[:, 0:256], in_=t1[:, 0:256], func=AF.Square, accum_out=fin[:, 3:4]),
                             reads=[B["tmpf1"], Bfin], writes=[B["tmpf2"], Bfin])
                        P.op("act", lambda e: e.activation(out=fin[:, 4:5], in_=fin[:, 3:4], func=AF.Sqrt, bias=self.eps_col[:, 0:1],
                                                           scale=1.0 / 256),
                             reads=[Bfin, B["consts"]], writes=[Bfin])
                        P.op("dve", lambda e: e.reciprocal(out=fin[:, 5:6], in_=fin[:, 4:5]), reads=[Bfin], writes=[Bfin])
                        P.op("dve", lambda e, onb=onb: e.tensor_scalar(out=onb[:], in0=t1[:, 0:256], scalar1=fin[:, 5:6], scalar2=None, op0=ALU.mult),
                             reads=[B["tmpf1"], Bfin], writes=[Bonb])

                def finalize_b(m, hd=hd):
                    def ftr(e):
                        for sl in range(2):
                            for c2 in range(2):
                                ins = e.transpose(out=self.psT[:, sl * 256 + c2 * 128:sl * 256 + (c2 + 1) * 128],
                                                  in_=onbs[sl][:, c2 * 128:(c2 + 1) * 128], identity=identb[:])
                        return ins
                    P.op("pe", ftr, reads=[Bonbs[0], Bonbs[1], Bidb], writes=[self.BpsT])
                    for c2 in range(2):
                        P.op("dve", lambda e, c2=c2: e.tensor_scalar(
                            out=QT[:, 2 * hd + c2, 256 * m:256 * m + 256].rearrange("p (s q) -> p s q", s=2),
                            in0=self.psT[:, 0:512].rearrange("p (s c q) -> p s c q", s=2, c=2)[:, :, c2, :],
                            scalar1=lamt[:, 6 + c2:7 + c2], scalar2=None, op0=ALU.mult),
                            reads=[self.BpsT, B["lamt"]], writes=[BQ[hd][m]])

                LA = 1
                pending = []
                for idx in range(len(steps) + LA):
                    if idx < len(steps):
                        emit_S(idx)
                    if idx >= LA:
                        emit_PV(idx - LA)
                    while pending and pending[0][0] <= idx - LA:
                        finalize_b(pending.pop(0)[1])
                while pending:
                    finalize_b(pending.pop(0)[1])
                st["closed"] = base + 4
                if STOP <= 14 + hd:
                    ada_next = 48
                    break
                for _ in range(4):
                    self.ada_unit("b_ada", ada_next, adaA, B["adaA"], "b_ada_b")
                    ada_next += 1
            assert ada_next == 48
        P.barrier()
        stage(21)
        allQ = [BQ[hd][m] for hd in range(8) for m in range(4)]
        self.proj_residual("b_o", lambda k, th: QT[:, k, th * 512:(th + 1) * 512], lambda th: allQ,
                           lambda m: adaA[:, 32 + m:33 + m], B["adaA"])
        stage(22)
        self.norm_stats()
        self.mk_gsc(1, "b_norm2_g", adaA[:, 64:80], B["adaA"])
        self.modulate(1, lambda c: adaA[:, 48 + c:49 + c], B["adaA"])
        self.ffn("b", lambda m: adaA[:, 80 + m:81 + m], B["adaA"])


_NC_CACHE = {}


def get_nc(mode):
    if mode not in _NC_CACHE:
        _NC_CACHE[mode] = Builder(mode).build()
    return _NC_CACHE[mode]


def host_cols(inp, b):
    cols = np.zeros((128, NCOL), np.float32)

    def put(name, arr):
        o, w = COLS[name]
        cols[:, o:o + w] = arr
    put("a_norm1_g", colv(inp["a_norm1_g"][0]))
    put("a_norm2_g", colv(inp["a_norm2_g"][0]))
    put("a_in_b_u", colv(inp["a_in_b"][0][:D]))
    put("a_ln_g", colv(inp["a_sgu_ln_g"][0]))
    put("a_ln_b", colv(inp["a_sgu_ln_b"][0]))
    put("kv_norm_g", colv(inp["kv_norm_g"]))
    put("b_norm1_g", colv(inp["b_norm1_g"][0]))
    put("b_norm2_g", colv(inp["b_norm2_g"][0]))
    put("a_ada_b", colv(inp["a_ada_b"][0]))
    put("kv_ada_b", colv(inp["kv_ada_b"]))
    put("b_ada_b", colv(inp["b_ada_b"][0]))
    put("k_norm_g", colv(inp["k_norm_g"]))
    put("q_norm_g", colv(inp["b_q_norm_g"][0]))
    put("subln_g", colv(inp["b_subln_g"][0]))
    put("lam", np.stack([inp["b_lambda_q1"][0], inp["b_lambda_k1"][0], inp["b_lambda_q2"][0], inp["b_lambda_k2"][0]], 1))
    inv_freq = 1.0 / (10000.0 ** (np.arange(0, 128, 2, dtype=np.float32) / 128.0))
    put("invf", (np.concatenate([inv_freq, inv_freq]) / (2 * np.pi)).astype(np.float32)[:, None])
    put("cT", colv(inp["c"][b]))
    return cols


def host_mask(j):
    m = np.zeros((128, 8, 128), np.float32)
    tri = (np.arange(128)[:, None] <= np.arange(128)[None, :]).astype(np.float32)
    for par in range(2):
        p = j if par == 0 else 7 - j
        for r in range(4):
            kbk = r if par == 0 else 4 + r
            if kbk < p:
                m[:, par * 4 + r, :] = 1.0
            elif kbk == p:
                m[:, par * 4 + r, :] = tri
    return m


def run_a(inp, ncores=8):
    nc = get_nc("A")
    shared = {
        "a_ada_w": tile_w(inp["a_ada_w"][0]), "a_in_w": tile_w(inp["a_in_w"][0]), "a_out_w": tile_w(inp["a_out_w"][0]),
        "a_ffn_wi": tile_w(inp["a_ffn_wi"][0]), "a_ffn_wo": tile_wo(inp["a_ffn_wo"][0]),
        "kv_ada_w": tile_w(inp["kv_ada_w"]), "kv_w": tile_w(inp["kv_w"]),
        "sgu_w": np.ascontiguousarray(inp["a_sgu_w"][0]),
        "rows": np.ascontiguousarray(np.stack([inp["a_in_b"][0][D:], inp["a_sgu_ln_b"][0], inp["a_sgu_b"][0].reshape(-1)]).astype(np.float32)),
    }
    in_maps = []
    for i in range(8):
        b, j = i // 4, i % 4
        idx = tok_index(j)
        m = dict(shared)
        m["xT"] = np.ascontiguousarray(inp["x"][b][idx].T)
        m["pos"] = np.ascontiguousarray(inp["positions"][b][idx][None, :]).astype(np.int32)
        m["cols"] = host_cols(inp, b)
        in_maps.append(m)
    in_maps = in_maps[:ncores]
    res = run_bass_kernel_spmd(nc, in_maps, core_ids=list(range(ncores)))
    return res.results


def run_b(inp, ra, ncores=8):
    nc = get_nc("B")
    shared = {
        "b_ada_w": tile_w(inp["b_ada_w"][0]), "b_q_w": tile_w(inp["b_q_w"][0]), "b_o_w": tile_w(inp["b_o_w"][0]),
        "b_ffn_wi": tile_w(inp["b_ffn_wi"][0]), "b_ffn_wo": tile_wo(inp["b_ffn_wo"][0]),
    }
    KTf, Vf = [], []
    for b in range(2):
        kt = np.zeros((16, 128, 4096), ml_dtypes.bfloat16)
        v = np.zeros((8, 4096, 256), ml_dtypes.bfloat16)
        for j in range(4):
            idx = tok_index(j)
            kt[:, :, idx] = ra[4 * b + j]["KT"]
            v[:, idx, :] = ra[4 * b + j]["V"]
        KTf.append(kt)
        Vf.append(np.ascontiguousarray(v.reshape(8, 2, 16, 128, 256).transpose(0, 1, 3, 2, 4)))
    in_maps = []
    for i in range(8):
        b, j = i // 4, i % 4
        idx = tok_index(j)
        m = dict(shared)
        m["xT"] = np.ascontiguousarray(ra[i]["outT"])
        m["pos"] = np.ascontiguousarray(inp["positions"][b][idx][None, :]).astype(np.int32)
        m["cols"] = host_cols(inp, b)
        m["KTf"] = KTf[b]
        m["Vf"] = Vf[b]
        m["mask"] = host_mask(j)
        in_maps.append(m)
    in_maps = in_maps[:ncores]
    res = run_bass_kernel_spmd(nc, in_maps, core_ids=list(range(ncores)))
    return res.results


def run_f(inp, ncores=8):
    nc = get_nc("F")
    shared = {
        "a_ada_w": tile_w(inp["a_ada_w"][0]), "a_in_w": tile_w(inp["a_in_w"][0]), "a_out_w": tile_w(inp["a_out_w"][0]),
        "a_ffn_wi": tile_w(inp["a_ffn_wi"][0]), "a_ffn_wo": tile_wo(inp["a_ffn_wo"][0]),
        "kv_ada_w": tile_w(inp["kv_ada_w"]), "kv_w": tile_w(inp["kv_w"]),
        "sgu_w": np.ascontiguousarray(inp["a_sgu_w"][0]),
        "rows": np.ascontiguousarray(np.stack([inp["a_in_b"][0][D:], inp["a_sgu_ln_b"][0], inp["a_sgu_b"][0].reshape(-1)]).astype(np.float32)),
        "b_ada_w": tile_w(inp["b_ada_w"][0]), "b_q_w": tile_w(inp["b_q_w"][0]), "b_o_w": tile_w(inp["b_o_w"][0]),
        "b_ffn_wi": tile_w(inp["b_ffn_wi"][0]), "b_ffn_wo": tile_wo(inp["b_ffn_wo"][0]),
    }
    in_maps = []
    for i in range(8):
        b, j = i // 4, i % 4
        idx = tok_index(j)
        m = dict(shared)
        m["xT"] = np.ascontiguousarray(inp["x"][b][idx].T)
        m["pos"] = np.ascontiguousarray(inp["positions"][b][idx][None, :]).astype(np.int32)
        m["cols"] = host_cols(inp, b)
        m["mask"] = host_mask(j)
        in_maps.append(m)
    in_maps = in_maps[:ncores]
    res = run_bass_kernel_spmd(nc, in_maps, core_ids=list(range(ncores)))
    return res.results


def kernel(**inp):
    inp = {k: np.asarray(v) for k, v in inp.items()}
    rf = run_f(inp)
    out = np.zeros((2, 4096, D), np.float32)
    for i in range(8):
        b, j = i // 4, i % 4
        out[b, tok_index(j), :] = rf[i]["outT"].T
    return out
```

```python
import math
import numpy as np
import ml_dtypes
import concourse.bass as bass
import concourse.mybir as mybir
from concourse.bass_utils import run_bass_kernel_spmd
from contextlib import ExitStack

F32 = mybir.dt.float32
BF16 = mybir.dt.bfloat16
I32 = mybir.dt.int32
ALU = mybir.AluOpType
AF = mybir.ActivationFunctionType

D = 2048
NCH = 16
T = 1024
DFF = 5632
NJ = 44
EPS = 1e-6
NSLOT = 5
ENGS = ("pe", "act", "dve", "pool", "sp")

COLS = {}
_o = 0
for _n, _w in [("a_norm1_g", 16), ("a_norm2_g", 16), ("a_in_b_u", 16), ("a_ln_g", 16), ("a_ln_b", 16), ("kv_norm_g", 16),
               ("b_norm1_g", 16), ("b_norm2_g", 16), ("a_ada_b", 96), ("kv_ada_b", 32), ("b_ada_b", 96),
               ("k_norm_g", 1), ("q_norm_g", 1), ("subln_g", 2), ("lam", 4), ("invf", 1), ("cT", 16)]:
    COLS[_n] = (_o, _w)
    _o += _w
NCOL = _o


class Buf:
    __slots__ = ("name", "w", "r")

    def __init__(self, name=""):
        self.name = name
        self.w = None
        self.r = {}


class Prog:
    def __init__(self, nc, es):
        self.nc = nc
        self.es = es
        self.q = {e: [] for e in ENGS}
        self.sem = {}
        self.cnt = {}
        self.seen = {e: {} for e in ENGS}
        for e in ENGS:
            self.new_sem("E_" + e)
        self.n_dma_sem = 0

    def new_sem(self, name):
        self.sem[name] = self.es.enter_context(self.nc.semaphore(name))
        self.cnt[name] = 0
        return name

    def dma_sem(self):
        self.n_dma_sem += 1
        return self.new_sem("D%d" % self.n_dma_sem)

    def op(self, eng, fn, reads=(), writes=(), dma=None, extra_waits=(), dma_inc=16):
        waits = {}

        def need(tok):
            if tok is None:
                return
            s, v = tok
            if waits.get(s, 0) < v:
                waits[s] = v

        for b in reads:
            need(b.w)
        for b in writes:
            need(b.w)
            for s, v in b.r.items():
                need((s, v))
        for t in extra_waits:
            need(t)
        if eng == "pe":
            waits.pop("E_pe", None)
        wl = []
        seen = self.seen[eng]
        for s, v in waits.items():
            if seen.get(s, 0) < v:
                seen[s] = v
                wl.append((s, v))
        if dma is not None:
            s = dma
            self.cnt[s] += dma_inc
            inc = (s, dma_inc)
        else:
            s = "E_" + eng
            self.cnt[s] += 1
            inc = (s, 1)
        tok = (s, self.cnt[s])
        self.q[eng].append((wl, fn, inc))
        for b in reads:
            if b.r.get(s, 0) < tok[1]:
                b.r[s] = tok[1]
        for b in writes:
            b.w = tok
            b.r = {}
        return tok

    def wait_only(self, eng, toks):
        wl = []
        seen = self.seen[eng]
        for s, v in toks:
            if seen.get(s, 0) < v:
                seen[s] = v
                wl.append((s, v))
        if wl:
            self.q[eng].append((wl, None, None))

    def barrier(self):
        toks = [(s, c) for s, c in self.cnt.items() if c > 0]
        for e in ENGS:
            self.wait_only(e, toks)

    def chain(self, eng, fns, reads=(), writes=()):
        for fn in fns:
            tok = self.op(eng, fn, reads=reads, writes=writes)
        return tok

    def emit(self):
        nc = self.nc
        block = self.es.enter_context(nc.Block())
        sem = self.sem

        def replay(engobj, items):
            for wl, fn, inc in items:
                for s, v in wl:
                    engobj.wait_ge(sem[s], v)
                if fn is not None:
                    ins = fn(engobj)
                    ins.then_inc(sem[inc[0]], inc[1])

        q = self.q

        @block.tensor
        def _(e):
            replay(e, q["pe"])

        @block.scalar
        def _(e):
            replay(e, q["act"])

        @block.vector
        def _(e):
            replay(e, q["dve"])

        @block.gpsimd
        def _(e):
            replay(e, q["pool"])

        @block.sync
        def _(e):
            replay(e, q["sp"])


class WStream:
    def __init__(self, P, nc, es, nslot):
        self.P = P
        self.nslot = nslot
        self.slots = [es.enter_context(nc.sbuf_tensor("s_wslot%d" % i, [128, 4096], BF16)) for i in range(nslot)]
        self.bufs = [Buf("wslot%d" % i) for i in range(nslot)]
        self.sems = [P.dma_sem() for _ in range(nslot)]
        self.plan = []
        self.issued = 0
        self.taken = 0
        self.closed = 0

    def add(self, key, src, shape):
        self.plan.append((key, src, shape))

    def pump(self):
        while self.issued < len(self.plan) and self.issued - self.nslot < self.closed:
            i = self.issued
            key, src, shape = self.plan[i]
            s = i % self.nslot
            dst = self.view(s, shape)
            self.P.op("pool", (lambda dst, src: (lambda e: e.dma_start(out=dst, in_=src)))(dst, src),
                      writes=[self.bufs[s]], dma=self.sems[s])
            self.issued += 1

    def view(self, s, shape):
        if shape == "k":
            return self.slots[s][:].rearrange("p (k f) -> p k f", k=16)
        else:
            return self.slots[s][:].rearrange("p (j f) -> p j f", j=2)

    def take(self, key):
        i = self.taken
        assert self.plan[i][0] == key, (self.plan[i][0], key)
        assert i - self.closed < self.nslot, "too many open weight units"
        self.pump()
        assert self.issued > i
        self.taken += 1
        s = i % self.nslot
        return self.view(s, self.plan[i][2]), self.bufs[s]

    def release(self):
        self.closed = self.taken
        self.pump()


def tile_w(W):
    K, Fd = W.shape
    return np.ascontiguousarray(W.reshape(K // 128, 128, Fd // 256, 256).transpose(2, 1, 0, 3))


def tile_wo(W):
    K, Fd = W.shape
    return np.ascontiguousarray(W.reshape(K // 256, 2, 128, Fd).transpose(0, 2, 1, 3))


def colv(v):
    v = np.asarray(v, np.float32).reshape(-1, 128)
    return np.ascontiguousarray(v.T)


def tok_index(j):
    idx = []
    for s in range(8):
        p = 8 * (s // 2) + (j if s % 2 == 0 else 7 - j)
        idx.append(np.arange(p * 128, (p + 1) * 128))
    return np.concatenate(idx)


LAM_INIT = 0.8 - 0.6 * math.exp(-0.3 * 1)


import os
STOP = int(os.environ.get("KSTOP", "99"))


class _Stop(Exception):
    pass


def stage(n):
    if STOP <= n:
        raise _Stop()


class Builder:
    def __init__(self, mode):
        self.mode = mode
        self.nc = bass.Bass("TRN2", target_bir_lowering=False)

    def sbt(self, name, shape, dt):
        self._uid = getattr(self, "_uid", 0) + 1
        return self.nc.sbuf_tensor("s_%s_%d" % (name, self._uid), list(shape), dt)

    def dram_in(self, name, shape, dt=F32):
        return self.nc.dram_tensor(name, list(shape), dt, kind="ExternalInput").ap()

    def dram_out(self, name, shape, dt=F32):
        return self.nc.dram_tensor(name, list(shape), dt, kind="ExternalOutput").ap()

    def build(self):
        nc = self.nc
        mode = self.mode
        with ExitStack() as es:
            self.es = es
            P = self.P = Prog(nc, es)
            sb = lambda n, s, d: es.enter_context(self.sbt("" + n, list(s), d))
            self.sb = sb
            xT_d = self.dram_in("xT", [D, T])
            cols_d = self.dram_in("cols", [128, NCOL])
            pos_d = self.dram_in("pos", [1, T], I32)
            out_d = self.dram_out("outT", [D, T])
            self.DBG = bool(int(os.environ.get("KDEBUG", "0")))
            if self.DBG:
                self.dbgK = self.dram_out("dbgK", [128, 4096], BF16)
                self.dbgV = self.dram_out("dbgV", [128, 4096], BF16)
                self.dbgA = self.dram_out("dbgA", [3, 1024, 1024], BF16)
                self.dbgL = self.dram_out("dbgL", [256, 1024], BF16)
            W = {}
            if mode in ("A", "F"):
                rows_d = self.dram_in("rows", [3, D])
                sguw_d = self.dram_in("sgu_w", [16, 128, 128])
                W["a_ada"] = self.dram_in("a_ada_w", [48, 128, 16, 256])
                W["a_in"] = self.dram_in("a_in_w", [16, 128, 16, 256])
                W["a_out"] = self.dram_in("a_out_w", [8, 128, 16, 256])
                W["a_wi"] = self.dram_in("a_ffn_wi", [44, 128, 16, 256])
                W["a_wo"] = self.dram_in("a_ffn_wo", [22, 128, 2, 2048])
                W["kv_ada"] = self.dram_in("kv_ada_w", [16, 128, 16, 256])
                W["kv"] = self.dram_in("kv_w", [16, 128, 16, 256])
            if mode == "A":
                KT_d = self.dram_out("KT", [16, 128, T], BF16)
                V_d = self.dram_out("V", [8, T, 256], BF16)
            if mode == "F":
                self.kt_loc = [nc.dram_tensor("kt_loc%d" % i, [256, T], BF16, kind="Internal").ap() for i in range(8)]
                self.kt_all = [nc.dram_tensor("kt_all%d" % i, [4 * 256, T], BF16, kind="Internal").ap() for i in range(8)]
                self.v_loc = [nc.dram_tensor("v_loc%d" % i, [256, T], BF16, kind="Internal").ap() for i in range(8)]
                self.v_all = [nc.dram_tensor("v_all%d" % i, [4 * 256, T], BF16, kind="Internal").ap() for i in range(8)]
                self.dm_loc = nc.dram_tensor("dm_loc", [128, 128], BF16, kind="Internal").ap()
                self.dm_all = nc.dram_tensor("dm_all", [4 * 128, 128], BF16, kind="Internal").ap()
                KT_d = [self.kt_loc[hh // 2][(hh % 2) * 128:(hh % 2) * 128 + 128, :] for hh in range(16)]
                V_d = [self.v_loc[u].rearrange("r (q d) -> (r q) d", d=256) for u in range(8)]
            if mode in ("B", "F"):
                W["b_ada"] = self.dram_in("b_ada_w", [48, 128, 16, 256])
                W["b_q"] = self.dram_in("b_q_w", [8, 128, 16, 256])
                W["b_o"] = self.dram_in("b_o_w", [8, 128, 16, 256])
                W["b_wi"] = self.dram_in("b_ffn_wi", [44, 128, 16, 256])
                W["b_wo"] = self.dram_in("b_ffn_wo", [22, 128, 2, 2048])
                mask_d = self.dram_in("mask", [128, 8, 128])
            if mode == "B":
                KTf_d = self.dram_in("KTf", [16, 128, 4096], BF16)
                Vf_d = self.dram_in("Vf", [8, 2, 128, 16, 256], BF16)
            if mode == "F":
                KTf_d = [self.kt_all[hh // 2].rearrange("(j s d) t -> s d j t", j=4, s=2, d=128)[hh % 2] for hh in range(16)]
                Vf_d = [[self.v_all[u].rearrange("(j r) (q d) -> j (r q) d", j=4, d=256)[j].rearrange("(s p) d -> p s d", p=128)
                         for j in range(4)] for u in range(8)]
            self.W = W

            self.xT = sb("xT", [128, NCH, T], F32)
            self.hT = sb("hT", [128, NCH, T], BF16)
            self.cols = sb("cols", [128, NCOL], F32)
            self.rstd = sb("rstd", [128, T], F32)
            self.ones = sb("ones", [128, 128], BF16)
            self.ones_f = sb("ones_f", [128, 128], F32)
            self.sc_bf = sb("sc_bf", [128, NCH], BF16)
            self.adaA = sb("adaA", [128, 96], F32)
            self.adaK = sb("adaK", [128, 32], F32)
            self.mods = sb("mods", [128, 4, 16], F32)
            self.tmpf = [sb("tmpf%d" % i, [128, 512], F32) for i in range(3)]
            self.sqb = [sb("sqb%d" % i, [128, 2, 512], BF16) for i in range(2)]
            self.eps_col = sb("eps_col", [128, 4], F32)
            B = self.B = {}
            for n in ["xT0", "xT1", "hT0", "hT1", "cols", "rstd0", "rstd1", "ones", "sc_bf", "adaA", "adaK", "mods",
                      "tmpf0", "tmpf1", "tmpf2", "sqb0", "sqb1", "out", "consts"]:
                B[n] = Buf(n)
            self.ps = [es.enter_context(nc.psum_tensor("ps%d" % i, [128, 512], F32)) for i in range(7)]
            self.psT = es.enter_context(nc.psum_tensor("psT", [128, 1024], BF16))
            self.Bps = [Buf("ps%d" % i) for i in range(7)]
            self.BpsT = Buf("psT")
            self.ws = WStream(P, nc, es, NSLOT)
            self.dl = P.dma_sem()
            self.dx = P.dma_sem()
            self.do = P.dma_sem()
            xT, cols, ones, ones_f, sc_bf = self.xT, self.cols, self.ones, self.ones_f, self.sc_bf

            ws = self.ws
            def padd(wname, idx):
                ws.add((wname, idx), W[wname][idx], "k")
            if mode in ("A", "F"):
                self.plan_ada("a_ada", 0)
                self.plan_ada("a_ada", 1)
                for cb in range(8):
                    padd("a_in", 8 + cb)
                    padd("a_ada", 16 + 2 * cb)
                    padd("a_ada", 17 + 2 * cb)
                for u in range(8):
                    padd("a_in", u)
                    padd("a_ada", 32 + u)
                for u in range(8):
                    padd("a_out", u)
                    padd("a_ada", 40 + u)
                self.plan_ffn("a", "kv_ada")
                for u in range(16):
                    padd("kv", u)
                    if mode == "F":
                        padd("b_ada", u)
            if mode in ("B", "F"):
                if mode == "B":
                    self.plan_ada("b_ada", 0)
                    self.plan_ada("b_ada", 1)
                for u in range(8):
                    ws.add(("b_q", u), W["b_q"][u], "k")
                for g in (2, 3, 4, 5):
                    self.plan_ada("b_ada", g)
                for u in range(8):
                    ws.add(("b_o", u), W["b_o"][u], "k")
                self.plan_ffn("b")

            P.op("sp", lambda e: e.dma_start(out=cols[:], in_=cols_d), writes=[B["cols"]], dma=P.dma_sem())
            for th in range(2):
                for cq in range(4):
                    src = xT_d.rearrange("(c p) t -> p c t", p=128)[:, 4 * cq:4 * cq + 4, th * 512:(th + 1) * 512]
                    dst = xT[:, 4 * cq:4 * cq + 4, th * 512:(th + 1) * 512]
                    P.op("sp", (lambda dst, src: (lambda e: e.dma_start(out=dst, in_=src)))(dst, src), dma=self.dx)
            B["xT0"].w = (self.dx, P.cnt[self.dx])
            B["xT1"].w = (self.dx, P.cnt[self.dx])
            P.op("dve", lambda e: e.memset(ones[:], 1.0), writes=[B["ones"]])
            P.op("dve", lambda e: e.memset(ones_f[:], 1.0), writes=[B["ones"]])
            P.op("dve", lambda e: e.memset(self.eps_col[:, 0:1], EPS), writes=[B["consts"]])
            P.op("dve", lambda e: e.memset(self.eps_col[:, 1:2], -3.1415920), writes=[B["consts"]])
            c0 = COLS["cT"][0]
            P.op("act", lambda e: e.activation(out=sc_bf[:], in_=cols[:, c0:c0 + 16], func=AF.Silu),
                 reads=[B["cols"]], writes=[B["sc_bf"]])

            self.Bktall = [Buf("ktall%d" % i) for i in range(8)]
            self.Bvall = [Buf("vall%d" % i) for i in range(8)]
            self.Bdummy = Buf("dummy")
            try:
                if mode in ("A", "F"):
                    self.layer_a(rows_d, sguw_d, pos_d, KT_d, V_d)
                if mode == "F" and self.DBG:
                    P.op("sp", lambda e: e.dma_start(out=self.dbgA[0], in_=self.kt_all[0]), reads=[self.Bktall[0]], dma=self.do)
                if mode in ("B", "F"):
                    self.layer_b(KTf_d, Vf_d, mask_d, pos_d)
                if mode == "F" and self.DBG:
                    P.op("sp", lambda e: e.dma_start(out=self.dbgA[2], in_=self.kt_all[0]), reads=[self.Bktall[0]], dma=self.do)
                    P.op("sp", lambda e: e.dma_start(out=self.dbgL, in_=self.kt_loc[0]), dma=self.do)
            except _Stop:
                ws.taken = len(ws.plan)

            for th in range(2):
                for cq in range(4):
                    dst = out_d.rearrange("(c p) t -> p c t", p=128)[:, 4 * cq:4 * cq + 4, th * 512:(th + 1) * 512]
                    src = xT[:, 4 * cq:4 * cq + 4, th * 512:(th + 1) * 512]
                    P.op("sp", (lambda dst, src: (lambda e: e.dma_start(out=dst, in_=src)))(dst, src),
                         reads=[B["xT%d" % th]], writes=[B["out"]], dma=self.do)
            P.wait_only("sp", [(self.do, P.cnt[self.do])] + [(d_, P.cnt[d_]) for d_ in getattr(self, "kv_out_sems", [])])
            assert ws.taken == len(ws.plan), (ws.taken, len(ws.plan))
            P.emit()
        return nc

    def col(self, name, i=0, n=1):
        o, w = COLS[name]
        return self.cols[:, o + i:o + i + n]

    def plan_ada(self, wname, g):
        for u in range(8):
            self.ws.add((wname, g * 8 + u), self.W[wname][g * 8 + u], "k")

    def plan_ffn(self, L, ada=None):
        for grp in range(11):
            for qq in range(2):
                q = grp * 2 + qq
                self.ws.add((L + "_wi", q), self.W[L + "_wi"][q], "k")
                self.ws.add((L + "_wi", 22 + q), self.W[L + "_wi"][22 + q], "k")
            for qq in range(2):
                q = grp * 2 + qq
                self.ws.add((L + "_wo", q), self.W[L + "_wo"][q], "j")
            if ada is not None and grp < 8:
                self.ws.add((ada, 2 * grp), self.W[ada][2 * grp], "k")
                self.ws.add((ada, 2 * grp + 1), self.W[ada][2 * grp + 1], "k")

    def ada_unit(self, wname, idx, dst, dbuf, bias_name):
        P, B = self.P, self.B
        bank = 6
        ps = self.ps[bank]
        g, u = idx // 8, idx % 8
        wv, wb = self.ws.take((wname, idx))

        def fn(e, wv=wv, u=u):
            for fc in range(2):
                c = u * 2 + fc
                for k in range(NCH):
                    ins = e.matmul(ps[:, c:c + 1], lhsT=wv[:, k, fc * 128:(fc + 1) * 128],
                                   rhs=self.sc_bf[:, k:k + 1], start=(k == 0), stop=(k == NCH - 1))
            return ins
        P.op("pe", fn, reads=[wb, B["sc_bf"]], writes=[self.Bps[bank]])
        self.ws.release()
        if u == 7:
            bo = COLS[bias_name][0] + g * 16
            P.op("dve", lambda e: e.tensor_tensor(out=dst[:, g * 16:(g + 1) * 16], in0=ps[:, 0:16],
                                                  in1=self.cols[:, bo:bo + 16], op=ALU.add),
                 reads=[self.Bps[bank], B["cols"]], writes=[dbuf])

    def ada_group(self, wname, g, dst, dbuf, bias_name):
        for u in range(8):
            self.ada_unit(wname, g * 8 + u, dst, dbuf, bias_name)

    def mk_gsc(self, slot, gname, sc_ap, src_buf):
        P, B = self.P, self.B
        o = COLS[gname][0]
        P.op("dve", lambda e: e.scalar_tensor_tensor(out=self.mods[:, slot, :], in0=sc_ap, scalar=1.0,
                                                     in1=self.cols[:, o:o + 16], op0=ALU.add, op1=ALU.mult),
             reads=[src_buf, B["cols"]], writes=[B["mods"]])

    def norm_stats(self):
        P, B = self.P, self.B
        xT = self.xT
        for th in range(2):
            bank = th
            for cq in range(8):
                sq = self.sqb[cq % 2]
                sqB = B["sqb%d" % (cq % 2)]
                P.op("act", lambda e, sq=sq, cq=cq, th=th: e.activation(
                    out=sq[:], in_=xT[:, 2 * cq:2 * cq + 2, th * 512:(th + 1) * 512], func=AF.Square),
                    reads=[B["xT%d" % th]], writes=[sqB])

                def fn(e, sq=sq, cq=cq, bank=bank):
                    for c in range(2):
                        ins = e.matmul(self.ps[bank][:], lhsT=self.ones[:], rhs=sq[:, c, :],
                                       start=(cq == 0 and c == 0), stop=(cq == 7 and c == 1))
                    return ins
                P.op("pe", fn, reads=[sqB, B["ones"]], writes=[self.Bps[bank]])
            tf = self.tmpf[2]
            P.op("act", lambda e, bank=bank, tf=tf: e.activation(out=tf[:], in_=self.ps[bank][:], func=AF.Sqrt,
                                                              bias=self.eps_col[:, 0:1], scale=1.0 / D),
                 reads=[self.Bps[bank], B["consts"]], writes=[B["tmpf2"]])
            P.op("dve", lambda e, th=th, tf=tf: e.reciprocal(out=self.rstd[:, th * 512:(th + 1) * 512], in_=tf[:]),
                 reads=[B["tmpf2"]], writes=[B["rstd%d" % th]])

    def modulate(self, gsc_slot, sh_ap_fn, sh_buf):
        P, B = self.P, self.B
        for th in range(2):
            for c in range(NCH):
                i = c % 2
                tf = self.tmpf[i]
                P.op("dve", lambda e, tf=tf, c=c, th=th: e.scalar_tensor_tensor(
                    out=tf[:], in0=self.xT[:, c, th * 512:(th + 1) * 512], scalar=self.mods[:, gsc_slot, c:c + 1],
                    in1=self.rstd[:, th * 512:(th + 1) * 512], op0=ALU.mult, op1=ALU.mult),
                    reads=[B["xT%d" % th], B["mods"], B["rstd%d" % th]], writes=[B["tmpf%d" % i]])
                P.op("act", lambda e, tf=tf, c=c, th=th: e.activation(
                    out=self.hT[:, c, th * 512:(th + 1) * 512], in_=tf[:], func=AF.Identity,
                    bias=sh_ap_fn(c), scale=1.0),
                    reads=[B["tmpf%d" % i], sh_buf], writes=[B["hT%d" % th]])

    def proj_residual(self, wname, rhs_fn, rhs_bufs, gate_ap_fn, gate_buf, after_unit=None):
        P, B = self.P, self.B
        n = 0
        for u in range(8):
            wv, wb = self.ws.take((wname, u))
            for fc in range(2):
                m = u * 2 + fc
                for th in range(2):
                    bank = n % 4
                    n += 1

                    def fn(e, wv=wv, fc=fc, th=th, bank=bank):
                        for k in range(NCH):
                            ins = e.matmul(self.ps[bank][:], lhsT=wv[:, k, fc * 128:(fc + 1) * 128],
                                           rhs=rhs_fn(k, th), start=(k == 0), stop=(k == NCH - 1))
                        return ins
                    P.op("pe", fn, reads=[wb] + rhs_bufs(th), writes=[self.Bps[bank]])
                    P.op("dve", lambda e, m=m, th=th, bank=bank: e.scalar_tensor_tensor(
                        out=self.xT[:, m, th * 512:(th + 1) * 512], in0=self.ps[bank][:], scalar=gate_ap_fn(m),
                        in1=self.xT[:, m, th * 512:(th + 1) * 512], op0=ALU.mult, op1=ALU.add),
                        reads=[self.Bps[bank], gate_buf, B["xT%d" % th]], writes=[B["xT%d" % th]])
            self.ws.release()
            if after_unit is not None:
                after_unit(u)

    def ffn(self, L, gate_ap_fn, gate_buf, between=None):
        P, B, nc = self.P, self.B, self.nc
        with ExitStack() as ph:
            aT = [ph.enter_context(self.sbt("aT%d" % i, [128, 4, T], BF16)) for i in range(2)]
            BaT = [[Buf("aT%d_%d" % (i, th)) for th in range(2)] for i in range(2)]
            sg = [ph.enter_context(self.sbt("sg%d" % i, [128, 512], F32)) for i in range(2)]
            Bsg = [Buf("sg0"), Buf("sg1")]
            ny = 0
            nsg = 0
            for grp in range(11):
                ab = grp % 2
                for qq in range(2):
                    q = grp * 2 + qq
                    wg, wgb = self.ws.take((L + "_wi", q))
                    wu, wub = self.ws.take((L + "_wi", 22 + q))
                    for fc in range(2):
                        jj = qq * 2 + fc
                        for th in range(2):
                            bg = th
                            bu = 2 + th

                            def fg(e, wg=wg, fc=fc, th=th, bg=bg):
                                for k in range(NCH):
                                    ins = e.matmul(self.ps[bg][:], lhsT=wg[:, k, fc * 128:(fc + 1) * 128],
                                                   rhs=self.hT[:, k, th * 512:(th + 1) * 512], start=(k == 0), stop=(k == NCH - 1))
                                return ins
                            P.op("pe", fg, reads=[wgb, B["hT%d" % th]], writes=[self.Bps[bg]])

                            def fu(e, wu=wu, fc=fc, th=th, bu=bu):
                                for k in range(NCH):
                                    ins = e.matmul(self.ps[bu][:], lhsT=wu[:, k, fc * 128:(fc + 1) * 128],
                                                   rhs=self.hT[:, k, th * 512:(th + 1) * 512], start=(k == 0), stop=(k == NCH - 1))
                                return ins
                            P.op("pe", fu, reads=[wub, B["hT%d" % th]], writes=[self.Bps[bu]])
                            si = nsg % 2
                            nsg += 1
                            P.op("act", lambda e, si=si, bg=bg: e.activation(out=sg[si][:], in_=self.ps[bg][:], func=AF.Silu),
                                 reads=[self.Bps[bg]], writes=[Bsg[si]])
                            P.op("dve", lambda e, si=si, bu=bu, ab=ab, jj=jj, th=th: e.tensor_tensor(
                                out=aT[ab][:, jj, th * 512:(th + 1) * 512], in0=self.ps[bu][:], in1=sg[si][:], op=ALU.mult),
                                reads=[self.Bps[bu], Bsg[si]], writes=[BaT[ab][th]])
                    self.ws.release()
                wo0, wob0 = self.ws.take((L + "_wo", grp * 2))
                wo1, wob1 = self.ws.take((L + "_wo", grp * 2 + 1))
                wos = (wo0, wo1)
                for m in range(NCH):
                    for th in range(2):
                        bank = 4 + ny % 2
                        ny += 1

                        def fy(e, m=m, th=th, bank=bank, wos=wos, ab=ab):
                            for jj in range(4):
                                ins = e.matmul(self.ps[bank][:], lhsT=wos[jj // 2][:, jj % 2, m * 128:(m + 1) * 128],
                                               rhs=aT[ab][:, jj, th * 512:(th + 1) * 512], start=(jj == 0), stop=(jj == 3))
                            return ins
                        P.op("pe", fy, reads=[wob0, wob1, BaT[ab][th]], writes=[self.Bps[bank]])
                        P.op("dve", lambda e, m=m, th=th, bank=bank: e.scalar_tensor_tensor(
                            out=self.xT[:, m, th * 512:(th + 1) * 512], in0=self.ps[bank][:], scalar=gate_ap_fn(m),
                            in1=self.xT[:, m, th * 512:(th + 1) * 512], op0=ALU.mult, op1=ALU.add),
                            reads=[self.Bps[bank], gate_buf, B["xT%d" % th]], writes=[B["xT%d" % th]])
                self.ws.release()
                if between is not None:
                    between(grp)
        P.barrier()

    def rope_tables(self, pos_d):
        P, B, nc = self.P, self.B, self.nc
        B["rope"] = Buf("rope")
        with ExitStack() as ph:
            posi = ph.enter_context(self.sbt("posi", [128, T], I32))
            ua = ph.enter_context(self.sbt("rope_u", [128, T], F32))
            ub = ph.enter_context(self.sbt("rope_k", [128, T], F32))
            Bp, Ba, Bb = Buf("posi"), Buf("ua"), Buf("ub")
            io = COLS["invf"][0]
            for off, dst in ((0.5, self.sinS), (0.75, self.cosF)):
                P.op("sp", lambda e: e.dma_start(out=posi[:], in_=pos_d.broadcast_to([128, T])), writes=[Bp], dma=P.dma_sem())
                P.op("dve", lambda e: e.tensor_copy(out=ua[:], in_=posi[:]), reads=[Bp], writes=[Ba])
                P.op("dve", lambda e, off=off: e.tensor_scalar(out=ua[:], in0=ua[:], scalar1=self.cols[:, io:io + 1], scalar2=off,
                                                               op0=ALU.mult, op1=ALU.add),
                     reads=[Ba, B["cols"]], writes=[Ba])
                P.op("dve", lambda e: e.tensor_copy(out=posi[:], in_=ua[:]), reads=[Ba], writes=[Bp])
                P.op("dve", lambda e: e.tensor_copy(out=ub[:], in_=posi[:]), reads=[Bp], writes=[Bb])
                P.op("dve", lambda e: e.tensor_tensor(out=ua[:], in0=ua[:], in1=ub[:], op=ALU.subtract),
                     reads=[Ba, Bb], writes=[Ba])
                P.op("dve", lambda e: e.tensor_single_scalar(out=ub[:], in_=ua[:], scalar=0.0, op=ALU.is_lt),
                     reads=[Ba], writes=[Bb])
                P.op("dve", lambda e: e.tensor_tensor(out=ua[:], in0=ua[:], in1=ub[:], op=ALU.add),
                     reads=[Ba, Bb], writes=[Ba])
                P.op("act", lambda e, dst=dst: e.activation(out=dst[:], in_=ua[:], func=AF.Sin, bias=self.eps_col[:, 1:2],
                                                            scale=6.2831845),
                     reads=[Ba, B["consts"]], writes=[B["rope"]])
            sinS_ = self.sinS
            P.op("dve", lambda e: e.tensor_scalar(out=sinS_[64:128, :], in0=sinS_[64:128, :], scalar1=-1.0, scalar2=None,
                                                  op0=ALU.mult),
                 reads=[B["rope"]], writes=[B["rope"]])
        P.barrier()

    def qk_head(self, ps_bank, th, gcol_ap, dst_ap, dst_buf):
        P, B = self.P, self.B
        ps = self.ps[ps_bank]
        bss = 2 + th
        sq = self.sqb[0]
        P.op("act", lambda e: e.activation(out=sq[:, 0, :], in_=ps[:], func=AF.Square),
             reads=[self.Bps[ps_bank]], writes=[B["sqb0"]])
        P.op("pe", lambda e: e.matmul(self.ps[bss][:], lhsT=self.ones[:], rhs=sq[:, 0, :], start=True, stop=True),
             reads=[B["sqb0"], B["ones"]], writes=[self.Bps[bss]])
        t0, t1, t2 = self.tmpf
        P.op("act", lambda e: e.activation(out=t2[:], in_=self.ps[bss][:], func=AF.Sqrt, bias=self.eps_col[:, 0:1],
                                           scale=1.0 / 128),
             reads=[self.Bps[bss], B["consts"]], writes=[B["tmpf2"]])
        P.op("dve", lambda e: e.reciprocal(out=t2[:], in_=t2[:]), reads=[B["tmpf2"]], writes=[B["tmpf2"]])
        qn = self.qn
        P.op("dve", lambda e: e.scalar_tensor_tensor(out=qn[:], in0=ps[:], scalar=gcol_ap, in1=t2[:],
                                                     op0=ALU.mult, op1=ALU.mult),
             reads=[self.Bps[ps_bank], B["tmpf2"], B["cols"]], writes=[B["qn"]])
        sl = slice(th * 512, (th + 1) * 512)
        cosF, sinS = self.cosF, self.sinS
        P.op("dve", lambda e: e.tensor_tensor(out=t0[:], in0=qn[:], in1=cosF[:, sl], op=ALU.mult),
             reads=[B["qn"], B["rope"]], writes=[B["tmpf0"]])
        P.op("dve", lambda e: e.tensor_tensor(out=t1[0:64, :], in0=qn[64:128, :], in1=sinS[64:128, sl], op=ALU.mult),
             reads=[B["qn"], B["rope"]], writes=[B["tmpf1"]])
        P.op("dve", lambda e: e.tensor_tensor(out=t1[64:128, :], in0=qn[0:64, :], in1=sinS[0:64, sl], op=ALU.mult),
             reads=[B["qn"], B["rope"]], writes=[B["tmpf1"]])
        P.op("dve", lambda e: e.tensor_tensor(out=dst_ap, in0=t0[:], in1=t1[:], op=ALU.add),
             reads=[B["tmpf0"], B["tmpf1"]], writes=[dst_buf])

    def layer_a(self, rows_d, sguw_d, pos_d, KT_d, V_d):
        P, B, nc, sb = self.P, self.B, self.nc, self.sb
        adaA, adaK = self.adaA, self.adaK
        stage(0)
        phA = ExitStack()
        wmT = phA.enter_context(self.sbt("wmT", [128, 16, 128], BF16))
        B2 = phA.enter_context(self.sbt("B2", [128, 16, 128], F32))
        inbv = phA.enter_context(self.sbt("inbv", [1, D], BF16))
        for n in ["wmT", "B2", "inbv"]:
            B[n] = Buf(n)
        P.op("pool", lambda e: e.dma_start(out=inbv[:], in_=rows_d[0:1, :]), writes=[B["inbv"]], dma=P.dma_sem())
        with ExitStack() as ph:
            sguw = ph.enter_context(self.sbt("sguw", [128, 16, 128], F32))
            sguwb = ph.enter_context(self.sbt("sguwb", [128, 16, 128], BF16))
            sgub = ph.enter_context(self.sbt("sgub", [128, 16, 128], F32))
            identb = ph.enter_context(self.sbt("identb0", [128, 128], BF16))
            tri = ph.enter_context(self.sbt("tri", [128, 128], BF16))
            Bsw, Bswb, Bsb, Bid, Btri = Buf("sguw"), Buf("sguwb"), Buf("sgub"), Buf("ident"), Buf("tri")
            P.op("sp", lambda e: e.dma_start(out=sguw[:], in_=sguw_d.rearrange("g t s -> t g s")), writes=[Bsw], dma=P.dma_sem())
            P.op("sp", lambda e: e.dma_start(out=sgub[:].rearrange("p g t -> p (g t)"), in_=rows_d[2:3, :].broadcast_to([128, D])),
                 writes=[Bsb], dma=P.dma_sem())
            P.op("pool", lambda e: e.memset(identb[:], 1.0), writes=[Bid])
            P.op("pool", lambda e: e.affine_select(out=identb[:], in_=identb[:], pattern=[[-1, 128]], compare_op=ALU.is_equal,
                                                   fill=0.0, base=0, channel_multiplier=1), reads=[Bid], writes=[Bid])
            P.op("pool", lambda e: e.memset(tri[:], 1.0), writes=[Btri])
            P.op("pool", lambda e: e.affine_select(out=tri[:], in_=tri[:], pattern=[[1, 128]], compare_op=ALU.is_ge,
                                                   fill=0.0, base=0, channel_multiplier=-1), reads=[Btri], writes=[Btri])
            P.op("dve", lambda e: e.tensor_copy(out=sguwb[:], in_=sguw[:]), reads=[Bsw], writes=[Bswb])
            for g4 in range(4):
                def ft(e, g4=g4):
                    for gi in range(4):
                        g = g4 * 4 + gi
                        ins = e.transpose(out=self.psT[:, gi * 128:(gi + 1) * 128], in_=sguwb[:, g, :], identity=identb[:])
                    return ins
                P.op("pe", ft, reads=[Bswb, Bid], writes=[self.BpsT])
                for gi in range(4):
                    g = g4 * 4 + gi
                    P.op("dve", lambda e, g=g, gi=gi: e.tensor_tensor(
                        out=wmT[:, g, :], in0=self.psT[:, gi * 128:(gi + 1) * 128], in1=tri[:], op=ALU.mult),
                        reads=[self.BpsT, Btri], writes=[B["wmT"]])
            lb = COLS["a_ln_b"][0]
            for q in range(4):
                bank = q % 2
                P.op("pe", lambda e, q=q, bank=bank: e.matmul(
                    self.ps[bank][:], lhsT=self.ones[:], rhs=wmT[:, 4 * q:4 * q + 4, :], start=True, stop=True),
                    reads=[B["wmT"], B["ones"]], writes=[self.Bps[bank]])
                for gi in range(4):
                    g = 4 * q + gi
                    P.op("dve", lambda e, g=g, gi=gi, bank=bank: e.scalar_tensor_tensor(
                        out=B2[:, g, :], in0=self.ps[bank][:, gi * 128:(gi + 1) * 128], scalar=self.cols[:, lb + g:lb + g + 1],
                        in1=sgub[:, g, :], op0=ALU.mult, op1=ALU.add),
                        reads=[self.Bps[bank], Bsb, B["cols"]], writes=[B["B2"]])
        P.barrier()
        stage(1)

        self.norm_stats()
        stage(2)
        self.ada_group("a_ada", 0, adaA, B["adaA"], "a_ada_b")
        self.ada_group("a_ada", 1, adaA, B["adaA"], "a_ada_b")
        self.mk_gsc(0, "a_norm1_g", adaA[:, 16:32], B["adaA"])
        self.modulate(0, lambda c: adaA[:, c:c + 1], B["adaA"])
        stage(3)

        with ExitStack() as ph:
            VS = ph.enter_context(self.sbt("VS", [128, 8, 16, 128], BF16))
            BVS = [Buf("VS%d" % tb) for tb in range(8)]
            st1 = ph.enter_context(self.sbt("st1", [128, 8, 8], F32))
            st2 = ph.enter_context(self.sbt("st2", [128, 8, 8], F32))
            stt_ = ph.enter_context(self.sbt("stt", [128, 8, 4], F32))
            junk = ph.enter_context(self.sbt("junk", [128, 256], BF16))
            ug = [ph.enter_context(self.sbt("ug%d" % i, [128, 512], BF16)) for i in range(2)]
            Bst = [Buf("st%d" % tb) for tb in range(8)]
            Bjunk = Buf("junk")
            Bug = [Buf("ug0"), Buf("ug1")]
            P.op("dve", lambda e: e.memset(st1[:], 0.0), writes=Bst)
            P.op("dve", lambda e: e.memset(st2[:], 0.0), writes=Bst)
            nb = 0
            for cb in range(8):
                wv, wb = self.ws.take(("a_in", 8 + cb))
                for tb in range(8):
                    bank = nb % 4
                    nb += 1

                    def fv(e, wv=wv, tb=tb, bank=bank, cb=cb):
                        o = self.ps[bank][:, 0:256]
                        for k in range(NCH):
                            e.matmul(o, lhsT=self.hT[:, k, tb * 128:(tb + 1) * 128], rhs=wv[:, k, :], start=(k == 0), stop=False)
                        return e.matmul(o, lhsT=self.ones[0:1, :], rhs=inbv[0:1, cb * 256:(cb + 1) * 256], start=False, stop=True)
                    P.op("pe", fv, reads=[wb, B["hT%d" % (tb // 4)], B["inbv"], B["ones"]], writes=[self.Bps[bank]])
                    vdst = VS[:, tb, 2 * cb:2 * cb + 2, :]
                    P.op("act", lambda e, vdst=vdst, bank=bank, tb=tb, cb=cb: e.activation(
                        out=vdst, in_=self.ps[bank][:, 0:256].rearrange("p (a b) -> p a b", a=2), func=AF.Gelu,
                        accum_out=st1[:, tb, cb:cb + 1]),
                        reads=[self.Bps[bank]], writes=[BVS[tb], Bst[tb]])
                    P.op("act", lambda e, vdst=vdst, tb=tb, cb=cb: e.activation(
                        out=junk[:].rearrange("p (a b) -> p a b", a=2), in_=vdst, func=AF.Square,
                        accum_out=st2[:, tb, cb:cb + 1]),
                        reads=[BVS[tb]], writes=[Bjunk, Bst[tb]])
                self.ws.release()
                self.ada_unit("a_ada", 16 + 2 * cb, adaA, B["adaA"], "a_ada_b")
                self.ada_unit("a_ada", 17 + 2 * cb, adaA, B["adaA"], "a_ada_b")
            stage(4)
            X = mybir.AxisListType.X
            for tb in range(8):
                P.chain("dve", [
                    lambda e, tb=tb: e.tensor_reduce(out=stt_[:, tb, 0:1], in_=st1[:, tb, :], axis=X, op=ALU.add),
                    lambda e, tb=tb: e.tensor_reduce(out=stt_[:, tb, 1:2], in_=st2[:, tb, :], axis=X, op=ALU.add),
                    lambda e, tb=tb: e.tensor_scalar(out=stt_[:, tb, 0:1], in0=stt_[:, tb, 0:1], scalar1=1.0 / D, scalar2=None, op0=ALU.mult),
                    lambda e, tb=tb: e.tensor_tensor(out=stt_[:, tb, 2:3], in0=stt_[:, tb, 0:1], in1=stt_[:, tb, 0:1], op=ALU.mult),
                    lambda e, tb=tb: e.scalar_tensor_tensor(out=stt_[:, tb, 1:2], in0=stt_[:, tb, 1:2], scalar=1.0 / D,
                                                            in1=stt_[:, tb, 2:3], op0=ALU.mult, op1=ALU.subtract),
                ], reads=[Bst[tb]], writes=[Bst[tb]])
                P.op("act", lambda e, tb=tb: e.activation(out=stt_[:, tb, 2:3], in_=stt_[:, tb, 1:2], func=AF.Sqrt,
                                                          bias=self.eps_col[:, 0:1], scale=1.0),
                     reads=[Bst[tb], B["consts"]], writes=[Bst[tb]])
                P.chain("dve", [
                    lambda e, tb=tb: e.reciprocal(out=stt_[:, tb, 2:3], in_=stt_[:, tb, 2:3]),
                    lambda e, tb=tb: e.scalar_tensor_tensor(out=stt_[:, tb, 3:4], in0=stt_[:, tb, 0:1], scalar=-1.0,
                                                            in1=stt_[:, tb, 2:3], op0=ALU.mult, op1=ALU.mult),
                ], reads=[Bst[tb]], writes=[Bst[tb]])
                P.op("dve", lambda e, tb=tb: e.tensor_scalar(
                    out=VS[:, tb, :, :], in0=VS[:, tb, :, :], scalar1=stt_[:, tb, 2:3], scalar2=stt_[:, tb, 3:4],
                    op0=ALU.mult, op1=ALU.add),
                    reads=[BVS[tb], Bst[tb]], writes=[BVS[tb]])
            lg = COLS["a_ln_g"][0]
            ns = 0
            for tb in range(8):
                for g4 in range(4):
                    bank = 4 + ns % 2
                    ns += 1

                    def fsg(e, tb=tb, g4=g4, bank=bank):
                        for gi in range(4):
                            g = g4 * 4 + gi
                            ins = e.matmul(self.ps[bank][:, gi * 128:(gi + 1) * 128], lhsT=VS[:, tb, g, :], rhs=wmT[:, g, :],
                                           start=True, stop=True)
                        return ins
                    P.op("pe", fsg, reads=[BVS[tb], B["wmT"]], writes=[self.Bps[bank]])
                    for gi in range(4):
                        g = g4 * 4 + gi
                        P.op("dve", lambda e, tb=tb, g=g, gi=gi, bank=bank: e.scalar_tensor_tensor(
                            out=VS[:, tb, g, :], in0=self.ps[bank][:, gi * 128:(gi + 1) * 128],
                            scalar=self.cols[:, lg + g:lg + g + 1], in1=B2[:, g, :], op0=ALU.mult, op1=ALU.add),
                            reads=[self.Bps[bank], B["B2"], B["cols"]], writes=[BVS[tb]])
            stage(5)
            bo = COLS["a_in_b_u"][0]
            nu = 0
            for u in range(8):
                wv, wb = self.ws.take(("a_in", u))
                for fc in range(2):
                    g = u * 2 + fc
                    for th in range(2):
                        bank = nu % 4
                        i = nu % 2
                        nu += 1

                        def fu(e, wv=wv, fc=fc, th=th, bank=bank):
                            for k in range(NCH):
                                ins = e.matmul(self.ps[bank][:], lhsT=wv[:, k, fc * 128:(fc + 1) * 128],
                                               rhs=self.hT[:, k, th * 512:(th + 1) * 512], start=(k == 0), stop=(k == NCH - 1))
                            return ins
                        P.op("pe", fu, reads=[wb, B["hT%d" % th]], writes=[self.Bps[bank]])
                        P.op("act", lambda e, i=i, bank=bank, g=g: e.activation(
                            out=ug[i][:], in_=self.ps[bank][:], func=AF.Gelu, bias=self.cols[:, bo + g:bo + g + 1], scale=1.0),
                            reads=[self.Bps[bank], B["cols"]], writes=[Bug[i]])
                        P.op("dve", lambda e, i=i, g=g, th=th: e.tensor_tensor(
                            out=VS[:, 4 * th:4 * th + 4, g, :], in0=VS[:, 4 * th:4 * th + 4, g, :],
                            in1=ug[i][:].rearrange("p (a b) -> p a b", a=4), op=ALU.mult),
                            reads=[Bug[i]] + BVS[4 * th:4 * th + 4], writes=BVS[4 * th:4 * th + 4])
                self.ws.release()
                self.ada_unit("a_ada", 32 + u, adaA, B["adaA"], "a_ada_b")
            self.proj_residual("a_out", lambda k, th: VS[:, 4 * th:4 * th + 4, k, :],
                               lambda th: BVS[4 * th:4 * th + 4], lambda m: adaA[:, 32 + m:33 + m], B["adaA"],
                               after_unit=lambda u: self.ada_unit("a_ada", 40 + u, adaA, B["adaA"], "a_ada_b"))
        phA.close()
        P.barrier()
        stage(6)
        self.norm_stats()
        self.mk_gsc(1, "a_norm2_g", adaA[:, 64:80], B["adaA"])
        self.modulate(1, lambda c: adaA[:, 48 + c:49 + c], B["adaA"])
        self.ffn("a", lambda m: adaA[:, 80 + m:81 + m], B["adaA"],
                 between=lambda grp: (self.ada_unit("kv_ada", 2 * grp, adaK, B["adaK"], "kv_ada_b"),
                                      self.ada_unit("kv_ada", 2 * grp + 1, adaK, B["adaK"], "kv_ada_b")) if grp < 8 else None)
        stage(7)
        self.norm_stats()
        self.mk_gsc(2, "kv_norm_g", adaK[:, 16:32], B["adaK"])
        self.modulate(2, lambda c: adaK[:, c:c + 1], B["adaK"])
        with ExitStack() as ph:
            self.cosF = ph.enter_context(self.sbt("cosF", [128, T], F32))
            self.sinS = ph.enter_context(self.sbt("sinS", [128, T], F32))
            self.rope_tables(pos_d)
            self.qn = ph.enter_context(self.sbt("qn", [128, 512], F32))
            B["qn"] = Buf("qn")
            kt = [ph.enter_context(self.sbt("kt%d" % i, [128, 512], BF16)) for i in range(2)]
            Bkt = [Buf("kt0"), Buf("kt1")]
            vt = [ph.enter_context(self.sbt("vt%d" % i, [128, 256], BF16)) for i in range(2)]
            Bvt = [Buf("vt0"), Buf("vt1")]
            kg = self.col("k_norm_g")
            dks = [P.dma_sem(), P.dma_sem()]
            dvs = [P.dma_sem(), P.dma_sem()]
            self.kv_out_sems = dks + dvs
            nk = 0
            for u in range(8):
                wv, wb = self.ws.take(("kv", u))
                for fc in range(2):
                    hh = u * 2 + fc
                    for th in range(2):
                        bank = th
                        i = nk % 2
                        nk += 1

                        def fk(e, wv=wv, fc=fc, th=th, bank=bank):
                            for k in range(NCH):
                                ins = e.matmul(self.ps[bank][:], lhsT=wv[:, k, fc * 128:(fc + 1) * 128],
                                               rhs=self.hT[:, k, th * 512:(th + 1) * 512], start=(k == 0), stop=(k == NCH - 1))
                            return ins
                        P.op("pe", fk, reads=[wb, B["hT%d" % th]], writes=[self.Bps[bank]])
                        self.qk_head(bank, th, kg, kt[i][:], Bkt[i])
                        P.op("sp", lambda e, i=i, hh=hh, th=th: e.dma_start(out=KT_d[hh][:, th * 512:(th + 1) * 512], in_=kt[i][:]),
                             reads=[Bkt[i]], dma=dks[i])
                self.ws.release()
                if self.mode == "F":
                    self.ada_unit("b_ada", u, adaA, B["adaA"], "b_ada_b")
                if self.mode == "F":
                    P.op("pool", lambda e, u=u: e.collective_compute("AllGather", ALU.bypass, replica_groups=[[0, 1, 2, 3], [4, 5, 6, 7]],
                                                                     ins=[self.kt_loc[u]], outs=[self.kt_all[u]]),
                         extra_waits=[(d_, P.cnt[d_]) for d_ in dks], writes=[self.Bktall[u]], dma=P.dma_sem(), dma_inc=1)
            nv = 0
            for u in range(8):
                wv, wb = self.ws.take(("kv", 8 + u))
                for tb in range(8):
                    bank = 4 + nv % 2
                    i = nv % 2
                    nv += 1

                    def fvv(e, wv=wv, tb=tb, bank=bank):
                        for k in range(NCH):
                            ins = e.matmul(self.ps[bank][:, 0:256], lhsT=self.hT[:, k, tb * 128:(tb + 1) * 128], rhs=wv[:, k, :],
                                           start=(k == 0), stop=(k == NCH - 1))
                        return ins
                    P.op("pe", fvv, reads=[wb, B["hT%d" % (tb // 4)]], writes=[self.Bps[bank]])
                    P.op("act", lambda e, i=i, bank=bank: e.activation(out=vt[i][:], in_=self.ps[bank][:, 0:256], func=AF.Copy),
                         reads=[self.Bps[bank]], writes=[Bvt[i]])
                    P.op("sp", lambda e, i=i, u=u, tb=tb: e.dma_start(out=V_d[u][tb * 128:(tb + 1) * 128, :], in_=vt[i][:]),
                         reads=[Bvt[i]], dma=dvs[i])
                self.ws.release()
                if self.mode == "F":
                    self.ada_unit("b_ada", 8 + u, adaA, B["adaA"], "b_ada_b")
                if self.mode == "F":
                    P.op("pool", lambda e, u=u: e.collective_compute("AllGather", ALU.bypass, replica_groups=[[0, 1, 2, 3], [4, 5, 6, 7]],
                                                                     ins=[self.v_loc[u]], outs=[self.v_all[u]]),
                         extra_waits=[(d_, P.cnt[d_]) for d_ in dvs], writes=[self.Bvall[u]], dma=P.dma_sem(), dma_inc=1)
            if self.mode == "F":
                Bdl = Buf("dmloc")
                P.op("sp", lambda e: e.dma_start(out=self.dm_loc, in_=self.ones[:]), reads=[B["ones"]], writes=[Bdl], dma=P.dma_sem())
                P.op("pool", lambda e: e.collective_compute("AllGather", ALU.bypass, replica_groups=[[0, 1, 2, 3], [4, 5, 6, 7]],
                                                            ins=[self.dm_loc], outs=[self.dm_all]),
                     reads=[Bdl], writes=[self.Bdummy], dma=P.dma_sem(), dma_inc=1)
        P.barrier()

    def layer_b(self, KTf_d, Vf_d, mask_d, pos_d):
        P, B, nc, sb = self.P, self.B, self.nc, self.sb
        adaA = self.adaA
        QT = sb("QT", [128, NCH, T], BF16)
        BQ = [[Buf("QT%d_%d" % (hd, m)) for m in range(4)] for hd in range(8)]
        mask = sb("mask", [128, 8, 128], BF16)
        lamt = sb("lamt", [128, 8], F32)
        lamb = sb("lamb", [128, 2], BF16)
        B["mask"] = Buf("mask")
        B["lamt"] = Buf("lamt")
        P.op("pool", lambda e: e.dma_start(out=mask[:], in_=mask_d), writes=[B["mask"]], dma=P.dma_sem())
        stage(10)
        lo = COLS["lam"][0]
        so = COLS["subln_g"][0]
        P.op("dve", lambda e: e.tensor_tensor(out=lamb[:, 0:1], in0=self.cols[:, lo:lo + 1], in1=self.cols[:, lo + 1:lo + 2], op=ALU.mult),
             reads=[B["cols"]], writes=[B["lamt"]])
        P.op("dve", lambda e: e.tensor_tensor(out=lamb[:, 1:2], in0=self.cols[:, lo + 2:lo + 3], in1=self.cols[:, lo + 3:lo + 4], op=ALU.mult),
             reads=[B["cols"]], writes=[B["lamt"]])
        P.op("pe", lambda e: e.matmul(self.ps[5][:, 0:2], lhsT=self.ones[:], rhs=lamb[:, 0:2], start=True, stop=True),
             reads=[B["lamt"], B["ones"]], writes=[self.Bps[5]])
        P.op("act", lambda e: e.activation(out=lamt[:, 2:4], in_=self.ps[5][:, 0:2], func=AF.Exp),
             reads=[self.Bps[5]], writes=[B["lamt"]])
        P.chain("dve", [
            lambda e: e.scalar_tensor_tensor(out=lamt[:, 4:5], in0=lamt[:, 2:3], scalar=LAM_INIT, in1=lamt[:, 3:4],
                                             op0=ALU.add, op1=ALU.subtract),
            lambda e: e.tensor_scalar(out=lamt[:, 5:6], in0=lamt[:, 4:5], scalar1=-1.0, scalar2=None, op0=ALU.mult),
            lambda e: e.tensor_scalar(out=lamt[:, 6:8], in0=self.cols[:, so:so + 2], scalar1=1.0 - LAM_INIT, scalar2=None, op0=ALU.mult),
        ], reads=[B["lamt"], B["cols"]], writes=[B["lamt"]])

        stage(11)
        self.norm_stats()
        if self.mode == "B":
            self.ada_group("b_ada", 0, adaA, B["adaA"], "b_ada_b")
            self.ada_group("b_ada", 1, adaA, B["adaA"], "b_ada_b")
        self.mk_gsc(0, "b_norm1_g", adaA[:, 16:32], B["adaA"])
        self.modulate(0, lambda c: adaA[:, c:c + 1], B["adaA"])
        stage(12)
        with ExitStack() as ph:
            self.cosF = ph.enter_context(self.sbt("cosF", [128, T], F32))
            self.sinS = ph.enter_context(self.sbt("sinS", [128, T], F32))
            self.rope_tables(pos_d)
            self.qn = ph.enter_context(self.sbt("qn", [128, 512], F32))
            B["qn"] = Buf("qn")
            qg = self.col("q_norm_g")
            for u in range(8):
                wv, wb = self.ws.take(("b_q", u))
                for fc in range(2):
                    hh = u * 2 + fc
                    for th in range(2):
                        bank = th

                        def fk(e, wv=wv, fc=fc, th=th, bank=bank):
                            for k in range(NCH):
                                ins = e.matmul(self.ps[bank][:], lhsT=wv[:, k, fc * 128:(fc + 1) * 128],
                                               rhs=self.hT[:, k, th * 512:(th + 1) * 512], start=(k == 0), stop=(k == NCH - 1))
                            return ins
                        P.op("pe", fk, reads=[wb, B["hT%d" % th]], writes=[self.Bps[bank]])
                        self.qk_head(bank, th, qg, QT[:, hh, th * 512:(th + 1) * 512], BQ[u][2 * th])
                        BQ[u][2 * th + 1].w = BQ[u][2 * th].w
                self.ws.release()
        P.barrier()
        stage(13)

        with ExitStack() as ph:
            ex = [ph.enter_context(self.sbt("kvx%d" % i, [128, 4096], BF16)) for i in range(1)]
            slots = [self.hT[:, 4 * i:4 * i + 4, :].rearrange("p c t -> p (c t)") for i in range(4)] + [e_[:] for e_ in ex]
            NS = len(slots)
            sbufs = [Buf("kvs%d" % i) for i in range(NS)]
            ssems = [P.dma_sem() for _ in range(NS)]
            plan = []
            fused = self.mode == "F"
            for hd in range(8):
                if fused:
                    plan.append(("ktf", KTf_d[2 * hd], hd))
                    plan.append(("ktf", KTf_d[2 * hd + 1], hd))
                    plan.append(("vf", (Vf_d[hd][0], Vf_d[hd][1]), hd))
                    plan.append(("vf", (Vf_d[hd][2], Vf_d[hd][3]), hd))
                else:
                    plan.append(("kt", KTf_d[2 * hd], hd))
                    plan.append(("kt", KTf_d[2 * hd + 1], hd))
                    plan.append(("v", Vf_d[hd, 0], hd))
                    plan.append(("v", Vf_d[hd, 1], hd))

            def kmap(kb):
                if not fused:
                    return kb * 128, kb // 16, kb % 16
                m8, r8 = kb // 8, kb % 8
                if r8 < 4:
                    j, sl_ = r8, 2 * m8
                else:
                    j, sl_ = 7 - r8, 2 * m8 + 1
                return j * 1024 + sl_ * 128, j // 2, (j % 2) * 8 + sl_
            st = {"issued": 0, "closed": 0}

            def kview(s, kind):
                if kind == "kt":
                    return slots[s]
                if kind == "ktf":
                    return slots[s].rearrange("p (j t) -> p j t", j=4)
                if kind == "vf":
                    return slots[s].rearrange("p (j s f) -> p j s f", j=2, s=8)
                return slots[s].rearrange("p (k f) -> p k f", k=16)

            def pump():
                while st["issued"] < len(plan) and st["issued"] - NS < st["closed"]:
                    i = st["issued"]
                    kind, src, phd = plan[i]
                    s = i % NS
                    dst = kview(s, kind)
                    if kind == "vf":
                        for jj in range(2):
                            P.op("sp", (lambda dst, src: (lambda e: e.dma_start(out=dst, in_=src)))(dst[:, jj], src[jj]),
                                 reads=[self.Bvall[phd], self.Bdummy], writes=[sbufs[s]], dma=ssems[s])
                    else:
                        P.op("sp", (lambda dst, src: (lambda e: e.dma_start(out=dst, in_=src)))(dst, src),
                             reads=([self.Bktall[phd], self.Bdummy] if kind == "ktf" else []),
                             writes=[sbufs[s]], dma=ssems[s])
                    st["issued"] += 1

            pT = [self.sqb[0][:, 0, :], self.sqb[0][:, 1, :], self.sqb[1][:, 0, :], self.sqb[1][:, 1, :]]
            BpT = [Buf("pT%d" % i) for i in range(4)]
            accs = [ph.enter_context(self.sbt("accs%d" % i, [128, 260], F32)) for i in range(4)]
            Baccs = [Buf("accs%d" % i) for i in range(4)]
            fin = ph.enter_context(self.sbt("fin", [128, 8], F32))
            Bfin = Buf("fin")
            onbs = [ph.enter_context(self.sbt("onb%d" % i, [128, 256], BF16)) for i in range(2)]
            Bonbs = [Buf("onb0"), Buf("onb1")]
            identb = ph.enter_context(self.sbt("identb", [128, 128], BF16))
            Bidb = Buf("identb")
            P.op("pool", lambda e: e.memset(identb[:], 1.0), writes=[Bidb])
            P.op("pool", lambda e: e.affine_select(out=identb[:], in_=identb[:], pattern=[[-1, 128]], compare_op=ALU.is_equal,
                                                   fill=0.0, base=0, channel_multiplier=1), reads=[Bidb], writes=[Bidb])
            t0, t1, t2 = self.tmpf
            scale = 128.0 ** -0.5
            npt = 0
            nst = 0
            ada_next = 16
            for hd in range(8):
                base = hd * 4
                pump()
                if hd == 0 and self.DBG:
                    if fused:
                        P.op("sp", lambda e: e.dma_start(out=self.dbgA[1], in_=self.kt_all[0]), reads=[self.Bktall[0]], dma=self.do)
                    P.op("sp", lambda e: e.dma_start(out=self.dbgK, in_=slots[0]), reads=[sbufs[0]], dma=self.do)
                    P.op("sp", lambda e: e.dma_start(out=self.dbgV, in_=slots[2]), reads=[sbufs[2]], dma=self.do)
                kts = [kview((base + i) % NS, "kt") for i in range(2)]
                vs = [kview((base + 2 + i) % NS, "v") for i in range(2)]
                kb_ = [sbufs[(base + i) % NS] for i in range(4)]
                steps = [(m, kb) for m in range(4) for kb in range(8 * m + 8)]
                info = {}
                SB = (0, 1)

                def emit_S(idx, hd=hd, kts=kts, kb_=kb_):
                    nonlocal nst, npt
                    m, kb = steps[idx]
                    kc, vh, vb = kmap(kb)
                    full = kb < 8 * m + 4
                    off = 0 if full else 128
                    sbank = SB[nst % 2]
                    nst += 1
                    pi = npt % 4
                    npt += 1
                    info[idx] = (pi, full, vh, vb)
                    psS = self.ps[sbank]

                    def fs(e):
                        for sub in range(2):
                            ins = e.matmul(psS[:, sub * 256 + off:sub * 256 + 256],
                                           lhsT=kts[sub][:, kc:kc + 128],
                                           rhs=QT[:, 2 * hd + sub, 256 * m + off:256 * m + 256], start=True, stop=True)
                        return ins
                    P.op("pe", fs, reads=[kb_[0], kb_[1], BQ[hd][m]], writes=[self.Bps[sbank]])
                    src3 = psS[:].rearrange("p (s q) -> p s q", s=2)[:, :, off:256]
                    dst3 = pT[pi].rearrange("p (s q) -> p s q", s=2)[:, :, off:256]
                    P.op("act", lambda e: e.activation(out=dst3, in_=src3, func=AF.Exp, scale=scale),
                         reads=[self.Bps[sbank]], writes=[BpT[pi]])
                    if kb >= 8 * m:
                        par = 0 if full else 1
                        r = kb - 8 * m - 4 * par
                        moff = 0 if par == 0 else 128
                        for sub in range(2):
                            dm = pT[pi][:, sub * 256 + moff:sub * 256 + moff + 128]
                            P.op("dve", lambda e, dm=dm: e.tensor_tensor(
                                out=dm, in0=dm, in1=mask[:, par * 4 + r, :], op=ALU.mult),
                                reads=[BpT[pi], B["mask"]], writes=[BpT[pi]])

                def emit_PV(idx, hd=hd, vs=vs, kb_=kb_):
                    m, kb = steps[idx]
                    pi, full, vh, vb = info[idx]
                    vv = vs[vh][:, vb, :]

                    def fpv(e):
                        for sl in ((0, 1) if full else (1,)):
                            last = (8 * m + 3) if sl == 0 else (8 * m + 7)
                            for sub in range(2):
                                acc = self.ps[2 + sl * 2 + sub]
                                lh = pT[pi][:, sub * 256 + sl * 128:sub * 256 + sl * 128 + 128]
                                e.matmul(acc[:, 0:256], lhsT=lh, rhs=vv, start=(kb == 0), stop=(kb == last),
                                         skip_group_check=True)
                                ins = e.matmul(acc[:, 256:257], lhsT=lh, rhs=self.ones[:, 0:1], start=False, stop=(kb == last),
                                               skip_group_check=True)
                        return ins
                    P.op("pe", fpv, reads=[BpT[pi], kb_[2 + vh], B["ones"]],
                         writes=[self.Bps[2 + sl * 2 + sub] for sl in ((0, 1) if full else (1,)) for sub in range(2)])
                    if kb == 8 * m + 7:
                        finalize(m)
                        pending.append((idx + 6, m))

                def finalize(m, hd=hd):
                    for a in range(4):
                        P.op("act", lambda e, a=a: e.activation(out=accs[a][:, 0:257], in_=self.ps[2 + a][:, 0:257], func=AF.Copy),
                             reads=[self.Bps[2 + a]], writes=[Baccs[a]])
                    for sl in range(2):
                        aA, aB = accs[2 * sl], accs[2 * sl + 1]
                        onb, Bonb = onbs[sl], Bonbs[sl]
                        P.chain("dve", [
                            lambda e, aA=aA: e.reciprocal(out=fin[:, 0:1], in_=aA[:, 256:257]),
                            lambda e, aB=aB: e.reciprocal(out=fin[:, 1:2], in_=aB[:, 256:257]),
                            lambda e: e.tensor_tensor(out=fin[:, 2:3], in0=fin[:, 1:2], in1=lamt[:, 5:6], op=ALU.mult),
                        ], reads=[Baccs[2 * sl], Baccs[2 * sl + 1], Bfin, B["lamt"]], writes=[Bfin])
                        P.op("dve", lambda e, aB=aB: e.tensor_scalar(out=t0[:, 0:256], in0=aB[:, 0:256], scalar1=fin[:, 2:3], scalar2=None,
                                                                     op0=ALU.mult),
                             reads=[Baccs[2 * sl + 1], Bfin], writes=[B["tmpf0"]])
                        P.op("dve", lambda e, aA=aA: e.scalar_tensor_tensor(out=t1[:, 0:256], in0=aA[:, 0:256], scalar=fin[:, 0:1],
                                                                            in1=t0[:, 0:256], op0=ALU.mult, op1=ALU.add),
                             reads=[Baccs[2 * sl], Bfin, B["tmpf0"]], writes=[B["tmpf1"]])
                        P.op("act", lambda e: e.activation(out=t2[:, 0:256], in_=t1[:, 0:256], func=AF.Square, accum_out=fin[:, 3:4]),
                             reads=[B["tmpf1"], Bfin], writes=[B["tmpf2"], Bfin])
                        P.op("act", lambda e: e.activation(out=fin[:, 4:5], in_=fin[:, 3:4], func=AF.Sqrt, bias=self.eps_col[:, 0:1],
                                                           scale=1.0 / 256),
                             reads=[Bfin, B["consts"]], writes=[Bfin])
                        P.op("dve", lambda e: e.reciprocal(out=fin[:, 5:6], in_=fin[:, 4:5]), reads=[Bfin], writes=[Bfin])
                        P.op("dve", lambda e, onb=onb: e.tensor_scalar(out=onb[:], in0=t1[:, 0:256], scalar1=fin[:, 5:6], scalar2=None, op0=ALU.mult),
                             reads=[B["tmpf1"], Bfin], writes=[Bonb])

                def finalize_b(m, hd=hd):
                    def ftr(e):
                        for sl in range(2):
                            for c2 in range(2):
                                ins = e.transpose(out=self.psT[:, sl * 256 + c2 * 128:sl * 256 + (c2 + 1) * 128],
                                                  in_=onbs[sl][:, c2 * 128:(c2 + 1) * 128], identity=identb[:])
                        return ins
                    P.op("pe", ftr, reads=[Bonbs[0], Bonbs[1], Bidb], writes=[self.BpsT])
                    for c2 in range(2):
                        P.op("dve", lambda e, c2=c2: e.tensor_scalar(
                            out=QT[:, 2 * hd + c2, 256 * m:256 * m + 256].rearrange("p (s q) -> p s q", s=2),
                            in0=self.psT[:, 0:512].rearrange("p (s c q) -> p s c q", s=2, c=2)[:, :, c2, :],
                            scalar1=lamt[:, 6 + c2:7 + c2], scalar2=None, op0=ALU.mult),
                            reads=[self.BpsT, B["lamt"]], writes=[BQ[hd][m]])

                LA = 1
                pending = []
                for idx in range(len(steps) + LA):
                    if idx < len(steps):
                        emit_S(idx)
                    if idx >= LA:
                        emit_PV(idx - LA)
                    while pending and pending[0][0] <= idx - LA:
                        finalize_b(pending.pop(0)[1])
                while pending:
                    finalize_b(pending.pop(0)[1])
                st["closed"] = base + 4
                if STOP <= 14 + hd:
                    ada_next = 48
                    break
                for _ in range(4):
                    self.ada_unit("b_ada", ada_next, adaA, B["adaA"], "b_ada_b")
                    ada_next += 1
            assert ada_next == 48
        P.barrier()
        stage(21)
        allQ = [BQ[hd][m] for hd in range(8) for m in range(4)]
        self.proj_residual("b_o", lambda k, th: QT[:, k, th * 512:(th + 1) * 512], lambda th: allQ,
                           lambda m: adaA[:, 32 + m:33 + m], B["adaA"])
        stage(22)
        self.norm_stats()
        self.mk_gsc(1, "b_norm2_g", adaA[:, 64:80], B["adaA"])
        self.modulate(1, lambda c: adaA[:, 48 + c:49 + c], B["adaA"])
        self.ffn("b", lambda m: adaA[:, 80 + m:81 + m], B["adaA"])


_NC_CACHE = {}


def get_nc(mode):
    if mode not in _NC_CACHE:
        _NC_CACHE[mode] = Builder(mode).build()
    return _NC_CACHE[mode]


def host_cols(inp, b):
    cols = np.zeros((128, NCOL), np.float32)

    def put(name, arr):
        o, w = COLS[name]
        cols[:, o:o + w] = arr
    put("a_norm1_g", colv(inp["a_norm1_g"][0]))
    put("a_norm2_g", colv(inp["a_norm2_g"][0]))
    put("a_in_b_u", colv(inp["a_in_b"][0][:D]))
    put("a_ln_g", colv(inp["a_sgu_ln_g"][0]))
    put("a_ln_b", colv(inp["a_sgu_ln_b"][0]))
    put("kv_norm_g", colv(inp["kv_norm_g"]))
    put("b_norm1_g", colv(inp["b_norm1_g"][0]))
    put("b_norm2_g", colv(inp["b_norm2_g"][0]))
    put("a_ada_b", colv(inp["a_ada_b"][0]))
    put("kv_ada_b", colv(inp["kv_ada_b"]))
    put("b_ada_b", colv(inp["b_ada_b"][0]))
    put("k_norm_g", colv(inp["k_norm_g"]))
    put("q_norm_g", colv(inp["b_q_norm_g"][0]))
    put("subln_g", colv(inp["b_subln_g"][0]))
    put("lam", np.stack([inp["b_lambda_q1"][0], inp["b_lambda_k1"][0], inp["b_lambda_q2"][0], inp["b_lambda_k2"][0]], 1))
    inv_freq = 1.0 / (10000.0 ** (np.arange(0, 128, 2, dtype=np.float32) / 128.0))
    put("invf", (np.concatenate([inv_freq, inv_freq]) / (2 * np.pi)).astype(np.float32)[:, None])
    put("cT", colv(inp["c"][b]))
    return cols


def host_mask(j):
    m = np.zeros((128, 8, 128), np.float32)
    tri = (np.arange(128)[:, None] <= np.arange(128)[None, :]).astype(np.float32)
    for par in range(2):
        p = j if par == 0 else 7 - j
        for r in range(4):
            kbk = r if par == 0 else 4 + r
            if kbk < p:
                m[:, par * 4 + r, :] = 1.0
            elif kbk == p:
                m[:, par * 4 + r, :] = tri
    return m


def run_a(inp, ncores=8):
    nc = get_nc("A")
    shared = {
        "a_ada_w": tile_w(inp["a_ada_w"][0]), "a_in_w": tile_w(inp["a_in_w"][0]), "a_out_w": tile_w(inp["a_out_w"][0]),
        "a_ffn_wi": tile_w(inp["a_ffn_wi"][0]), "a_ffn_wo": tile_wo(inp["a_ffn_wo"][0]),
        "kv_ada_w": tile_w(inp["kv_ada_w"]), "kv_w": tile_w(inp["kv_w"]),
        "sgu_w": np.ascontiguousarray(inp["a_sgu_w"][0]),
        "rows": np.ascontiguousarray(np.stack([inp["a_in_b"][0][D:], inp["a_sgu_ln_b"][0], inp["a_sgu_b"][0].reshape(-1)]).astype(np.float32)),
    }
    in_maps = []
    for i in range(8):
        b, j = i // 4, i % 4
        idx = tok_index(j)
        m = dict(shared)
        m["xT"] = np.ascontiguousarray(inp["x"][b][idx].T)
        m["pos"] = np.ascontiguousarray(inp["positions"][b][idx][None, :]).astype(np.int32)
        m["cols"] = host_cols(inp, b)
        in_maps.append(m)
    in_maps = in_maps[:ncores]
    res = run_bass_kernel_spmd(nc, in_maps, core_ids=list(range(ncores)))
    return res.results


def run_b(inp, ra, ncores=8):
    nc = get_nc("B")
    shared = {
        "b_ada_w": tile_w(inp["b_ada_w"][0]), "b_q_w": tile_w(inp["b_q_w"][0]), "b_o_w": tile_w(inp["b_o_w"][0]),
        "b_ffn_wi": tile_w(inp["b_ffn_wi"][0]), "b_ffn_wo": tile_wo(inp["b_ffn_wo"][0]),
    }
    KTf, Vf = [], []
    for b in range(2):
        kt = np.zeros((16, 128, 4096), ml_dtypes.bfloat16)
        v = np.zeros((8, 4096, 256), ml_dtypes.bfloat16)
        for j in range(4):
            idx = tok_index(j)
            kt[:, :, idx] = ra[4 * b + j]["KT"]
            v[:, idx, :] = ra[4 * b + j]["V"]
        KTf.append(kt)
        Vf.append(np.ascontiguousarray(v.reshape(8, 2, 16, 128, 256).transpose(0, 1, 3, 2, 4)))
    in_maps = []
    for i in range(8):
        b, j = i // 4, i % 4
        idx = tok_index(j)
        m = dict(shared)
        m["xT"] = np.ascontiguousarray(ra[i]["outT"])
        m["pos"] = np.ascontiguousarray(inp["positions"][b][idx][None, :]).astype(np.int32)
        m["cols"] = host_cols(inp, b)
        m["KTf"] = KTf[b]
        m["Vf"] = Vf[b]
        m["mask"] = host_mask(j)
        in_maps.append(m)
    in_maps = in_maps[:ncores]
    res = run_bass_kernel_spmd(nc, in_maps, core_ids=list(range(ncores)))
    return res.results


def run_f(inp, ncores=8):
    nc = get_nc("F")
    shared = {
        "a_ada_w": tile_w(inp["a_ada_w"][0]), "a_in_w": tile_w(inp["a_in_w"][0]), "a_out_w": tile_w(inp["a_out_w"][0]),
        "a_ffn_wi": tile_w(inp["a_ffn_wi"][0]), "a_ffn_wo": tile_wo(inp["a_ffn_wo"][0]),
        "kv_ada_w": tile_w(inp["kv_ada_w"]), "kv_w": tile_w(inp["kv_w"]),
        "sgu_w": np.ascontiguousarray(inp["a_sgu_w"][0]),
        "rows": np.ascontiguousarray(np.stack([inp["a_in_b"][0][D:], inp["a_sgu_ln_b"][0], inp["a_sgu_b"][0].reshape(-1)]).astype(np.float32)),
        "b_ada_w": tile_w(inp["b_ada_w"][0]), "b_q_w": tile_w(inp["b_q_w"][0]), "b_o_w": tile_w(inp["b_o_w"][0]),
        "b_ffn_wi": tile_w(inp["b_ffn_wi"][0]), "b_ffn_wo": tile_wo(inp["b_ffn_wo"][0]),
    }
    in_maps = []
    for i in range(8):
        b, j = i // 4, i % 4
        idx = tok_index(j)
        m = dict(shared)
        m["xT"] = np.ascontiguousarray(inp["x"][b][idx].T)
        m["pos"] = np.ascontiguousarray(inp["positions"][b][idx][None, :]).astype(np.int32)
        m["cols"] = host_cols(inp, b)
        m["mask"] = host_mask(j)
        in_maps.append(m)
    in_maps = in_maps[:ncores]
    res = run_bass_kernel_spmd(nc, in_maps, core_ids=list(range(ncores)))
    return res.results


def kernel(**inp):
    inp = {k: np.asarray(v) for k, v in inp.items()}
    rf = run_f(inp)
    out = np.zeros((2, 4096, D), np.float32)
    for i in range(8):
        b, j = i // 4, i % 4
        out[b, tok_index(j), :] = rf[i]["outT"].T
    return out
```

```python
import math
import numpy as np
import ml_dtypes
import concourse.bass as bass
import concourse.mybir as mybir
from concourse.bass_utils import run_bass_kernel_spmd
from contextlib import ExitStack

F32 = mybir.dt.float32
BF16 = mybir.dt.bfloat16
I32 = mybir.dt.int32
ALU = mybir.AluOpType
AF = mybir.ActivationFunctionType

D = 2048
NCH = 16
T = 1024
DFF = 5632
NJ = 44
EPS = 1e-6
NSLOT = 5
ENGS = ("pe", "act", "dve", "pool", "sp")

COLS = {}
_o = 0
for _n, _w in [("a_norm1_g", 16), ("a_norm2_g", 16), ("a_in_b_u", 16), ("a_ln_g", 16), ("a_ln_b", 16), ("kv_norm_g", 16),
               ("b_norm1_g", 16), ("b_norm2_g", 16), ("a_ada_b", 96), ("kv_ada_b", 32), ("b_ada_b", 96),
               ("k_norm_g", 1), ("q_norm_g", 1), ("subln_g", 2), ("lam", 4), ("invf", 1), ("cT", 16)]:
    COLS[_n] = (_o, _w)
    _o += _w
NCOL = _o


class Buf:
    __slots__ = ("name", "w", "r")

    def __init__(self, name=""):
        self.name = name
        self.w = None
        self.r = {}


class Prog:
    def __init__(self, nc, es):
        self.nc = nc
        self.es = es
        self.q = {e: [] for e in ENGS}
        self.sem = {}
        self.cnt = {}
        self.seen = {e: {} for e in ENGS}
        for e in ENGS:
            self.new_sem("E_" + e)
        self.n_dma_sem = 0

    def new_sem(self, name):
        self.sem[name] = self.es.enter_context(self.nc.semaphore(name))
        self.cnt[name] = 0
        return name

    def dma_sem(self):
        self.n_dma_sem += 1
        return self.new_sem("D%d" % self.n_dma_sem)

    def op(self, eng, fn, reads=(), writes=(), dma=None, extra_waits=(), dma_inc=16):
        waits = {}

        def need(tok):
            if tok is None:
                return
            s, v = tok
            if waits.get(s, 0) < v:
                waits[s] = v

        for b in reads:
            need(b.w)
        for b in writes:
            need(b.w)
            for s, v in b.r.items():
                need((s, v))
        for t in extra_waits:
            need(t)
        if eng == "pe":
            waits.pop("E_pe", None)
        wl = []
        seen = self.seen[eng]
        for s, v in waits.items():
            if seen.get(s, 0) < v:
                seen[s] = v
                wl.append((s, v))
        if dma is not None:
            s = dma
            self.cnt[s] += dma_inc
            inc = (s, dma_inc)
        else:
            s = "E_" + eng
            self.cnt[s] += 1
            inc = (s, 1)
        tok = (s, self.cnt[s])
        self.q[eng].append((wl, fn, inc))
        for b in reads:
            if b.r.get(s, 0) < tok[1]:
                b.r[s] = tok[1]
        for b in writes:
            b.w = tok
            b.r = {}
        return tok

    def wait_only(self, eng, toks):
        wl = []
        seen = self.seen[eng]
        for s, v in toks:
            if seen.get(s, 0) < v:
                seen[s] = v
                wl.append((s, v))
        if wl:
            self.q[eng].append((wl, None, None))

    def barrier(self):
        toks = [(s, c) for s, c in self.cnt.items() if c > 0]
        for e in ENGS:
            self.wait_only(e, toks)

    def chain(self, eng, fns, reads=(), writes=()):
        for fn in fns:
            tok = self.op(eng, fn, reads=reads, writes=writes)
        return tok

    def emit(self):
        nc = self.nc
        block = self.es.enter_context(nc.Block())
        sem = self.sem

        def replay(engobj, items):
            for wl, fn, inc in items:
                for s, v in wl:
                    engobj.wait_ge(sem[s], v)
                if fn is not None:
                    ins = fn(engobj)
                    ins.then_inc(sem[inc[0]], inc[1])

        q = self.q

        @block.tensor
        def _(e):
            replay(e, q["pe"])

        @block.scalar
        def _(e):
            replay(e, q["act"])

        @block.vector
        def _(e):
            replay(e, q["dve"])

        @block.gpsimd
        def _(e):
            replay(e, q["pool"])

        @block.sync
        def _(e):
            replay(e, q["sp"])


class WStream:
    def __init__(self, P, nc, es, nslot):
        self.P = P
        self.nslot = nslot
        self.slots = [es.enter_context(nc.sbuf_tensor("s_wslot%d" % i, [128, 4096], BF16)) for i in range(nslot)]
        self.bufs = [Buf("wslot%d" % i) for i in range(nslot)]
        self.sems = [P.dma_sem() for _ in range(nslot)]
        self.plan = []
        self.issued = 0
        self.taken = 0
        self.closed = 0

    def add(self, key, src, shape):
        self.plan.append((key, src, shape))

    def pump(self):
        while self.issued < len(self.plan) and self.issued - self.nslot < self.closed:
            i = self.issued
            key, src, shape = self.plan[i]
            s = i % self.nslot
            dst = self.view(s, shape)
            self.P.op("pool", (lambda dst, src: (lambda e: e.dma_start(out=dst, in_=src)))(dst, src),
                      writes=[self.bufs[s]], dma=self.sems[s])
            self.issued += 1

    def view(self, s, shape):
        if shape == "k":
            return self.slots[s][:].rearrange("p (k f) -> p k f", k=16)
        else:
            return self.slots[s][:].rearrange("p (j f) -> p j f", j=2)

    def take(self, key):
        i = self.taken
        assert self.plan[i][0] == key, (self.plan[i][0], key)
        assert i - self.closed < self.nslot, "too many open weight units"
        self.pump()
        assert self.issued > i
        self.taken += 1
        s = i % self.nslot
        return self.view(s, self.plan[i][2]), self.bufs[s]

    def release(self):
        self.closed = self.taken
        self.pump()


def tile_w(W):
    K, Fd = W.shape
    return np.ascontiguousarray(W.reshape(K // 128, 128, Fd // 256, 256).transpose(2, 1, 0, 3))


def tile_wo(W):
    K, Fd = W.shape
    return np.ascontiguousarray(W.reshape(K // 256, 2, 128, Fd).transpose(0, 2, 1, 3))


def colv(v):
    v = np.asarray(v, np.float32).reshape(-1, 128)
    return np.ascontiguousarray(v.T)


def tok_index(j):
    idx = []
    for s in range(8):
        p = 8 * (s // 2) + (j if s % 2 == 0 else 7 - j)
        idx.append(np.arange(p * 128, (p + 1) * 128))
    return np.concatenate(idx)


LAM_INIT = 0.8 - 0.6 * math.exp(-0.3 * 1)


import os
STOP = int(os.environ.get("KSTOP", "99"))


class _Stop(Exception):
    pass


def stage(n):
    if STOP <= n:
        raise _Stop()


class Builder:
    def __init__(self, mode):
        self.mode = mode
        self.nc = bass.Bass("TRN2", target_bir_lowering=False)

    def sbt(self, name, shape, dt):
        self._uid = getattr(self, "_uid", 0) + 1
        return self.nc.sbuf_tensor("s_%s_%d" % (name, self._uid), list(shape), dt)

    def dram_in(self, name, shape, dt=F32):
        return self.nc.dram_tensor(name, list(shape), dt, kind="ExternalInput").ap()

    def dram_out(self, name, shape, dt=F32):
        return self.nc.dram_tensor(name, list(shape), dt, kind="ExternalOutput").ap()

    def build(self):
        nc = self.nc
        mode = self.mode
        with ExitStack() as es:
            self.es = es
            P = self.P = Prog(nc, es)
            sb = lambda n, s, d: es.enter_context(self.sbt("" + n, list(s), d))
            self.sb = sb
            xT_d = self.dram_in("xT", [D, T])
            cols_d = self.dram_in("cols", [128, NCOL])
            pos_d = self.dram_in("pos", [1, T], I32)
            out_d = self.dram_out("outT", [D, T])
            self.DBG = bool(int(os.environ.get("KDEBUG", "0")))
            if self.DBG:
                self.dbgK = self.dram_out("dbgK", [128, 4096], BF16)
                self.dbgV = self.dram_out("dbgV", [128, 4096], BF16)
                self.dbgA = self.dram_out("dbgA", [3, 1024, 1024], BF16)
                self.dbgL = self.dram_out("dbgL", [256, 1024], BF16)
            W = {}
            if mode in ("A", "F"):
                rows_d = self.dram_in("rows", [3, D])
                sguw_d = self.dram_in("sgu_w", [16, 128, 128])
                W["a_ada"] = self.dram_in("a_ada_w", [48, 128, 16, 256])
                W["a_in"] = self.dram_in("a_in_w", [16, 128, 16, 256])
                W["a_out"] = self.dram_in("a_out_w", [8, 128, 16, 256])
                W["a_wi"] = self.dram_in("a_ffn_wi", [44, 128, 16, 256])
                W["a_wo"] = self.dram_in("a_ffn_wo", [22, 128, 2, 2048])
                W["kv_ada"] = self.dram_in("kv_ada_w", [16, 128, 16, 256])
                W["kv"] = self.dram_in("kv_w", [16, 128, 16, 256])
            if mode == "A":
                KT_d = self.dram_out("KT", [16, 128, T], BF16)
                V_d = self.dram_out("V", [8, T, 256], BF16)
            if mode == "F":
                self.kt_loc = [nc.dram_tensor("kt_loc%d" % i, [256, T], BF16, kind="Internal").ap() for i in range(8)]
                self.kt_all = [nc.dram_tensor("kt_all%d" % i, [4 * 256, T], BF16, kind="Internal").ap() for i in range(8)]
                self.v_loc = [nc.dram_tensor("v_loc%d" % i, [256, T], BF16, kind="Internal").ap() for i in range(8)]
                self.v_all = [nc.dram_tensor("v_all%d" % i, [4 * 256, T], BF16, kind="Internal").ap() for i in range(8)]
                self.dm_loc = nc.dram_tensor("dm_loc", [128, 128], BF16, kind="Internal").ap()
                self.dm_all = nc.dram_tensor("dm_all", [4 * 128, 128], BF16, kind="Internal").ap()
                KT_d = [self.kt_loc[hh // 2][(hh % 2) * 128:(hh % 2) * 128 + 128, :] for hh in range(16)]
                V_d = [self.v_loc[u].rearrange("r (q d) -> (r q) d", d=256) for u in range(8)]
            if mode in ("B", "F"):
                W["b_ada"] = self.dram_in("b_ada_w", [48, 128, 16, 256])
                W["b_q"] = self.dram_in("b_q_w", [8, 128, 16, 256])
                W["b_o"] = self.dram_in("b_o_w", [8, 128, 16, 256])
                W["b_wi"] = self.dram_in("b_ffn_wi", [44, 128, 16, 256])
                W["b_wo"] = self.dram_in("b_ffn_wo", [22, 128, 2, 2048])
                mask_d = self.dram_in("mask", [128, 8, 128])
            if mode == "B":
                KTf_d = self.dram_in("KTf", [16, 128, 4096], BF16)
                Vf_d = self.dram_in("Vf", [8, 2, 128, 16, 256], BF16)
            if mode == "F":
                KTf_d = [self.kt_all[hh // 2].rearrange("(j s d) t -> s d j t", j=4, s=2, d=128)[hh % 2] for hh in range(16)]
                Vf_d = [[self.v_all[u].rearrange("(j r) (q d) -> j (r q) d", j=4, d=256)[j].rearrange("(s p) d -> p s d", p=128)
                         for j in range(4)] for u in range(8)]
            self.W = W

            self.xT = sb("xT", [128, NCH, T], F32)
            self.hT = sb("hT", [128, NCH, T], BF16)
            self.cols = sb("cols", [128, NCOL], F32)
            self.rstd = sb("rstd", [128, T], F32)
            self.ones = sb("ones", [128, 128], BF16)
            self.ones_f = sb("ones_f", [128, 128], F32)
            self.sc_bf = sb("sc_bf", [128, NCH], BF16)
            self.adaA = sb("adaA", [128, 96], F32)
            self.adaK = sb("adaK", [128, 32], F32)
            self.mods = sb("mods", [128, 4, 16], F32)
            self.tmpf = [sb("tmpf%d" % i, [128, 512], F32) for i in range(3)]
            self.sqb = [sb("sqb%d" % i, [128, 2, 512], BF16) for i in range(2)]
            self.eps_col = sb("eps_col", [128, 4], F32)
            B = self.B = {}
            for n in ["xT0", "xT1", "hT0", "hT1", "cols", "rstd0", "rstd1", "ones", "sc_bf", "adaA", "adaK", "mods",
                      "tmpf0", "tmpf1", "tmpf2", "sqb0", "sqb1", "out", "consts"]:
                B[n] = Buf(n)
            self.ps = [es.enter_context(nc.psum_tensor("ps%d" % i, [128, 512], F32)) for i in range(7)]
            self.psT = es.enter_context(nc.psum_tensor("psT", [128, 1024], BF16))
            self.psA = self.psT[:, 512:1024].bitcast(F32)
            self.Bps = [Buf("ps%d" % i) for i in range(7)]
            self.BpsT = Buf("psT")
            self.BpsA = self.BpsT
            self.ws = WStream(P, nc, es, NSLOT)
            self.dl = P.dma_sem()
            self.dx = P.dma_sem()
            self.do = P.dma_sem()
            xT, cols, ones, ones_f, sc_bf = self.xT, self.cols, self.ones, self.ones_f, self.sc_bf

            ws = self.ws
            def padd(wname, idx):
                ws.add((wname, idx), W[wname][idx], "k")
            if mode in ("A", "F"):
                self.plan_ada("a_ada", 0)
                self.plan_ada("a_ada", 1)
                for cb in range(8):
                    padd("a_in", 8 + cb)
                    padd("a_ada", 16 + 2 * cb)
                    padd("a_ada", 17 + 2 * cb)
                for u in range(8):
                    padd("a_in", u)
                    padd("a_ada", 32 + u)
                for u in range(8):
                    padd("a_out", u)
                    padd("a_ada", 40 + u)
                self.plan_ffn("a", "kv_ada")
                for u in range(16):
                    padd("kv", u)
                    if mode == "F":
                        padd("b_ada", u)
            if mode in ("B", "F"):
                if mode == "B":
                    self.plan_ada("b_ada", 0)
                    self.plan_ada("b_ada", 1)
                for u in range(8):
                    ws.add(("b_q", u), W["b_q"][u], "k")
                for g in (2, 3, 4, 5):
                    self.plan_ada("b_ada", g)
                for u in range(8):
                    ws.add(("b_o", u), W["b_o"][u], "k")
                self.plan_ffn("b")

            P.op("sp", lambda e: e.dma_start(out=cols[:], in_=cols_d), writes=[B["cols"]], dma=P.dma_sem())
            for th in range(2):
                for cq in range(4):
                    src = xT_d.rearrange("(c p) t -> p c t", p=128)[:, 4 * cq:4 * cq + 4, th * 512:(th + 1) * 512]
                    dst = xT[:, 4 * cq:4 * cq + 4, th * 512:(th + 1) * 512]
                    P.op("sp", (lambda dst, src: (lambda e: e.dma_start(out=dst, in_=src)))(dst, src), dma=self.dx)
            B["xT0"].w = (self.dx, P.cnt[self.dx])
            B["xT1"].w = (self.dx, P.cnt[self.dx])
            P.op("dve", lambda e: e.memset(ones[:], 1.0), writes=[B["ones"]])
            P.op("dve", lambda e: e.memset(ones_f[:], 1.0), writes=[B["ones"]])
            P.op("dve", lambda e: e.memset(self.eps_col[:, 0:1], EPS), writes=[B["consts"]])
            P.op("dve", lambda e: e.memset(self.eps_col[:, 1:2], -3.1415920), writes=[B["consts"]])
            c0 = COLS["cT"][0]
            P.op("act", lambda e: e.activation(out=sc_bf[:], in_=cols[:, c0:c0 + 16], func=AF.Silu),
                 reads=[B["cols"]], writes=[B["sc_bf"]])

            self.Bktall = [Buf("ktall%d" % i) for i in range(8)]
            self.Bvall = [Buf("vall%d" % i) for i in range(8)]
            self.Bdummy = Buf("dummy")
            try:
                if mode in ("A", "F"):
                    self.layer_a(rows_d, sguw_d, pos_d, KT_d, V_d)
                if mode == "F" and self.DBG:
                    P.op("sp", lambda e: e.dma_start(out=self.dbgA[0], in_=self.kt_all[0]), reads=[self.Bktall[0]], dma=self.do)
                if mode in ("B", "F"):
                    self.layer_b(KTf_d, Vf_d, mask_d, pos_d)
                if mode == "F" and self.DBG:
                    P.op("sp", lambda e: e.dma_start(out=self.dbgA[2], in_=self.kt_all[0]), reads=[self.Bktall[0]], dma=self.do)
                    P.op("sp", lambda e: e.dma_start(out=self.dbgL, in_=self.kt_loc[0]), dma=self.do)
            except _Stop:
                ws.taken = len(ws.plan)

            for th in range(2):
                for cq in range(4):
                    dst = out_d.rearrange("(c p) t -> p c t", p=128)[:, 4 * cq:4 * cq + 4, th * 512:(th + 1) * 512]
                    src = xT[:, 4 * cq:4 * cq + 4, th * 512:(th + 1) * 512]
                    P.op("sp", (lambda dst, src: (lambda e: e.dma_start(out=dst, in_=src)))(dst, src),
                         reads=[B["xT%d" % th]], writes=[B["out"]], dma=self.do)
            P.wait_only("sp", [(self.do, P.cnt[self.do])] + [(d_, P.cnt[d_]) for d_ in getattr(self, "kv_out_sems", [])])
            assert ws.taken == len(ws.plan), (ws.taken, len(ws.plan))
            P.emit()
        return nc

    def col(self, name, i=0, n=1):
        o, w = COLS[name]
        return self.cols[:, o + i:o + i + n]

    def plan_ada(self, wname, g):
        for u in range(8):
            self.ws.add((wname, g * 8 + u), self.W[wname][g * 8 + u], "k")

    def plan_ffn(self, L, ada=None):
        for grp in range(11):
            for qq in range(2):
                q = grp * 2 + qq
                self.ws.add((L + "_wi", q), self.W[L + "_wi"][q], "k")
                self.ws.add((L + "_wi", 22 + q), self.W[L + "_wi"][22 + q], "k")
            for qq in range(2):
                q = grp * 2 + qq
                self.ws.add((L + "_wo", q), self.W[L + "_wo"][q], "j")
            if ada is not None and grp < 8:
                self.ws.add((ada, 2 * grp), self.W[ada][2 * grp], "k")
                self.ws.add((ada, 2 * grp + 1), self.W[ada][2 * grp + 1], "k")

    def ada_unit(self, wname, idx, dst, dbuf, bias_name):
        P, B = self.P, self.B
        ps = self.psA
        g, u = idx // 8, idx % 8
        wv, wb = self.ws.take((wname, idx))

        def fn(e, wv=wv, u=u):
            for fc in range(2):
                c = u * 2 + fc
                for k in range(NCH):
                    ins = e.matmul(ps[:, c:c + 1], lhsT=wv[:, k, fc * 128:(fc + 1) * 128],
                                   rhs=self.sc_bf[:, k:k + 1], start=(k == 0), stop=(k == NCH - 1))
            return ins
        P.op("pe", fn, reads=[wb, B["sc_bf"]], writes=[self.BpsA])
        self.ws.release()
        if u == 7:
            bo = COLS[bias_name][0] + g * 16
            P.op("dve", lambda e: e.tensor_tensor(out=dst[:, g * 16:(g + 1) * 16], in0=ps[:, 0:16],
                                                  in1=self.cols[:, bo:bo + 16], op=ALU.add),
                 reads=[self.BpsA, B["cols"]], writes=[dbuf])

    def ada_group(self, wname, g, dst, dbuf, bias_name):
        for u in range(8):
            self.ada_unit(wname, g * 8 + u, dst, dbuf, bias_name)

    def mk_gsc(self, slot, gname, sc_ap, src_buf):
        P, B = self.P, self.B
        o = COLS[gname][0]
        P.op("dve", lambda e: e.scalar_tensor_tensor(out=self.mods[:, slot, :], in0=sc_ap, scalar=1.0,
                                                     in1=self.cols[:, o:o + 16], op0=ALU.add, op1=ALU.mult),
             reads=[src_buf, B["cols"]], writes=[B["mods"]])

    def norm_stats(self):
        P, B = self.P, self.B
        xT = self.xT
        for th in range(2):
            bank = th
            for cq in range(8):
                sq = self.sqb[cq % 2]
                sqB = B["sqb%d" % (cq % 2)]
                P.op("act", lambda e, sq=sq, cq=cq, th=th: e.activation(
                    out=sq[:], in_=xT[:, 2 * cq:2 * cq + 2, th * 512:(th + 1) * 512], func=AF.Square),
                    reads=[B["xT%d" % th]], writes=[sqB])

                def fn(e, sq=sq, cq=cq, bank=bank):
                    for c in range(2):
                        ins = e.matmul(self.ps[bank][:], lhsT=self.ones[:], rhs=sq[:, c, :],
                                       start=(cq == 0 and c == 0), stop=(cq == 7 and c == 1))
                    return ins
                P.op("pe", fn, reads=[sqB, B["ones"]], writes=[self.Bps[bank]])
            tf = self.tmpf[2]
            P.op("act", lambda e, bank=bank, tf=tf: e.activation(out=tf[:], in_=self.ps[bank][:], func=AF.Sqrt,
                                                              bias=self.eps_col[:, 0:1], scale=1.0 / D),
                 reads=[self.Bps[bank], B["consts"]], writes=[B["tmpf2"]])
            P.op("dve", lambda e, th=th, tf=tf: e.reciprocal(out=self.rstd[:, th * 512:(th + 1) * 512], in_=tf[:]),
                 reads=[B["tmpf2"]], writes=[B["rstd%d" % th]])

    def modulate(self, gsc_slot, sh_ap_fn, sh_buf):
        P, B = self.P, self.B
        for th in range(2):
            for c in range(NCH):
                i = c % 2
                tf = self.tmpf[i]
                P.op("dve", lambda e, tf=tf, c=c, th=th: e.scalar_tensor_tensor(
                    out=tf[:], in0=self.xT[:, c, th * 512:(th + 1) * 512], scalar=self.mods[:, gsc_slot, c:c + 1],
                    in1=self.rstd[:, th * 512:(th + 1) * 512], op0=ALU.mult, op1=ALU.mult),
                    reads=[B["xT%d" % th], B["mods"], B["rstd%d" % th]], writes=[B["tmpf%d" % i]])
                P.op("act", lambda e, tf=tf, c=c, th=th: e.activation(
                    out=self.hT[:, c, th * 512:(th + 1) * 512], in_=tf[:], func=AF.Identity,
                    bias=sh_ap_fn(c), scale=1.0),
                    reads=[B["tmpf%d" % i], sh_buf], writes=[B["hT%d" % th]])

    def proj_residual(self, wname, rhs_fn, rhs_bufs, gate_ap_fn, gate_buf, after_unit=None):
        P, B = self.P, self.B
        n = 0
        for u in range(8):
            wv, wb = self.ws.take((wname, u))
            for fc in range(2):
                m = u * 2 + fc
                for th in range(2):
                    bank = n % 4
                    n += 1

                    def fn(e, wv=wv, fc=fc, th=th, bank=bank):
                        for k in range(NCH):
                            ins = e.matmul(self.ps[bank][:], lhsT=wv[:, k, fc * 128:(fc + 1) * 128],
                                           rhs=rhs_fn(k, th), start=(k == 0), stop=(k == NCH - 1))
                        return ins
                    P.op("pe", fn, reads=[wb] + rhs_bufs(th), writes=[self.Bps[bank]])
                    P.op("dve", lambda e, m=m, th=th, bank=bank: e.scalar_tensor_tensor(
                        out=self.xT[:, m, th * 512:(th + 1) * 512], in0=self.ps[bank][:], scalar=gate_ap_fn(m),
                        in1=self.xT[:, m, th * 512:(th + 1) * 512], op0=ALU.mult, op1=ALU.add),
                        reads=[self.Bps[bank], gate_buf, B["xT%d" % th]], writes=[B["xT%d" % th]])
            self.ws.release()
            if after_unit is not None:
                after_unit(u)

    def ffn(self, L, gate_ap_fn, gate_buf, between=None):
        P, B, nc = self.P, self.B, self.nc
        with ExitStack() as ph:
            aT = [ph.enter_context(self.sbt("aT%d" % i, [128, 4, T], BF16)) for i in range(2)]
            BaT = [[Buf("aT%d_%d" % (i, th)) for th in range(2)] for i in range(2)]
            sg = [ph.enter_context(self.sbt("sg%d" % i, [128, 512], F32)) for i in range(2)]
            Bsg = [Buf("sg0"), Buf("sg1")]
            ny = 0
            nsg = 0
            for grp in range(11):
                ab = grp % 2
                for qq in range(2):
                    q = grp * 2 + qq
                    wg, wgb = self.ws.take((L + "_wi", q))
                    wu, wub = self.ws.take((L + "_wi", 22 + q))
                    for fc in range(2):
                        jj = qq * 2 + fc
                        for th in range(2):
                            bg = th
                            bu = 2 + th

                            def fg(e, wg=wg, fc=fc, th=th, bg=bg):
                                for k in range(NCH):
                                    ins = e.matmul(self.ps[bg][:], lhsT=wg[:, k, fc * 128:(fc + 1) * 128],
                                                   rhs=self.hT[:, k, th * 512:(th + 1) * 512], start=(k == 0), stop=(k == NCH - 1))
                                return ins
                            P.op("pe", fg, reads=[wgb, B["hT%d" % th]], writes=[self.Bps[bg]])

                            def fu(e, wu=wu, fc=fc, th=th, bu=bu):
                                for k in range(NCH):
                                    ins = e.matmul(self.ps[bu][:], lhsT=wu[:, k, fc * 128:(fc + 1) * 128],
                                                   rhs=self.hT[:, k, th * 512:(th + 1) * 512], start=(k == 0), stop=(k == NCH - 1))
                                return ins
                            P.op("pe", fu, reads=[wub, B["hT%d" % th]], writes=[self.Bps[bu]])
                            si = nsg % 2
                            nsg += 1
                            P.op("act", lambda e, si=si, bg=bg: e.activation(out=sg[si][:], in_=self.ps[bg][:], func=AF.Silu),
                                 reads=[self.Bps[bg]], writes=[Bsg[si]])
                            P.op("dve", lambda e, si=si, bu=bu, ab=ab, jj=jj, th=th: e.tensor_tensor(
                                out=aT[ab][:, jj, th * 512:(th + 1) * 512], in0=self.ps[bu][:], in1=sg[si][:], op=ALU.mult),
                                reads=[self.Bps[bu], Bsg[si]], writes=[BaT[ab][th]])
                    self.ws.release()
                wo0, wob0 = self.ws.take((L + "_wo", grp * 2))
                wo1, wob1 = self.ws.take((L + "_wo", grp * 2 + 1))
                wos = (wo0, wo1)
                for m in range(NCH):
                    for th in range(2):
                        bank = 4 + ny % 2
                        ny += 1

                        def fy(e, m=m, th=th, bank=bank, wos=wos, ab=ab):
                            for jj in range(4):
                                ins = e.matmul(self.ps[bank][:], lhsT=wos[jj // 2][:, jj % 2, m * 128:(m + 1) * 128],
                                               rhs=aT[ab][:, jj, th * 512:(th + 1) * 512], start=(jj == 0), stop=(jj == 3))
                            return ins
                        P.op("pe", fy, reads=[wob0, wob1, BaT[ab][th]], writes=[self.Bps[bank]])
                        P.op("dve", lambda e, m=m, th=th, bank=bank: e.scalar_tensor_tensor(
                            out=self.xT[:, m, th * 512:(th + 1) * 512], in0=self.ps[bank][:], scalar=gate_ap_fn(m),
                            in1=self.xT[:, m, th * 512:(th + 1) * 512], op0=ALU.mult, op1=ALU.add),
                            reads=[self.Bps[bank], gate_buf, B["xT%d" % th]], writes=[B["xT%d" % th]])
                self.ws.release()
                if between is not None:
                    between(grp)
        P.barrier()

    def rope_tables(self, pos_d):
        P, B, nc = self.P, self.B, self.nc
        B["rope"] = Buf("rope")
        with ExitStack() as ph:
            posi = ph.enter_context(self.sbt("posi", [128, T], I32))
            ua = ph.enter_context(self.sbt("rope_u", [128, T], F32))
            ub = ph.enter_context(self.sbt("rope_k", [128, T], F32))
            Bp, Ba, Bb = Buf("posi"), Buf("ua"), Buf("ub")
            io = COLS["invf"][0]
            for off, dst in ((0.5, self.sinS), (0.75, self.cosF)):
                P.op("sp", lambda e: e.dma_start(out=posi[:], in_=pos_d.broadcast_to([128, T])), writes=[Bp], dma=P.dma_sem())
                P.op("dve", lambda e: e.tensor_copy(out=ua[:], in_=posi[:]), reads=[Bp], writes=[Ba])
                P.op("dve", lambda e, off=off: e.tensor_scalar(out=ua[:], in0=ua[:], scalar1=self.cols[:, io:io + 1], scalar2=off,
                                                               op0=ALU.mult, op1=ALU.add),
                     reads=[Ba, B["cols"]], writes=[Ba])
                P.op("dve", lambda e: e.tensor_copy(out=posi[:], in_=ua[:]), reads=[Ba], writes=[Bp])
                P.op("dve", lambda e: e.tensor_copy(out=ub[:], in_=posi[:]), reads=[Bp], writes=[Bb])
                P.op("dve", lambda e: e.tensor_tensor(out=ua[:], in0=ua[:], in1=ub[:], op=ALU.subtract),
                     reads=[Ba, Bb], writes=[Ba])
                P.op("dve", lambda e: e.tensor_single_scalar(out=ub[:], in_=ua[:], scalar=0.0, op=ALU.is_lt),
                     reads=[Ba], writes=[Bb])
                P.op("dve", lambda e: e.tensor_tensor(out=ua[:], in0=ua[:], in1=ub[:], op=ALU.add),
                     reads=[Ba, Bb], writes=[Ba])
                P.op("act", lambda e, dst=dst: e.activation(out=dst[:], in_=ua[:], func=AF.Sin, bias=self.eps_col[:, 1:2],
                                                            scale=6.2831845),
                     reads=[Ba, B["consts"]], writes=[B["rope"]])
            sinS_ = self.sinS
            P.op("dve", lambda e: e.tensor_scalar(out=sinS_[64:128, :], in0=sinS_[64:128, :], scalar1=-1.0, scalar2=None,
                                                  op0=ALU.mult),
                 reads=[B["rope"]], writes=[B["rope"]])
        P.barrier()

    def qk_head(self, ps_bank, th, gcol_ap, dst_ap, dst_buf):
        P, B = self.P, self.B
        ps = self.ps[ps_bank]
        bss = 2 + th
        sq = self.sqb[0]
        P.op("act", lambda e: e.activation(out=sq[:, 0, :], in_=ps[:], func=AF.Square),
             reads=[self.Bps[ps_bank]], writes=[B["sqb0"]])
        P.op("pe", lambda e: e.matmul(self.ps[bss][:], lhsT=self.ones[:], rhs=sq[:, 0, :], start=True, stop=True),
             reads=[B["sqb0"], B["ones"]], writes=[self.Bps[bss]])
        t0, t1, t2 = self.tmpf
        P.op("act", lambda e: e.activation(out=t2[:], in_=self.ps[bss][:], func=AF.Sqrt, bias=self.eps_col[:, 0:1],
                                           scale=1.0 / 128),
             reads=[self.Bps[bss], B["consts"]], writes=[B["tmpf2"]])
        P.op("dve", lambda e: e.reciprocal(out=t2[:], in_=t2[:]), reads=[B["tmpf2"]], writes=[B["tmpf2"]])
        qn = self.qn
        P.op("dve", lambda e: e.scalar_tensor_tensor(out=qn[:], in0=ps[:], scalar=gcol_ap, in1=t2[:],
                                                     op0=ALU.mult, op1=ALU.mult),
             reads=[self.Bps[ps_bank], B["tmpf2"], B["cols"]], writes=[B["qn"]])
        sl = slice(th * 512, (th + 1) * 512)
        cosF, sinS = self.cosF, self.sinS
        P.op("dve", lambda e: e.tensor_tensor(out=t0[:], in0=qn[:], in1=cosF[:, sl], op=ALU.mult),
             reads=[B["qn"], B["rope"]], writes=[B["tmpf0"]])
        P.op("dve", lambda e: e.tensor_tensor(out=t1[0:64, :], in0=qn[64:128, :], in1=sinS[64:128, sl], op=ALU.mult),
             reads=[B["qn"], B["rope"]], writes=[B["tmpf1"]])
        P.op("dve", lambda e: e.tensor_tensor(out=t1[64:128, :], in0=qn[0:64, :], in1=sinS[0:64, sl], op=ALU.mult),
             reads=[B["qn"], B["rope"]], writes=[B["tmpf1"]])
        P.op("dve", lambda e: e.tensor_tensor(out=dst_ap, in0=t0[:], in1=t1[:], op=ALU.add),
             reads=[B["tmpf0"], B["tmpf1"]], writes=[dst_buf])

    def layer_a(self, rows_d, sguw_d, pos_d, KT_d, V_d):
        P, B, nc, sb = self.P, self.B, self.nc, self.sb
        adaA, adaK = self.adaA, self.adaK
        stage(0)
        phA = ExitStack()
        wmT = phA.enter_context(self.sbt("wmT", [128, 16, 128], BF16))
        B2 = phA.enter_context(self.sbt("B2", [128, 16, 128], F32))
        inbv = phA.enter_context(self.sbt("inbv", [1, D], BF16))
        for n in ["wmT", "B2", "inbv"]:
            B[n] = Buf(n)
        P.op("pool", lambda e: e.dma_start(out=inbv[:], in_=rows_d[0:1, :]), writes=[B["inbv"]], dma=P.dma_sem())
        with ExitStack() as ph:
            sguw = ph.enter_context(self.sbt("sguw", [128, 16, 128], F32))
            sguwb = ph.enter_context(self.sbt("sguwb", [128, 16, 128], BF16))
            sgub = ph.enter_context(self.sbt("sgub", [128, 16, 128], F32))
            identb = ph.enter_context(self.sbt("identb0", [128, 128], BF16))
            tri = ph.enter_context(self.sbt("tri", [128, 128], BF16))
            Bsw, Bswb, Bsb, Bid, Btri = Buf("sguw"), Buf("sguwb"), Buf("sgub"), Buf("ident"), Buf("tri")
            P.op("sp", lambda e: e.dma_start(out=sguw[:], in_=sguw_d.rearrange("g t s -> t g s")), writes=[Bsw], dma=P.dma_sem())
            P.op("sp", lambda e: e.dma_start(out=sgub[:].rearrange("p g t -> p (g t)"), in_=rows_d[2:3, :].broadcast_to([128, D])),
                 writes=[Bsb], dma=P.dma_sem())
            P.op("pool", lambda e: e.memset(identb[:], 1.0), writes=[Bid])
            P.op("pool", lambda e: e.affine_select(out=identb[:], in_=identb[:], pattern=[[-1, 128]], compare_op=ALU.is_equal,
                                                   fill=0.0, base=0, channel_multiplier=1), reads=[Bid], writes=[Bid])
            P.op("pool", lambda e: e.memset(tri[:], 1.0), writes=[Btri])
            P.op("pool", lambda e: e.affine_select(out=tri[:], in_=tri[:], pattern=[[1, 128]], compare_op=ALU.is_ge,
                                                   fill=0.0, base=0, channel_multiplier=-1), reads=[Btri], writes=[Btri])
            P.op("dve", lambda e: e.tensor_copy(out=sguwb[:], in_=sguw[:]), reads=[Bsw], writes=[Bswb])
            for g4 in range(4):
                def ft(e, g4=g4):
                    for gi in range(4):
                        g = g4 * 4 + gi
                        ins = e.transpose(out=self.psT[:, gi * 128:(gi + 1) * 128], in_=sguwb[:, g, :], identity=identb[:])
                    return ins
                P.op("pe", ft, reads=[Bswb, Bid], writes=[self.BpsT])
                for gi in range(4):
                    g = g4 * 4 + gi
                    P.op("dve", lambda e, g=g, gi=gi: e.tensor_tensor(
                        out=wmT[:, g, :], in0=self.psT[:, gi * 128:(gi + 1) * 128], in1=tri[:], op=ALU.mult),
                        reads=[self.BpsT, Btri], writes=[B["wmT"]])
            lb = COLS["a_ln_b"][0]
            for q in range(4):
                bank = q % 2
                P.op("pe", lambda e, q=q, bank=bank: e.matmul(
                    self.ps[bank][:], lhsT=self.ones[:], rhs=wmT[:, 4 * q:4 * q + 4, :], start=True, stop=True),
                    reads=[B["wmT"], B["ones"]], writes=[self.Bps[bank]])
                for gi in range(4):
                    g = 4 * q + gi
                    P.op("dve", lambda e, g=g, gi=gi, bank=bank: e.scalar_tensor_tensor(
                        out=B2[:, g, :], in0=self.ps[bank][:, gi * 128:(gi + 1) * 128], scalar=self.cols[:, lb + g:lb + g + 1],
                        in1=sgub[:, g, :], op0=ALU.mult, op1=ALU.add),
                        reads=[self.Bps[bank], Bsb, B["cols"]], writes=[B["B2"]])
        P.barrier()
        stage(1)

        self.norm_stats()
        stage(2)
        self.ada_group("a_ada", 0, adaA, B["adaA"], "a_ada_b")
        self.ada_group("a_ada", 1, adaA, B["adaA"], "a_ada_b")
        self.mk_gsc(0, "a_norm1_g", adaA[:, 16:32], B["adaA"])
        self.modulate(0, lambda c: adaA[:, c:c + 1], B["adaA"])
        stage(3)

        with ExitStack() as ph:
            VS = ph.enter_context(self.sbt("VS", [128, 8, 16, 128], BF16))
            BVS = [Buf("VS%d" % tb) for tb in range(8)]
            st1 = ph.enter_context(self.sbt("st1", [128, 8, 8], F32))
            st2 = ph.enter_context(self.sbt("st2", [128, 8, 8], F32))
            stt_ = ph.enter_context(self.sbt("stt", [128, 8, 4], F32))
            junk = ph.enter_context(self.sbt("junk", [128, 256], BF16))
            ug = [ph.enter_context(self.sbt("ug%d" % i, [128, 512], BF16)) for i in range(2)]
            Bst = [Buf("st%d" % tb) for tb in range(8)]
            Bjunk = Buf("junk")
            Bug = [Buf("ug0"), Buf("ug1")]
            P.op("dve", lambda e: e.memset(st1[:], 0.0), writes=Bst)
            P.op("dve", lambda e: e.memset(st2[:], 0.0), writes=Bst)
            nb = 0
            for cb in range(8):
                wv, wb = self.ws.take(("a_in", 8 + cb))
                for tb in range(8):
                    bank = nb % 4
                    nb += 1

                    def fv(e, wv=wv, tb=tb, bank=bank, cb=cb):
                        o = self.ps[bank][:, 0:256]
                        for k in range(NCH):
                            e.matmul(o, lhsT=self.hT[:, k, tb * 128:(tb + 1) * 128], rhs=wv[:, k, :], start=(k == 0), stop=False)
                        return e.matmul(o, lhsT=self.ones[0:1, :], rhs=inbv[0:1, cb * 256:(cb + 1) * 256], start=False, stop=True)
                    P.op("pe", fv, reads=[wb, B["hT%d" % (tb // 4)], B["inbv"], B["ones"]], writes=[self.Bps[bank]])
                    vdst = VS[:, tb, 2 * cb:2 * cb + 2, :]
                    P.op("act", lambda e, vdst=vdst, bank=bank, tb=tb, cb=cb: e.activation(
                        out=vdst, in_=self.ps[bank][:, 0:256].rearrange("p (a b) -> p a b", a=2), func=AF.Gelu,
                        accum_out=st1[:, tb, cb:cb + 1]),
                        reads=[self.Bps[bank]], writes=[BVS[tb], Bst[tb]])
                    P.op("act", lambda e, vdst=vdst, tb=tb, cb=cb: e.activation(
                        out=junk[:].rearrange("p (a b) -> p a b", a=2), in_=vdst, func=AF.Square,
                        accum_out=st2[:, tb, cb:cb + 1]),
                        reads=[BVS[tb]], writes=[Bjunk, Bst[tb]])
                self.ws.release()
                self.ada_unit("a_ada", 16 + 2 * cb, adaA, B["adaA"], "a_ada_b")
                self.ada_unit("a_ada", 17 + 2 * cb, adaA, B["adaA"], "a_ada_b")
            stage(4)
            X = mybir.AxisListType.X
            for tb in range(8):
                P.chain("dve", [
                    lambda e, tb=tb: e.tensor_reduce(out=stt_[:, tb, 0:1], in_=st1[:, tb, :], axis=X, op=ALU.add),
                    lambda e, tb=tb: e.tensor_reduce(out=stt_[:, tb, 1:2], in_=st2[:, tb, :], axis=X, op=ALU.add),
                    lambda e, tb=tb: e.tensor_scalar(out=stt_[:, tb, 0:1], in0=stt_[:, tb, 0:1], scalar1=1.0 / D, scalar2=None, op0=ALU.mult),
                    lambda e, tb=tb: e.tensor_tensor(out=stt_[:, tb, 2:3], in0=stt_[:, tb, 0:1], in1=stt_[:, tb, 0:1], op=ALU.mult),
                    lambda e, tb=tb: e.scalar_tensor_tensor(out=stt_[:, tb, 1:2], in0=stt_[:, tb, 1:2], scalar=1.0 / D,
                                                            in1=stt_[:, tb, 2:3], op0=ALU.mult, op1=ALU.subtract),
                ], reads=[Bst[tb]], writes=[Bst[tb]])
                P.op("act", lambda e, tb=tb: e.activation(out=stt_[:, tb, 2:3], in_=stt_[:, tb, 1:2], func=AF.Sqrt,
                                                          bias=self.eps_col[:, 0:1], scale=1.0),
                     reads=[Bst[tb], B["consts"]], writes=[Bst[tb]])
                P.chain("dve", [
                    lambda e, tb=tb: e.reciprocal(out=stt_[:, tb, 2:3], in_=stt_[:, tb, 2:3]),
                    lambda e, tb=tb: e.scalar_tensor_tensor(out=stt_[:, tb, 3:4], in0=stt_[:, tb, 0:1], scalar=-1.0,
                                                            in1=stt_[:, tb, 2:3], op0=ALU.mult, op1=ALU.mult),
                ], reads=[Bst[tb]], writes=[Bst[tb]])
                P.op("dve", lambda e, tb=tb: e.tensor_scalar(
                    out=VS[:, tb, :, :], in0=VS[:, tb, :, :], scalar1=stt_[:, tb, 2:3], scalar2=stt_[:, tb, 3:4],
                    op0=ALU.mult, op1=ALU.add),
                    reads=[BVS[tb], Bst[tb]], writes=[BVS[tb]])
            lg = COLS["a_ln_g"][0]
            ns = 0
            for tb in range(8):
                for g4 in range(4):
                    bank = 4 + ns % 2
                    ns += 1

                    def fsg(e, tb=tb, g4=g4, bank=bank):
                        for gi in range(4):
                            g = g4 * 4 + gi
                            ins = e.matmul(self.ps[bank][:, gi * 128:(gi + 1) * 128], lhsT=VS[:, tb, g, :], rhs=wmT[:, g, :],
                                           start=True, stop=True)
                        return ins
                    P.op("pe", fsg, reads=[BVS[tb], B["wmT"]], writes=[self.Bps[bank]])
                    for gi in range(4):
                        g = g4 * 4 + gi
                        P.op("dve", lambda e, tb=tb, g=g, gi=gi, bank=bank: e.scalar_tensor_tensor(
                            out=VS[:, tb, g, :], in0=self.ps[bank][:, gi * 128:(gi + 1) * 128],
                            scalar=self.cols[:, lg + g:lg + g + 1], in1=B2[:, g, :], op0=ALU.mult, op1=ALU.add),
                            reads=[self.Bps[bank], B["B2"], B["cols"]], writes=[BVS[tb]])
            stage(5)
            bo = COLS["a_in_b_u"][0]
            nu = 0
            for u in range(8):
                wv, wb = self.ws.take(("a_in", u))
                for fc in range(2):
                    g = u * 2 + fc
                    for th in range(2):
                        bank = nu % 4
                        i = nu % 2
                        nu += 1

                        def fu(e, wv=wv, fc=fc, th=th, bank=bank):
                            for k in range(NCH):
                                ins = e.matmul(self.ps[bank][:], lhsT=wv[:, k, fc * 128:(fc + 1) * 128],
                                               rhs=self.hT[:, k, th * 512:(th + 1) * 512], start=(k == 0), stop=(k == NCH - 1))
                            return ins
                        P.op("pe", fu, reads=[wb, B["hT%d" % th]], writes=[self.Bps[bank]])
                        P.op("act", lambda e, i=i, bank=bank, g=g: e.activation(
                            out=ug[i][:], in_=self.ps[bank][:], func=AF.Gelu, bias=self.cols[:, bo + g:bo + g + 1], scale=1.0),
                            reads=[self.Bps[bank], B["cols"]], writes=[Bug[i]])
                        P.op("dve", lambda e, i=i, g=g, th=th: e.tensor_tensor(
                            out=VS[:, 4 * th:4 * th + 4, g, :], in0=VS[:, 4 * th:4 * th + 4, g, :],
                            in1=ug[i][:].rearrange("p (a b) -> p a b", a=4), op=ALU.mult),
                            reads=[Bug[i]] + BVS[4 * th:4 * th + 4], writes=BVS[4 * th:4 * th + 4])
                self.ws.release()
                self.ada_unit("a_ada", 32 + u, adaA, B["adaA"], "a_ada_b")
            self.proj_residual("a_out", lambda k, th: VS[:, 4 * th:4 * th + 4, k, :],
                               lambda th: BVS[4 * th:4 * th + 4], lambda m: adaA[:, 32 + m:33 + m], B["adaA"],
                               after_unit=lambda u: self.ada_unit("a_ada", 40 + u, adaA, B["adaA"], "a_ada_b"))
        phA.close()
        P.barrier()
        stage(6)
        self.norm_stats()
        self.mk_gsc(1, "a_norm2_g", adaA[:, 64:80], B["adaA"])
        self.modulate(1, lambda c: adaA[:, 48 + c:49 + c], B["adaA"])
        self.ffn("a", lambda m: adaA[:, 80 + m:81 + m], B["adaA"],
                 between=lambda grp: (self.ada_unit("kv_ada", 2 * grp, adaK, B["adaK"], "kv_ada_b"),
                                      self.ada_unit("kv_ada", 2 * grp + 1, adaK, B["adaK"], "kv_ada_b")) if grp < 8 else None)
        stage(7)
        self.norm_stats()
        self.mk_gsc(2, "kv_norm_g", adaK[:, 16:32], B["adaK"])
        self.modulate(2, lambda c: adaK[:, c:c + 1], B["adaK"])
        with ExitStack() as ph:
            self.cosF = ph.enter_context(self.sbt("cosF", [128, T], F32))
            self.sinS = ph.enter_context(self.sbt("sinS", [128, T], F32))
            self.rope_tables(pos_d)
            self.qn = ph.enter_context(self.sbt("qn", [128, 512], F32))
            B["qn"] = Buf("qn")
            kt = [ph.enter_context(self.sbt("kt%d" % i, [128, 512], BF16)) for i in range(2)]
            Bkt = [Buf("kt0"), Buf("kt1")]
            vt = [ph.enter_context(self.sbt("vt%d" % i, [128, 256], BF16)) for i in range(2)]
            Bvt = [Buf("vt0"), Buf("vt1")]
            kg = self.col("k_norm_g")
            dks = [P.dma_sem(), P.dma_sem()]
            dvs = [P.dma_sem(), P.dma_sem()]
            self.kv_out_sems = dks + dvs
            nk = 0
            for u in range(8):
                wv, wb = self.ws.take(("kv", u))
                for fc in range(2):
                    hh = u * 2 + fc
                    for th in range(2):
                        bank = th
                        i = nk % 2
                        nk += 1

                        def fk(e, wv=wv, fc=fc, th=th, bank=bank):
                            for k in range(NCH):
                                ins = e.matmul(self.ps[bank][:], lhsT=wv[:, k, fc * 128:(fc + 1) * 128],
                                               rhs=self.hT[:, k, th * 512:(th + 1) * 512], start=(k == 0), stop=(k == NCH - 1))
                            return ins
                        P.op("pe", fk, reads=[wb, B["hT%d" % th]], writes=[self.Bps[bank]])
                        self.qk_head(bank, th, kg, kt[i][:], Bkt[i])
                        P.op("sp", lambda e, i=i, hh=hh, th=th: e.dma_start(out=KT_d[hh][:, th * 512:(th + 1) * 512], in_=kt[i][:]),
                             reads=[Bkt[i]], dma=dks[i])
                self.ws.release()
                if self.mode == "F":
                    self.ada_unit("b_ada", u, adaA, B["adaA"], "b_ada_b")
                if self.mode == "F":
                    P.op("pool", lambda e, u=u: e.collective_compute("AllGather", ALU.bypass, replica_groups=[[0, 1, 2, 3], [4, 5, 6, 7]],
                                                                     ins=[self.kt_loc[u]], outs=[self.kt_all[u]]),
                         extra_waits=[(d_, P.cnt[d_]) for d_ in dks], writes=[self.Bktall[u]], dma=P.dma_sem(), dma_inc=1)
            nv = 0
            for u in range(8):
                wv, wb = self.ws.take(("kv", 8 + u))
                for tb in range(8):
                    bank = 4 + nv % 2
                    i = nv % 2
                    nv += 1

                    def fvv(e, wv=wv, tb=tb, bank=bank):
                        for k in range(NCH):
                            ins = e.matmul(self.ps[bank][:, 0:256], lhsT=self.hT[:, k, tb * 128:(tb + 1) * 128], rhs=wv[:, k, :],
                                           start=(k == 0), stop=(k == NCH - 1))
                        return ins
                    P.op("pe", fvv, reads=[wb, B["hT%d" % (tb // 4)]], writes=[self.Bps[bank]])
                    P.op("act", lambda e, i=i, bank=bank: e.activation(out=vt[i][:], in_=self.ps[bank][:, 0:256], func=AF.Copy),
                         reads=[self.Bps[bank]], writes=[Bvt[i]])
                    P.op("sp", lambda e, i=i, u=u, tb=tb: e.dma_start(out=V_d[u][tb * 128:(tb + 1) * 128, :], in_=vt[i][:]),
                         reads=[Bvt[i]], dma=dvs[i])
                self.ws.release()
                if self.mode == "F":
                    self.ada_unit("b_ada", 8 + u, adaA, B["adaA"], "b_ada_b")
                if self.mode == "F":
                    P.op("pool", lambda e, u=u: e.collective_compute("AllGather", ALU.bypass, replica_groups=[[0, 1, 2, 3], [4, 5, 6, 7]],
                                                                     ins=[self.v_loc[u]], outs=[self.v_all[u]]),
                         extra_waits=[(d_, P.cnt[d_]) for d_ in dvs], writes=[self.Bvall[u]], dma=P.dma_sem(), dma_inc=1)
            if self.mode == "F":
                Bdl = Buf("dmloc")
                P.op("sp", lambda e: e.dma_start(out=self.dm_loc, in_=self.ones[:]), reads=[B["ones"]], writes=[Bdl], dma=P.dma_sem())
                P.op("pool", lambda e: e.collective_compute("AllGather", ALU.bypass, replica_groups=[[0, 1, 2, 3], [4, 5, 6, 7]],
                                                            ins=[self.dm_loc], outs=[self.dm_all]),
                     reads=[Bdl], writes=[self.Bdummy], dma=P.dma_sem(), dma_inc=1)
        P.barrier()

    def layer_b(self, KTf_d, Vf_d, mask_d, pos_d):
        P, B, nc, sb = self.P, self.B, self.nc, self.sb
        adaA = self.adaA
        QT = sb("QT", [128, NCH, T], BF16)
        BQ = [[Buf("QT%d_%d" % (hd, m)) for m in range(4)] for hd in range(8)]
        mask = sb("mask", [128, 8, 128], BF16)
        lamt = sb("lamt", [128, 8], F32)
        lamb = sb("lamb", [128, 2], BF16)
        B["mask"] = Buf("mask")
        B["lamt"] = Buf("lamt")
        P.op("pool", lambda e: e.dma_start(out=mask[:], in_=mask_d), writes=[B["mask"]], dma=P.dma_sem())
        stage(10)
        lo = COLS["lam"][0]
        so = COLS["subln_g"][0]
        P.op("dve", lambda e: e.tensor_tensor(out=lamb[:, 0:1], in0=self.cols[:, lo:lo + 1], in1=self.cols[:, lo + 1:lo + 2], op=ALU.mult),
             reads=[B["cols"]], writes=[B["lamt"]])
        P.op("dve", lambda e: e.tensor_tensor(out=lamb[:, 1:2], in0=self.cols[:, lo + 2:lo + 3], in1=self.cols[:, lo + 3:lo + 4], op=ALU.mult),
             reads=[B["cols"]], writes=[B["lamt"]])
        P.op("pe", lambda e: e.matmul(self.ps[5][:, 0:2], lhsT=self.ones[:], rhs=lamb[:, 0:2], start=True, stop=True),
             reads=[B["lamt"], B["ones"]], writes=[self.Bps[5]])
        P.op("act", lambda e: e.activation(out=lamt[:, 2:4], in_=self.ps[5][:, 0:2], func=AF.Exp),
             reads=[self.Bps[5]], writes=[B["lamt"]])
        P.chain("dve", [
            lambda e: e.scalar_tensor_tensor(out=lamt[:, 4:5], in0=lamt[:, 2:3], scalar=LAM_INIT, in1=lamt[:, 3:4],
                                             op0=ALU.add, op1=ALU.subtract),
            lambda e: e.tensor_scalar(out=lamt[:, 5:6], in0=lamt[:, 4:5], scalar1=-1.0, scalar2=None, op0=ALU.mult),
            lambda e: e.tensor_scalar(out=lamt[:, 6:8], in0=self.cols[:, so:so + 2], scalar1=1.0 - LAM_INIT, scalar2=None, op0=ALU.mult),
        ], reads=[B["lamt"], B["cols"]], writes=[B["lamt"]])

        stage(11)
        self.norm_stats()
        if self.mode == "B":
            self.ada_group("b_ada", 0, adaA, B["adaA"], "b_ada_b")
            self.ada_group("b_ada", 1, adaA, B["adaA"], "b_ada_b")
        self.mk_gsc(0, "b_norm1_g", adaA[:, 16:32], B["adaA"])
        self.modulate(0, lambda c: adaA[:, c:c + 1], B["adaA"])
        stage(12)
        with ExitStack() as ph:
            self.cosF = ph.enter_context(self.sbt("cosF", [128, T], F32))
            self.sinS = ph.enter_context(self.sbt("sinS", [128, T], F32))
            self.rope_tables(pos_d)
            self.qn = ph.enter_context(self.sbt("qn", [128, 512], F32))
            B["qn"] = Buf("qn")
            qg = self.col("q_norm_g")
            for u in range(8):
                wv, wb = self.ws.take(("b_q", u))
                for fc in range(2):
                    hh = u * 2 + fc
                    for th in range(2):
                        bank = th

                        def fk(e, wv=wv, fc=fc, th=th, bank=bank):
                            for k in range(NCH):
                                ins = e.matmul(self.ps[bank][:], lhsT=wv[:, k, fc * 128:(fc + 1) * 128],
                                               rhs=self.hT[:, k, th * 512:(th + 1) * 512], start=(k == 0), stop=(k == NCH - 1))
                            return ins
                        P.op("pe", fk, reads=[wb, B["hT%d" % th]], writes=[self.Bps[bank]])
                        self.qk_head(bank, th, qg, QT[:, hh, th * 512:(th + 1) * 512], BQ[u][2 * th])
                        BQ[u][2 * th + 1].w = BQ[u][2 * th].w
                self.ws.release()
        P.barrier()
        stage(13)

        with ExitStack() as ph:
            ex = [ph.enter_context(self.sbt("kvx%d" % i, [128, 4096], BF16)) for i in range(1)]
            slots = [self.hT[:, 4 * i:4 * i + 4, :].rearrange("p c t -> p (c t)") for i in range(4)] + [e_[:] for e_ in ex]
            NS = len(slots)
            sbufs = [Buf("kvs%d" % i) for i in range(NS)]
            ssems = [P.dma_sem() for _ in range(NS)]
            plan = []
            fused = self.mode == "F"
            for hd in range(8):
                if fused:
                    plan.append(("ktf", KTf_d[2 * hd], hd))
                    plan.append(("ktf", KTf_d[2 * hd + 1], hd))
                    plan.append(("vf", (Vf_d[hd][0], Vf_d[hd][1]), hd))
                    plan.append(("vf", (Vf_d[hd][2], Vf_d[hd][3]), hd))
                else:
                    plan.append(("kt", KTf_d[2 * hd], hd))
                    plan.append(("kt", KTf_d[2 * hd + 1], hd))
                    plan.append(("v", Vf_d[hd, 0], hd))
                    plan.append(("v", Vf_d[hd, 1], hd))

            def kmap(kb):
                if not fused:
                    return kb * 128, kb // 16, kb % 16
                m8, r8 = kb // 8, kb % 8
                if r8 < 4:
                    j, sl_ = r8, 2 * m8
                else:
                    j, sl_ = 7 - r8, 2 * m8 + 1
                return j * 1024 + sl_ * 128, j // 2, (j % 2) * 8 + sl_
            st = {"issued": 0, "closed": 0}

            def kview(s, kind):
                if kind == "kt":
                    return slots[s]
                if kind == "ktf":
                    return slots[s].rearrange("p (j t) -> p j t", j=4)
                if kind == "vf":
                    return slots[s].rearrange("p (j s f) -> p j s f", j=2, s=8)
                return slots[s].rearrange("p (k f) -> p k f", k=16)

            def pump():
                while st["issued"] < len(plan) and st["issued"] - NS < st["closed"]:
                    i = st["issued"]
                    kind, src, phd = plan[i]
                    s = i % NS
                    dst = kview(s, kind)
                    if kind == "vf":
                        for jj in range(2):
                            P.op("sp", (lambda dst, src: (lambda e: e.dma_start(out=dst, in_=src)))(dst[:, jj], src[jj]),
                                 reads=[self.Bvall[phd], self.Bdummy], writes=[sbufs[s]], dma=ssems[s])
                    else:
                        P.op("sp", (lambda dst, src: (lambda e: e.dma_start(out=dst, in_=src)))(dst, src),
                             reads=([self.Bktall[phd], self.Bdummy] if kind == "ktf" else []),
                             writes=[sbufs[s]], dma=ssems[s])
                    st["issued"] += 1

            pT = [self.sqb[0][:, 0, :], self.sqb[0][:, 1, :], self.sqb[1][:, 0, :], self.sqb[1][:, 1, :]]
            BpT = [Buf("pT%d" % i) for i in range(4)]
            accs = [ph.enter_context(self.sbt("accs%d" % i, [128, 260], F32)) for i in range(4)]
            Baccs = [Buf("accs%d" % i) for i in range(4)]
            fin = ph.enter_context(self.sbt("fin", [128, 8], F32))
            Bfin = Buf("fin")
            onbs = [ph.enter_context(self.sbt("onb%d" % i, [128, 256], BF16)) for i in range(2)]
            Bonbs = [Buf("onb0"), Buf("onb1")]
            identb = ph.enter_context(self.sbt("identb", [128, 128], BF16))
            Bidb = Buf("identb")
            P.op("pool", lambda e: e.memset(identb[:], 1.0), writes=[Bidb])
            P.op("pool", lambda e: e.affine_select(out=identb[:], in_=identb[:], pattern=[[-1, 128]], compare_op=ALU.is_equal,
                                                   fill=0.0, base=0, channel_multiplier=1), reads=[Bidb], writes=[Bidb])
            t0, t1, t2 = self.tmpf
            scale = 128.0 ** -0.5
            npt = 0
            nst = 0
            ada_next = 16
            for hd in range(8):
                base = hd * 4
                pump()
                if hd == 0 and self.DBG:
                    if fused:
                        P.op("sp", lambda e: e.dma_start(out=self.dbgA[1], in_=self.kt_all[0]), reads=[self.Bktall[0]], dma=self.do)
                    P.op("sp", lambda e: e.dma_start(out=self.dbgK, in_=slots[0]), reads=[sbufs[0]], dma=self.do)
                    P.op("sp", lambda e: e.dma_start(out=self.dbgV, in_=slots[2]), reads=[sbufs[2]], dma=self.do)
                kts = [kview((base + i) % NS, "kt") for i in range(2)]
                vs = [kview((base + 2 + i) % NS, "v") for i in range(2)]
                kb_ = [sbufs[(base + i) % NS] for i in range(4)]
                steps = [(m, kb) for m in range(4) for kb in range(8 * m + 8)]
                info = {}
                SB = (0, 1, 6)

                def emit_S(idx, hd=hd, kts=kts, kb_=kb_):
                    nonlocal nst, npt
                    m, kb = steps[idx]
                    kc, vh, vb = kmap(kb)
                    full = kb < 8 * m + 4
                    off = 0 if full else 128
                    sbank = SB[nst % 3]
                    nst += 1
                    pi = npt % 4
                    npt += 1
                    info[idx] = (pi, full, vh, vb)
                    psS = self.ps[sbank]

                    def fs(e):
                        for sub in range(2):
                            ins = e.matmul(psS[:, sub * 256 + off:sub * 256 + 256],
                                           lhsT=kts[sub][:, kc:kc + 128],
                                           rhs=QT[:, 2 * hd + sub, 256 * m + off:256 * m + 256], start=True, stop=True)
                        return ins
                    P.op("pe", fs, reads=[kb_[0], kb_[1], BQ[hd][m]], writes=[self.Bps[sbank]])
                    src3 = psS[:].rearrange("p (s q) -> p s q", s=2)[:, :, off:256]
                    dst3 = pT[pi].rearrange("p (s q) -> p s q", s=2)[:, :, off:256]
                    P.op("act", lambda e: e.activation(out=dst3, in_=src3, func=AF.Exp, scale=scale),
                         reads=[self.Bps[sbank]], writes=[BpT[pi]])
                    if kb >= 8 * m:
                        par = 0 if full else 1
                        r = kb - 8 * m - 4 * par
                        moff = 0 if par == 0 else 128
                        for sub in range(2):
                            dm = pT[pi][:, sub * 256 + moff:sub * 256 + moff + 128]
                            P.op("dve", lambda e, dm=dm: e.tensor_tensor(
                                out=dm, in0=dm, in1=mask[:, par * 4 + r, :], op=ALU.mult),
                                reads=[BpT[pi], B["mask"]], writes=[BpT[pi]])

                def emit_PV(idx, hd=hd, vs=vs, kb_=kb_):
                    m, kb = steps[idx]
                    pi, full, vh, vb = info[idx]
                    vv = vs[vh][:, vb, :]

                    def fpv(e):
                        for sl in ((0, 1) if full else (1,)):
                            last = (8 * m + 3) if sl == 0 else (8 * m + 7)
                            for sub in range(2):
                                acc = self.ps[2 + sl * 2 + sub]
                                lh = pT[pi][:, sub * 256 + sl * 128:sub * 256 + sl * 128 + 128]
                                e.matmul(acc[:, 0:256], lhsT=lh, rhs=vv, start=(kb == 0), stop=(kb == last),
                                         skip_group_check=True)
                                ins = e.matmul(acc[:, 256:257], lhsT=lh, rhs=self.ones[:, 0:1], start=False, stop=(kb == last),
                                               skip_group_check=True)
                        return ins
                    P.op("pe", fpv, reads=[BpT[pi], kb_[2 + vh], B["ones"]],
                         writes=[self.Bps[2 + sl * 2 + sub] for sl in ((0, 1) if full else (1,)) for sub in range(2)])
                    if kb == 8 * m + 7:
                        finalize(m)
                        pending.append((idx + 6, m))

                def finalize(m, hd=hd):
                    for a in range(4):
                        P.op("act", lambda e, a=a: e.activation(out=accs[a][:, 0:257], in_=self.ps[2 + a][:, 0:257], func=AF.Copy),
                             reads=[self.Bps[2 + a]], writes=[Baccs[a]])
                    for sl in range(2):
                        aA, aB = accs[2 * sl], accs[2 * sl + 1]
                        onb, Bonb = onbs[sl], Bonbs[sl]
                        P.chain("dve", [
                            lambda e, aA=aA: e.reciprocal(out=fin[:, 0:1], in_=aA[:, 256:257]),
                            lambda e, aB=aB: e.reciprocal(out=fin[:, 1:2], in_=aB[:, 256:257]),
                            lambda e: e.tensor_tensor(out=fin[:, 2:3], in0=fin[:, 1:2], in1=lamt[:, 5:6], op=ALU.mult),
                        ], reads=[Baccs[2 * sl], Baccs[2 * sl + 1], Bfin, B["lamt"]], writes=[Bfin])
                        P.op("dve", lambda e, aB=aB: e.tensor_scalar(out=t0[:, 0:256], in0=aB[:, 0:256], scalar1=fin[:, 2:3], scalar2=None,
                                                                     op0=ALU.mult),
                             reads=[Baccs[2 * sl + 1], Bfin], writes=[B["tmpf0"]])
                        P.op("dve", lambda e, aA=aA: e.scalar_tensor_tensor(out=t1[:, 0:256], in0=aA[:, 0:256], scalar=fin[:, 0:1],
                                                                            in1=t0[:, 0:256], op0=ALU.mult, op1=ALU.add),
                             reads=[Baccs[2 * sl], Bfin, B["tmpf0"]], writes=[B["tmpf1"]])
                        P.op("act", lambda e: e.activation(out=t2[:, 0:256], in_=t1[:, 0:256], func=AF.Square, accum_out=fin[:, 3:4]),
                             reads=[B["tmpf1"], Bfin], writes=[B["tmpf2"], Bfin])
                        P.op("act", lambda e: e.activation(out=fin[:, 4:5], in_=fin[:, 3:4], func=AF.Sqrt, bias=self.eps_col[:, 0:1],
                                                           scale=1.0 / 256),
                             reads=[Bfin, B["consts"]], writes=[Bfin])
                        P.op("dve", lambda e: e.reciprocal(out=fin[:, 5:6], in_=fin[:, 4:5]), reads=[Bfin], writes=[Bfin])
                        P.op("dve", lambda e, onb=onb: e.tensor_scalar(out=onb[:], in0=t1[:, 0:256], scalar1=fin[:, 5:6], scalar2=None, op0=ALU.mult),
                             reads=[B["tmpf1"], Bfin], writes=[Bonb])

                def finalize_b(m, hd=hd):
                    def ftr(e):
                        for sl in range(2):
                            for c2 in range(2):
                                ins = e.transpose(out=self.psT[:, sl * 256 + c2 * 128:sl * 256 + (c2 + 1) * 128],
                                                  in_=onbs[sl][:, c2 * 128:(c2 + 1) * 128], identity=identb[:])
                        return ins
                    P.op("pe", ftr, reads=[Bonbs[0], Bonbs[1], Bidb], writes=[self.BpsT])
                    for c2 in range(2):
                        P.op("dve", lambda e, c2=c2: e.tensor_scalar(
                            out=QT[:, 2 * hd + c2, 256 * m:256 * m + 256].rearrange("p (s q) -> p s q", s=2),
                            in0=self.psT[:, 0:512].rearrange("p (s c q) -> p s c q", s=2, c=2)[:, :, c2, :],
                            scalar1=lamt[:, 6 + c2:7 + c2], scalar2=None, op0=ALU.mult),
                            reads=[self.BpsT, B["lamt"]], writes=[BQ[hd][m]])

                LA = 2
                pending = []
                for idx in range(len(steps) + LA):
                    if idx < len(steps):
                        emit_S(idx)
                    if idx >= LA:
                        emit_PV(idx - LA)
                    while pending and pending[0][0] <= idx - LA:
                        finalize_b(pending.pop(0)[1])
                while pending:
                    finalize_b(pending.pop(0)[1])
                st["closed"] = base + 4
                if STOP <= 14 + hd:
                    ada_next = 48
                    break
                for _ in range(4):
                    self.ada_unit("b_ada", ada_next, adaA, B["adaA"], "b_ada_b")
                    ada_next += 1
            assert ada_next == 48
        P.barrier()
        stage(21)
        allQ = [BQ[hd][m] for hd in range(8) for m in range(4)]
        self.proj_residual("b_o", lambda k, th: QT[:, k, th * 512:(th + 1) * 512], lambda th: allQ,
                           lambda m: adaA[:, 32 + m:33 + m], B["adaA"])
        stage(22)
        self.norm_stats()
        self.mk_gsc(1, "b_norm2_g", adaA[:, 64:80], B["adaA"])
        self.modulate(1, lambda c: adaA[:, 48 + c:49 + c], B["adaA"])
        self.ffn("b", lambda m: adaA[:, 80 + m:81 + m], B["adaA"])


_NC_CACHE = {}


def get_nc(mode):
    if mode not in _NC_CACHE:
        _NC_CACHE[mode] = Builder(mode).build()
    return _NC_CACHE[mode]


def host_cols(inp, b):
    cols = np.zeros((128, NCOL), np.float32)

    def put(name, arr):
        o, w = COLS[name]
        cols[:, o:o + w] = arr
    put("a_norm1_g", colv(inp["a_norm1_g"][0]))
    put("a_norm2_g", colv(inp["a_norm2_g"][0]))
    put("a_in_b_u", colv(inp["a_in_b"][0][:D]))
    put("a_ln_g", colv(inp["a_sgu_ln_g"][0]))
    put("a_ln_b", colv(inp["a_sgu_ln_b"][0]))
    put("kv_norm_g", colv(inp["kv_norm_g"]))
    put("b_norm1_g", colv(inp["b_norm1_g"][0]))
    put("b_norm2_g", colv(inp["b_norm2_g"][0]))
    put("a_ada_b", colv(inp["a_ada_b"][0]))
    put("kv_ada_b", colv(inp["kv_ada_b"]))
    put("b_ada_b", colv(inp["b_ada_b"][0]))
    put("k_norm_g", colv(inp["k_norm_g"]))
    put("q_norm_g", colv(inp["b_q_norm_g"][0]))
    put("subln_g", colv(inp["b_subln_g"][0]))
    put("lam", np.stack([inp["b_lambda_q1"][0], inp["b_lambda_k1"][0], inp["b_lambda_q2"][0], inp["b_lambda_k2"][0]], 1))
    inv_freq = 1.0 / (10000.0 ** (np.arange(0, 128, 2, dtype=np.float32) / 128.0))
    put("invf", (np.concatenate([inv_freq, inv_freq]) / (2 * np.pi)).astype(np.float32)[:, None])
    put("cT", colv(inp["c"][b]))
    return cols


def host_mask(j):
    m = np.zeros((128, 8, 128), np.float32)
    tri = (np.arange(128)[:, None] <= np.arange(128)[None, :]).astype(np.float32)
    for par in range(2):
        p = j if par == 0 else 7 - j
        for r in range(4):
            kbk = r if par == 0 else 4 + r
            if kbk < p:
                m[:, par * 4 + r, :] = 1.0
            elif kbk == p:
                m[:, par * 4 + r, :] = tri
    return m


def run_a(inp, ncores=8):
    nc = get_nc("A")
    shared = {
        "a_ada_w": tile_w(inp["a_ada_w"][0]), "a_in_w": tile_w(inp["a_in_w"][0]), "a_out_w": tile_w(inp["a_out_w"][0]),
        "a_ffn_wi": tile_w(inp["a_ffn_wi"][0]), "a_ffn_wo": tile_wo(inp["a_ffn_wo"][0]),
        "kv_ada_w": tile_w(inp["kv_ada_w"]), "kv_w": tile_w(inp["kv_w"]),
        "sgu_w": np.ascontiguousarray(inp["a_sgu_w"][0]),
        "rows": np.ascontiguousarray(np.stack([inp["a_in_b"][0][D:], inp["a_sgu_ln_b"][0], inp["a_sgu_b"][0].reshape(-1)]).astype(np.float32)),
    }
    in_maps = []
    for i in range(8):
        b, j = i // 4, i % 4
        idx = tok_index(j)
        m = dict(shared)
        m["xT"] = np.ascontiguousarray(inp["x"][b][idx].T)
        m["pos"] = np.ascontiguousarray(inp["positions"][b][idx][None, :]).astype(np.int32)
        m["cols"] = host_cols(inp, b)
        in_maps.append(m)
    in_maps = in_maps[:ncores]
    res = run_bass_kernel_spmd(nc, in_maps, core_ids=list(range(ncores)))
    return res.results


def run_b(inp, ra, ncores=8):
    nc = get_nc("B")
    shared = {
        "b_ada_w": tile_w(inp["b_ada_w"][0]), "b_q_w": tile_w(inp["b_q_w"][0]), "b_o_w": tile_w(inp["b_o_w"][0]),
        "b_ffn_wi": tile_w(inp["b_ffn_wi"][0]), "b_ffn_wo": tile_wo(inp["b_ffn_wo"][0]),
    }
    KTf, Vf = [], []
    for b in range(2):
        kt = np.zeros((16, 128, 4096), ml_dtypes.bfloat16)
        v = np.zeros((8, 4096, 256), ml_dtypes.bfloat16)
        for j in range(4):
            idx = tok_index(j)
            kt[:, :, idx] = ra[4 * b + j]["KT"]
            v[:, idx, :] = ra[4 * b + j]["V"]
        KTf.append(kt)
        Vf.append(np.ascontiguousarray(v.reshape(8, 2, 16, 128, 256).transpose(0, 1, 3, 2, 4)))
    in_maps = []
    for i in range(8):
        b, j = i // 4, i % 4
        idx = tok_index(j)
        m = dict(shared)
        m["xT"] = np.ascontiguousarray(ra[i]["outT"])
        m["pos"] = np.ascontiguousarray(inp["positions"][b][idx][None, :]).astype(np.int32)
        m["cols"] = host_cols(inp, b)
        m["KTf"] = KTf[b]
        m["Vf"] = Vf[b]
        m["mask"] = host_mask(j)
        in_maps.append(m)
    in_maps = in_maps[:ncores]
    res = run_bass_kernel_spmd(nc, in_maps, core_ids=list(range(ncores)))
    return res.results


def run_f(inp, ncores=8):
    nc = get_nc("F")
    shared = {
        "a_ada_w": tile_w(inp["a_ada_w"][0]), "a_in_w": tile_w(inp["a_in_w"][0]), "a_out_w": tile_w(inp["a_out_w"][0]),
        "a_ffn_wi": tile_w(inp["a_ffn_wi"][0]), "a_ffn_wo": tile_wo(inp["a_ffn_wo"][0]),
        "kv_ada_w": tile_w(inp["kv_ada_w"]), "kv_w": tile_w(inp["kv_w"]),
        "sgu_w": np.ascontiguousarray(inp["a_sgu_w"][0]),
        "rows": np.ascontiguousarray(np.stack([inp["a_in_b"][0][D:], inp["a_sgu_ln_b"][0], inp["a_sgu_b"][0].reshape(-1)]).astype(np.float32)),
        "b_ada_w": tile_w(inp["b_ada_w"][0]), "b_q_w": tile_w(inp["b_q_w"][0]), "b_o_w": tile_w(inp["b_o_w"][0]),
        "b_ffn_wi": tile_w(inp["b_ffn_wi"][0]), "b_ffn_wo": tile_wo(inp["b_ffn_wo"][0]),
    }
    in_maps = []
    for i in range(8):
        b, j = i // 4, i % 4
        idx = tok_index(j)
        m = dict(shared)
        m["xT"] = np.ascontiguousarray(inp["x"][b][idx].T)
        m["pos"] = np.ascontiguousarray(inp["positions"][b][idx][None, :]).astype(np.int32)
        m["cols"] = host_cols(inp, b)
        m["mask"] = host_mask(j)
        in_maps.append(m)
    in_maps = in_maps[:ncores]
    res = run_bass_kernel_spmd(nc, in_maps, core_ids=list(range(ncores)))
    return res.results


def kernel(**inp):
    inp = {k: np.asarray(v) for k, v in inp.items()}
    rf = run_f(inp)
    out = np.zeros((2, 4096, D), np.float32)
    for i in range(8):
        b, j = i // 4, i % 4
        out[b, tok_index(j), :] = rf[i]["outT"].T
    return out
```

```python
import math
import numpy as np
import ml_dtypes
import concourse.bass as bass
import concourse.mybir as mybir
from concourse.bass_utils import run_bass_kernel_spmd
from contextlib import ExitStack

F32 = mybir.dt.float32
BF16 = mybir.dt.bfloat16
I32 = mybir.dt.int32
ALU = mybir.AluOpType
AF = mybir.ActivationFunctionType

D = 2048
NCH = 16
T = 1024
DFF = 5632
NJ = 44
EPS = 1e-6
NSLOT = 5
ENGS = ("pe", "act", "dve", "pool", "sp")

COLS = {}
_o = 0
for _n, _w in [("a_norm1_g", 16), ("a_norm2_g", 16), ("a_in_b_u", 16), ("a_ln_g", 16), ("a_ln_b", 16), ("kv_norm_g", 16),
               ("b_norm1_g", 16), ("b_norm2_g", 16), ("a_ada_b", 96), ("kv_ada_b", 32), ("b_ada_b", 96),
               ("k_norm_g", 1), ("q_norm_g", 1), ("subln_g", 2), ("lam", 4), ("invf", 1), ("cT", 16)]:
    COLS[_n] = (_o, _w)
    _o += _w
NCOL = _o


class Buf:
    __slots__ = ("name", "w", "r")

    def __init__(self, name=""):
        self.name = name
        self.w = None
        self.r = {}


class Prog:
    def __init__(self, nc, es):
        self.nc = nc
        self.es = es
        self.q = {e: [] for e in ENGS}
        self.sem = {}
        self.cnt = {}
        self.seen = {e: {} for e in ENGS}
        for e in ENGS:
            self.new_sem("E_" + e)
        self.n_dma_sem = 0

    def new_sem(self, name):
        self.sem[name] = self.es.enter_context(self.nc.semaphore(name))
        self.cnt[name] = 0
        return name

    def dma_sem(self):
        self.n_dma_sem += 1
        return self.new_sem("D%d" % self.n_dma_sem)

    def op(self, eng, fn, reads=(), writes=(), dma=None, extra_waits=(), dma_inc=16):
        waits = {}

        def need(tok):
            if tok is None:
                return
            s, v = tok
            if waits.get(s, 0) < v:
                waits[s] = v

        for b in reads:
            need(b.w)
        for b in writes:
            need(b.w)
            for s, v in b.r.items():
                need((s, v))
        for t in extra_waits:
            need(t)
        if eng == "pe":
            waits.pop("E_pe", None)
        wl = []
        seen = self.seen[eng]
        for s, v in waits.items():
            if seen.get(s, 0) < v:
                seen[s] = v
                wl.append((s, v))
        if dma is not None:
            s = dma
            self.cnt[s] += dma_inc
            inc = (s, dma_inc)
        else:
            s = "E_" + eng
            self.cnt[s] += 1
            inc = (s, 1)
        tok = (s, self.cnt[s])
        self.q[eng].append((wl, fn, inc))
        for b in reads:
            if b.r.get(s, 0) < tok[1]:
                b.r[s] = tok[1]
        for b in writes:
            b.w = tok
            b.r = {}
        return tok

    def wait_only(self, eng, toks):
        wl = []
        seen = self.seen[eng]
        for s, v in toks:
            if seen.get(s, 0) < v:
                seen[s] = v
                wl.append((s, v))
        if wl:
            self.q[eng].append((wl, None, None))

    def barrier(self):
        toks = [(s, c) for s, c in self.cnt.items() if c > 0]
        for e in ENGS:
            self.wait_only(e, toks)

    def chain(self, eng, fns, reads=(), writes=()):
        for fn in fns:
            tok = self.op(eng, fn, reads=reads, writes=writes)
        return tok

    def emit(self):
        nc = self.nc
        block = self.es.enter_context(nc.Block())
        sem = self.sem

        def replay(engobj, items):
            for wl, fn, inc in items:
                for s, v in wl:
                    engobj.wait_ge(sem[s], v)
                if fn is not None:
                    ins = fn(engobj)
                    ins.then_inc(sem[inc[0]], inc[1])

        q = self.q

        @block.tensor
        def _(e):
            replay(e, q["pe"])

        @block.scalar
        def _(e):
            replay(e, q["act"])

        @block.vector
        def _(e):
            replay(e, q["dve"])

        @block.gpsimd
        def _(e):
            replay(e, q["pool"])

        @block.sync
        def _(e):
            replay(e, q["sp"])


class WStream:
    def __init__(self, P, nc, es, nslot):
        self.P = P
        self.nslot = nslot
        self.slots = [es.enter_context(nc.sbuf_tensor("s_wslot%d" % i, [128, 4096], BF16)) for i in range(nslot)]
        self.bufs = [Buf("wslot%d" % i) for i in range(nslot)]
        self.sems = [P.dma_sem() for _ in range(nslot)]
        self.plan = []
        self.issued = 0
        self.taken = 0
        self.closed = 0

    def add(self, key, src, shape):
        self.plan.append((key, src, shape))

    def pump(self):
        while self.issued < len(self.plan) and self.issued - self.nslot < self.closed:
            i = self.issued
            key, src, shape = self.plan[i]
            s = i % self.nslot
            dst = self.view(s, shape)
            self.P.op("pool", (lambda dst, src: (lambda e: e.dma_start(out=dst, in_=src)))(dst, src),
                      writes=[self.bufs[s]], dma=self.sems[s])
            self.issued += 1

    def view(self, s, shape):
        if shape == "k":
            return self.slots[s][:].rearrange("p (k f) -> p k f", k=16)
        else:
            return self.slots[s][:].rearrange("p (j f) -> p j f", j=2)

    def take(self, key):
        i = self.taken
        assert self.plan[i][0] == key, (self.plan[i][0], key)
        assert i - self.closed < self.nslot, "too many open weight units"
        self.pump()
        assert self.issued > i
        self.taken += 1
        s = i % self.nslot
        return self.view(s, self.plan[i][2]), self.bufs[s]

    def release(self):
        self.closed = self.taken
        self.pump()


def tile_w(W):
    K, Fd = W.shape
    return np.ascontiguousarray(W.reshape(K // 128, 128, Fd // 256, 256).transpose(2, 1, 0, 3))


def tile_wo(W):
    K, Fd = W.shape
    return np.ascontiguousarray(W.reshape(K // 256, 2, 128, Fd).transpose(0, 2, 1, 3))


def colv(v):
    v = np.asarray(v, np.float32).reshape(-1, 128)
    return np.ascontiguousarray(v.T)


def tok_index(j):
    idx = []
    for s in range(8):
        p = 8 * (s // 2) + (j if s % 2 == 0 else 7 - j)
        idx.append(np.arange(p * 128, (p + 1) * 128))
    return np.concatenate(idx)


LAM_INIT = 0.8 - 0.6 * math.exp(-0.3 * 1)


import os
STOP = int(os.environ.get("KSTOP", "99"))


class _Stop(Exception):
    pass


def stage(n):
    if STOP <= n:
        raise _Stop()


class Builder:
    def __init__(self, mode):
        self.mode = mode
        self.nc = bass.Bass("TRN2", target_bir_lowering=False)

    def sbt(self, name, shape, dt):
        self._uid = getattr(self, "_uid", 0) + 1
        return self.nc.sbuf_tensor("s_%s_%d" % (name, self._uid), list(shape), dt)

    def dram_in(self, name, shape, dt=F32):
        return self.nc.dram_tensor(name, list(shape), dt, kind="ExternalInput").ap()

    def dram_out(self, name, shape, dt=F32):
        return self.nc.dram_tensor(name, list(shape), dt, kind="ExternalOutput").ap()

    def build(self):
        nc = self.nc
        mode = self.mode
        with ExitStack() as es:
            self.es = es
            P = self.P = Prog(nc, es)
            sb = lambda n, s, d: es.enter_context(self.sbt("" + n, list(s), d))
            self.sb = sb
            xT_d = self.dram_in("xT", [D, T])
            cols_d = self.dram_in("cols", [128, NCOL])
            pos_d = self.dram_in("pos", [1, T], I32)
            out_d = self.dram_out("outT", [D, T])
            self.DBG = bool(int(os.environ.get("KDEBUG", "0")))
            if self.DBG:
                self.dbgK = self.dram_out("dbgK", [128, 4096], BF16)
                self.dbgV = self.dram_out("dbgV", [128, 4096], BF16)
                self.dbgA = self.dram_out("dbgA", [3, 1024, 1024], BF16)
                self.dbgL = self.dram_out("dbgL", [256, 1024], BF16)
            W = {}
            if mode in ("A", "F"):
                rows_d = self.dram_in("rows", [3, D])
                sguw_d = self.dram_in("sgu_w", [16, 128, 128])
                W["a_ada"] = self.dram_in("a_ada_w", [48, 128, 16, 256])
                W["a_in"] = self.dram_in("a_in_w", [16, 128, 16, 256])
                W["a_out"] = self.dram_in("a_out_w", [8, 128, 16, 256])
                W["a_wi"] = self.dram_in("a_ffn_wi", [44, 128, 16, 256])
                W["a_wo"] = self.dram_in("a_ffn_wo", [22, 128, 2, 2048])
                W["kv_ada"] = self.dram_in("kv_ada_w", [16, 128, 16, 256])
                W["kv"] = self.dram_in("kv_w", [16, 128, 16, 256])
            if mode == "A":
                KT_d = self.dram_out("KT", [16, 128, T], BF16)
                V_d = self.dram_out("V", [8, T, 256], BF16)
            if mode == "F":
                self.kt_loc = [nc.dram_tensor("kt_loc%d" % i, [256, T], BF16, kind="Internal").ap() for i in range(8)]
                self.kt_all = [nc.dram_tensor("kt_all%d" % i, [4 * 256, T], BF16, kind="Internal").ap() for i in range(8)]
                self.v_loc = [nc.dram_tensor("v_loc%d" % i, [256, T], BF16, kind="Internal").ap() for i in range(8)]
                self.v_all = [nc.dram_tensor("v_all%d" % i, [4 * 256, T], BF16, kind="Internal").ap() for i in range(8)]
                self.dm_loc = nc.dram_tensor("dm_loc", [128, 128], BF16, kind="Internal").ap()
                self.dm_all = nc.dram_tensor("dm_all", [4 * 128, 128], BF16, kind="Internal").ap()
                KT_d = [self.kt_loc[hh // 2][(hh % 2) * 128:(hh % 2) * 128 + 128, :] for hh in range(16)]
                V_d = [self.v_loc[u].rearrange("r (q d) -> (r q) d", d=256) for u in range(8)]
            if mode in ("B", "F"):
                W["b_ada"] = self.dram_in("b_ada_w", [48, 128, 16, 256])
                W["b_q"] = self.dram_in("b_q_w", [8, 128, 16, 256])
                W["b_o"] = self.dram_in("b_o_w", [8, 128, 16, 256])
                W["b_wi"] = self.dram_in("b_ffn_wi", [44, 128, 16, 256])
                W["b_wo"] = self.dram_in("b_ffn_wo", [22, 128, 2, 2048])
                mask_d = self.dram_in("mask", [128, 8, 128])
            if mode == "B":
                KTf_d = self.dram_in("KTf", [16, 128, 4096], BF16)
                Vf_d = self.dram_in("Vf", [8, 2, 128, 16, 256], BF16)
            if mode == "F":
                KTf_d = [self.kt_all[hh // 2].rearrange("(j s d) t -> s d j t", j=4, s=2, d=128)[hh % 2] for hh in range(16)]
                Vf_d = [[self.v_all[u].rearrange("(j r) (q d) -> j (r q) d", j=4, d=256)[j].rearrange("(s p) d -> p s d", p=128)
                         for j in range(4)] for u in range(8)]
            self.W = W

            self.xT = sb("xT", [128, NCH, T], F32)
            self.hT = sb("hT", [128, NCH, T], BF16)
            self.cols = sb("cols", [128, NCOL], F32)
            self.rstd = sb("rstd", [128, T], F32)
            self.ones = sb("ones", [128, 128], BF16)
            self.ones_f = sb("ones_f", [128, 128], F32)
            self.sc_bf = sb("sc_bf", [128, NCH], BF16)
            self.adaA = sb("adaA", [128, 96], F32)
            self.adaK = sb("adaK", [128, 32], F32)
            self.mods = sb("mods", [128, 4, 16], F32)
            self.tmpf = [sb("tmpf%d" % i, [128, 512], F32) for i in range(3)]
            self.sqb = [sb("sqb%d" % i, [128, 2, 512], BF16) for i in range(2)]
            self.eps_col = sb("eps_col", [128, 4], F32)
            B = self.B = {}
            for n in ["xT0", "xT1", "hT0", "hT1", "cols", "rstd0", "rstd1", "ones", "sc_bf", "adaA", "adaK", "mods",
                      "tmpf0", "tmpf1", "tmpf2", "sqb0", "sqb1", "out", "consts"]:
                B[n] = Buf(n)
            self.ps = [es.enter_context(nc.psum_tensor("ps%d" % i, [128, 512], F32)) for i in range(7)]
            self.psT = es.enter_context(nc.psum_tensor("psT", [128, 1024], BF16))
            self.psA = self.psT[:, 512:1024].bitcast(F32)
            self.Bps = [Buf("ps%d" % i) for i in range(7)]
            self.BpsT = Buf("psT")
            self.BpsA = self.BpsT
            self.ws = WStream(P, nc, es, NSLOT)
            self.dl = P.dma_sem()
            self.dx = P.dma_sem()
            self.do = P.dma_sem()
            xT, cols, ones, ones_f, sc_bf = self.xT, self.cols, self.ones, self.ones_f, self.sc_bf

            ws = self.ws
            def padd(wname, idx):
                ws.add((wname, idx), W[wname][idx], "k")
            if mode in ("A", "F"):
                self.plan_ada("a_ada", 0)
                self.plan_ada("a_ada", 1)
                for cb in range(8):
                    padd("a_in", 8 + cb)
                    padd("a_ada", 16 + 2 * cb)
                    padd("a_ada", 17 + 2 * cb)
                for u in range(8):
                    padd("a_in", u)
                    padd("a_ada", 32 + u)
                for u in range(8):
                    padd("a_out", u)
                    padd("a_ada", 40 + u)
                self.plan_ffn("a", "kv_ada")
                for u in range(16):
                    padd("kv", u)
                    if mode == "F":
                        padd("b_ada", u)
            if mode in ("B", "F"):
                if mode == "B":
                    self.plan_ada("b_ada", 0)
                    self.plan_ada("b_ada", 1)
                for u in range(8):
                    ws.add(("b_q", u), W["b_q"][u], "k")
                for g in (2, 3, 4, 5):
                    self.plan_ada("b_ada", g)
                for u in range(8):
                    ws.add(("b_o", u), W["b_o"][u], "k")
                self.plan_ffn("b")

            P.op("sp", lambda e: e.dma_start(out=cols[:], in_=cols_d), writes=[B["cols"]], dma=P.dma_sem())
            for th in range(2):
                for cq in range(4):
                    src = xT_d.rearrange("(c p) t -> p c t", p=128)[:, 4 * cq:4 * cq + 4, th * 512:(th + 1) * 512]
                    dst = xT[:, 4 * cq:4 * cq + 4, th * 512:(th + 1) * 512]
                    P.op("sp", (lambda dst, src: (lambda e: e.dma_start(out=dst, in_=src)))(dst, src), dma=self.dx)
            B["xT0"].w = (self.dx, P.cnt[self.dx])
            B["xT1"].w = (self.dx, P.cnt[self.dx])
            P.op("dve", lambda e: e.memset(ones[:], 1.0), writes=[B["ones"]])
            P.op("dve", lambda e: e.memset(ones_f[:], 1.0), writes=[B["ones"]])
            P.op("dve", lambda e: e.memset(self.eps_col[:, 0:1], EPS), writes=[B["consts"]])
            P.op("dve", lambda e: e.memset(self.eps_col[:, 1:2], -3.1415920), writes=[B["consts"]])
            c0 = COLS["cT"][0]
            P.op("act", lambda e: e.activation(out=sc_bf[:], in_=cols[:, c0:c0 + 16], func=AF.Silu),
                 reads=[B["cols"]], writes=[B["sc_bf"]])

            self.Bktall = [Buf("ktall%d" % i) for i in range(8)]
            self.Bvall = [Buf("vall%d" % i) for i in range(8)]
            self.Bdummy = Buf("dummy")
            try:
                if mode in ("A", "F"):
                    self.layer_a(rows_d, sguw_d, pos_d, KT_d, V_d)
                if mode == "F" and self.DBG:
                    P.op("sp", lambda e: e.dma_start(out=self.dbgA[0], in_=self.kt_all[0]), reads=[self.Bktall[0]], dma=self.do)
                if mode in ("B", "F"):
                    self.layer_b(KTf_d, Vf_d, mask_d, pos_d)
                if mode == "F" and self.DBG:
                    P.op("sp", lambda e: e.dma_start(out=self.dbgA[2], in_=self.kt_all[0]), reads=[self.Bktall[0]], dma=self.do)
                    P.op("sp", lambda e: e.dma_start(out=self.dbgL, in_=self.kt_loc[0]), dma=self.do)
            except _Stop:
                ws.taken = len(ws.plan)

            for th in range(2):
                for cq in range(4):
                    dst = out_d.rearrange("(c p) t -> p c t", p=128)[:, 4 * cq:4 * cq + 4, th * 512:(th + 1) * 512]
                    src = xT[:, 4 * cq:4 * cq + 4, th * 512:(th + 1) * 512]
                    P.op("sp", (lambda dst, src: (lambda e: e.dma_start(out=dst, in_=src)))(dst, src),
                         reads=[B["xT%d" % th]], writes=[B["out"]], dma=self.do)
            P.wait_only("sp", [(self.do, P.cnt[self.do])] + [(d_, P.cnt[d_]) for d_ in getattr(self, "kv_out_sems", [])])
            assert ws.taken == len(ws.plan), (ws.taken, len(ws.plan))
            P.emit()
        return nc

    def col(self, name, i=0, n=1):
        o, w = COLS[name]
        return self.cols[:, o + i:o + i + n]

    def plan_ada(self, wname, g):
        for u in range(8):
            self.ws.add((wname, g * 8 + u), self.W[wname][g * 8 + u], "k")

    def plan_ffn(self, L, ada=None):
        for grp in range(11):
            for qq in range(2):
                q = grp * 2 + qq
                self.ws.add((L + "_wi", q), self.W[L + "_wi"][q], "k")
                self.ws.add((L + "_wi", 22 + q), self.W[L + "_wi"][22 + q], "k")
            for qq in range(2):
                q = grp * 2 + qq
                self.ws.add((L + "_wo", q), self.W[L + "_wo"][q], "j")
            if ada is not None and grp < 8:
                self.ws.add((ada, 2 * grp), self.W[ada][2 * grp], "k")
                self.ws.add((ada, 2 * grp + 1), self.W[ada][2 * grp + 1], "k")

    def ada_unit(self, wname, idx, dst, dbuf, bias_name):
        P, B = self.P, self.B
        ps = self.psA
        g, u = idx // 8, idx % 8
        wv, wb = self.ws.take((wname, idx))

        def fn(e, wv=wv, u=u):
            for fc in range(2):
                c = u * 2 + fc
                for k in range(NCH):
                    ins = e.matmul(ps[:, c:c + 1], lhsT=wv[:, k, fc * 128:(fc + 1) * 128],
                                   rhs=self.sc_bf[:, k:k + 1], start=(k == 0), stop=(k == NCH - 1))
            return ins
        P.op("pe", fn, reads=[wb, B["sc_bf"]], writes=[self.BpsA])
        self.ws.release()
        if u == 7:
            bo = COLS[bias_name][0] + g * 16
            P.op("dve", lambda e: e.tensor_tensor(out=dst[:, g * 16:(g + 1) * 16], in0=ps[:, 0:16],
                                                  in1=self.cols[:, bo:bo + 16], op=ALU.add),
                 reads=[self.BpsA, B["cols"]], writes=[dbuf])

    def ada_group(self, wname, g, dst, dbuf, bias_name):
        for u in range(8):
            self.ada_unit(wname, g * 8 + u, dst, dbuf, bias_name)

    def mk_gsc(self, slot, gname, sc_ap, src_buf):
        P, B = self.P, self.B
        o = COLS[gname][0]
        P.op("dve", lambda e: e.scalar_tensor_tensor(out=self.mods[:, slot, :], in0=sc_ap, scalar=1.0,
                                                     in1=self.cols[:, o:o + 16], op0=ALU.add, op1=ALU.mult),
             reads=[src_buf, B["cols"]], writes=[B["mods"]])

    def norm_stats(self):
        P, B = self.P, self.B
        xT = self.xT
        for th in range(2):
            bank = th
            for cq in range(8):
                sq = self.sqb[cq % 2]
                sqB = B["sqb%d" % (cq % 2)]
                P.op("act", lambda e, sq=sq, cq=cq, th=th: e.activation(
                    out=sq[:], in_=xT[:, 2 * cq:2 * cq + 2, th * 512:(th + 1) * 512], func=AF.Square),
                    reads=[B["xT%d" % th]], writes=[sqB])

                def fn(e, sq=sq, cq=cq, bank=bank):
                    for c in range(2):
                        ins = e.matmul(self.ps[bank][:], lhsT=self.ones[:], rhs=sq[:, c, :],
                                       start=(cq == 0 and c == 0), stop=(cq == 7 and c == 1))
                    return ins
                P.op("pe", fn, reads=[sqB, B["ones"]], writes=[self.Bps[bank]])
            tf = self.tmpf[2]
            P.op("act", lambda e, bank=bank, tf=tf: e.activation(out=tf[:], in_=self.ps[bank][:], func=AF.Sqrt,
                                                              bias=self.eps_col[:, 0:1], scale=1.0 / D),
                 reads=[self.Bps[bank], B["consts"]], writes=[B["tmpf2"]])
            P.op("dve", lambda e, th=th, tf=tf: e.reciprocal(out=self.rstd[:, th * 512:(th + 1) * 512], in_=tf[:]),
                 reads=[B["tmpf2"]], writes=[B["rstd%d" % th]])

    def modulate(self, gsc_slot, sh_ap_fn, sh_buf):
        P, B = self.P, self.B
        for th in range(2):
            for c in range(NCH):
                i = c % 2
                tf = self.tmpf[i]
                P.op("dve", lambda e, tf=tf, c=c, th=th: e.scalar_tensor_tensor(
                    out=tf[:], in0=self.xT[:, c, th * 512:(th + 1) * 512], scalar=self.mods[:, gsc_slot, c:c + 1],
                    in1=self.rstd[:, th * 512:(th + 1) * 512], op0=ALU.mult, op1=ALU.mult),
                    reads=[B["xT%d" % th], B["mods"], B["rstd%d" % th]], writes=[B["tmpf%d" % i]])
                P.op("act", lambda e, tf=tf, c=c, th=th: e.activation(
                    out=self.hT[:, c, th * 512:(th + 1) * 512], in_=tf[:], func=AF.Identity,
                    bias=sh_ap_fn(c), scale=1.0),
                    reads=[B["tmpf%d" % i], sh_buf], writes=[B["hT%d" % th]])

    def proj_residual(self, wname, rhs_fn, rhs_bufs, gate_ap_fn, gate_buf, after_unit=None):
        P, B = self.P, self.B
        n = 0
        for u in range(8):
            wv, wb = self.ws.take((wname, u))
            for fc in range(2):
                m = u * 2 + fc
                for th in range(2):
                    bank = n % 4
                    n += 1

                    def fn(e, wv=wv, fc=fc, th=th, bank=bank):
                        for k in range(NCH):
                            ins = e.matmul(self.ps[bank][:], lhsT=wv[:, k, fc * 128:(fc + 1) * 128],
                                           rhs=rhs_fn(k, th), start=(k == 0), stop=(k == NCH - 1))
                        return ins
                    P.op("pe", fn, reads=[wb] + rhs_bufs(th), writes=[self.Bps[bank]])
                    P.op("dve", lambda e, m=m, th=th, bank=bank: e.scalar_tensor_tensor(
                        out=self.xT[:, m, th * 512:(th + 1) * 512], in0=self.ps[bank][:], scalar=gate_ap_fn(m),
                        in1=self.xT[:, m, th * 512:(th + 1) * 512], op0=ALU.mult, op1=ALU.add),
                        reads=[self.Bps[bank], gate_buf, B["xT%d" % th]], writes=[B["xT%d" % th]])
            self.ws.release()
            if after_unit is not None:
                after_unit(u)

    def ffn(self, L, gate_ap_fn, gate_buf, between=None):
        P, B, nc = self.P, self.B, self.nc
        with ExitStack() as ph:
            aT = [ph.enter_context(self.sbt("aT%d" % i, [128, 4, T], BF16)) for i in range(2)]
            BaT = [[Buf("aT%d_%d" % (i, th)) for th in range(2)] for i in range(2)]
            sg = [ph.enter_context(self.sbt("sg%d" % i, [128, 512], F32)) for i in range(2)]
            Bsg = [Buf("sg0"), Buf("sg1")]
            ny = 0
            nsg = 0
            for grp in range(11):
                ab = grp % 2
                for qq in range(2):
                    q = grp * 2 + qq
                    wg, wgb = self.ws.take((L + "_wi", q))
                    wu, wub = self.ws.take((L + "_wi", 22 + q))
                    for fc in range(2):
                        jj = qq * 2 + fc
                        for th in range(2):
                            bg = th
                            bu = 2 + th

                            def fg(e, wg=wg, fc=fc, th=th, bg=bg):
                                for k in range(NCH):
                                    ins = e.matmul(self.ps[bg][:], lhsT=wg[:, k, fc * 128:(fc + 1) * 128],
                                                   rhs=self.hT[:, k, th * 512:(th + 1) * 512], start=(k == 0), stop=(k == NCH - 1))
                                return ins
                            P.op("pe", fg, reads=[wgb, B["hT%d" % th]], writes=[self.Bps[bg]])

                            def fu(e, wu=wu, fc=fc, th=th, bu=bu):
                                for k in range(NCH):
                                    ins = e.matmul(self.ps[bu][:], lhsT=wu[:, k, fc * 128:(fc + 1) * 128],
                                                   rhs=self.hT[:, k, th * 512:(th + 1) * 512], start=(k == 0), stop=(k == NCH - 1))
                                return ins
                            P.op("pe", fu, reads=[wub, B["hT%d" % th]], writes=[self.Bps[bu]])
                            si = nsg % 2
                            nsg += 1
                            P.op("act", lambda e, si=si, bg=bg: e.activation(out=sg[si][:], in_=self.ps[bg][:], func=AF.Silu),
                                 reads=[self.Bps[bg]], writes=[Bsg[si]])
                            P.op("dve", lambda e, si=si, bu=bu, ab=ab, jj=jj, th=th: e.tensor_tensor(
                                out=aT[ab][:, jj, th * 512:(th + 1) * 512], in0=self.ps[bu][:], in1=sg[si][:], op=ALU.mult),
                                reads=[self.Bps[bu], Bsg[si]], writes=[BaT[ab][th]])
                    self.ws.release()
                wo0, wob0 = self.ws.take((L + "_wo", grp * 2))
                wo1, wob1 = self.ws.take((L + "_wo", grp * 2 + 1))
                wos = (wo0, wo1)
                for m in range(NCH):
                    for th in range(2):
                        bank = 4 + ny % 3
                        ny += 1

                        def fy(e, m=m, th=th, bank=bank, wos=wos, ab=ab):
                            for jj in range(4):
                                ins = e.matmul(self.ps[bank][:], lhsT=wos[jj // 2][:, jj % 2, m * 128:(m + 1) * 128],
                                               rhs=aT[ab][:, jj, th * 512:(th + 1) * 512], start=(jj == 0), stop=(jj == 3))
                            return ins
                        P.op("pe", fy, reads=[wob0, wob1, BaT[ab][th]], writes=[self.Bps[bank]])
                        P.op("dve", lambda e, m=m, th=th, bank=bank: e.scalar_tensor_tensor(
                            out=self.xT[:, m, th * 512:(th + 1) * 512], in0=self.ps[bank][:], scalar=gate_ap_fn(m),
                            in1=self.xT[:, m, th * 512:(th + 1) * 512], op0=ALU.mult, op1=ALU.add),
                            reads=[self.Bps[bank], gate_buf, B["xT%d" % th]], writes=[B["xT%d" % th]])
                self.ws.release()
                if between is not None:
                    between(grp)
        P.barrier()

    def rope_tables(self, pos_d):
        P, B, nc = self.P, self.B, self.nc
        B["rope"] = Buf("rope")
        with ExitStack() as ph:
            posi = ph.enter_context(self.sbt("posi", [128, T], I32))
            ua = ph.enter_context(self.sbt("rope_u", [128, T], F32))
            ub = ph.enter_context(self.sbt("rope_k", [128, T], F32))
            Bp, Ba, Bb = Buf("posi"), Buf("ua"), Buf("ub")
            io = COLS["invf"][0]
            for off, dst in ((0.5, self.sinS), (0.75, self.cosF)):
                P.op("sp", lambda e: e.dma_start(out=posi[:], in_=pos_d.broadcast_to([128, T])), writes=[Bp], dma=P.dma_sem())
                P.op("dve", lambda e: e.tensor_copy(out=ua[:], in_=posi[:]), reads=[Bp], writes=[Ba])
                P.op("dve", lambda e, off=off: e.tensor_scalar(out=ua[:], in0=ua[:], scalar1=self.cols[:, io:io + 1], scalar2=off,
                                                               op0=ALU.mult, op1=ALU.add),
                     reads=[Ba, B["cols"]], writes=[Ba])
                P.op("dve", lambda e: e.tensor_copy(out=posi[:], in_=ua[:]), reads=[Ba], writes=[Bp])
                P.op("dve", lambda e: e.tensor_copy(out=ub[:], in_=posi[:]), reads=[Bp], writes=[Bb])
                P.op("dve", lambda e: e.tensor_tensor(out=ua[:], in0=ua[:], in1=ub[:], op=ALU.subtract),
                     reads=[Ba, Bb], writes=[Ba])
                P.op("dve", lambda e: e.tensor_single_scalar(out=ub[:], in_=ua[:], scalar=0.0, op=ALU.is_lt),
                     reads=[Ba], writes=[Bb])
                P.op("dve", lambda e: e.tensor_tensor(out=ua[:], in0=ua[:], in1=ub[:], op=ALU.add),
                     reads=[Ba, Bb], writes=[Ba])
                P.op("act", lambda e, dst=dst: e.activation(out=dst[:], in_=ua[:], func=AF.Sin, bias=self.eps_col[:, 1:2],
                                                            scale=6.2831845),
                     reads=[Ba, B["consts"]], writes=[B["rope"]])
            sinS_ = self.sinS
            P.op("dve", lambda e: e.tensor_scalar(out=sinS_[64:128, :], in0=sinS_[64:128, :], scalar1=-1.0, scalar2=None,
                                                  op0=ALU.mult),
                 reads=[B["rope"]], writes=[B["rope"]])
        P.barrier()

    def qk_head(self, ps_bank, th, gcol_ap, dst_ap, dst_buf):
        P, B = self.P, self.B
        ps = self.ps[ps_bank]
        bss = 2 + th
        sq = self.sqb[0]
        P.op("act", lambda e: e.activation(out=sq[:, 0, :], in_=ps[:], func=AF.Square),
             reads=[self.Bps[ps_bank]], writes=[B["sqb0"]])
        P.op("pe", lambda e: e.matmul(self.ps[bss][:], lhsT=self.ones[:], rhs=sq[:, 0, :], start=True, stop=True),
             reads=[B["sqb0"], B["ones"]], writes=[self.Bps[bss]])
        t0, t1, t2 = self.tmpf
        P.op("act", lambda e: e.activation(out=t2[:], in_=self.ps[bss][:], func=AF.Sqrt, bias=self.eps_col[:, 0:1],
                                           scale=1.0 / 128),
             reads=[self.Bps[bss], B["consts"]], writes=[B["tmpf2"]])
        P.op("dve", lambda e: e.reciprocal(out=t2[:], in_=t2[:]), reads=[B["tmpf2"]], writes=[B["tmpf2"]])
        qn = self.qn
        P.op("dve", lambda e: e.scalar_tensor_tensor(out=qn[:], in0=ps[:], scalar=gcol_ap, in1=t2[:],
                                                     op0=ALU.mult, op1=ALU.mult),
             reads=[self.Bps[ps_bank], B["tmpf2"], B["cols"]], writes=[B["qn"]])
        sl = slice(th * 512, (th + 1) * 512)
        cosF, sinS = self.cosF, self.sinS
        P.op("dve", lambda e: e.tensor_tensor(out=t0[:], in0=qn[:], in1=cosF[:, sl], op=ALU.mult),
             reads=[B["qn"], B["rope"]], writes=[B["tmpf0"]])
        P.op("dve", lambda e: e.tensor_tensor(out=t1[0:64, :], in0=qn[64:128, :], in1=sinS[64:128, sl], op=ALU.mult),
             reads=[B["qn"], B["rope"]], writes=[B["tmpf1"]])
        P.op("dve", lambda e: e.tensor_tensor(out=t1[64:128, :], in0=qn[0:64, :], in1=sinS[0:64, sl], op=ALU.mult),
             reads=[B["qn"], B["rope"]], writes=[B["tmpf1"]])
        P.op("dve", lambda e: e.tensor_tensor(out=dst_ap, in0=t0[:], in1=t1[:], op=ALU.add),
             reads=[B["tmpf0"], B["tmpf1"]], writes=[dst_buf])

    def layer_a(self, rows_d, sguw_d, pos_d, KT_d, V_d):
        P, B, nc, sb = self.P, self.B, self.nc, self.sb
        adaA, adaK = self.adaA, self.adaK
        stage(0)
        phA = ExitStack()
        wmT = phA.enter_context(self.sbt("wmT", [128, 16, 128], BF16))
        B2 = phA.enter_context(self.sbt("B2", [128, 16, 128], F32))
        inbv = phA.enter_context(self.sbt("inbv", [1, D], BF16))
        for n in ["wmT", "B2", "inbv"]:
            B[n] = Buf(n)
        P.op("pool", lambda e: e.dma_start(out=inbv[:], in_=rows_d[0:1, :]), writes=[B["inbv"]], dma=P.dma_sem())
        with ExitStack() as ph:
            sguw = ph.enter_context(self.sbt("sguw", [128, 16, 128], F32))
            sguwb = ph.enter_context(self.sbt("sguwb", [128, 16, 128], BF16))
            sgub = ph.enter_context(self.sbt("sgub", [128, 16, 128], F32))
            identb = ph.enter_context(self.sbt("identb0", [128, 128], BF16))
            tri = ph.enter_context(self.sbt("tri", [128, 128], BF16))
            Bsw, Bswb, Bsb, Bid, Btri = Buf("sguw"), Buf("sguwb"), Buf("sgub"), Buf("ident"), Buf("tri")
            P.op("sp", lambda e: e.dma_start(out=sguw[:], in_=sguw_d.rearrange("g t s -> t g s")), writes=[Bsw], dma=P.dma_sem())
            P.op("sp", lambda e: e.dma_start(out=sgub[:].rearrange("p g t -> p (g t)"), in_=rows_d[2:3, :].broadcast_to([128, D])),
                 writes=[Bsb], dma=P.dma_sem())
            P.op("pool", lambda e: e.memset(identb[:], 1.0), writes=[Bid])
            P.op("pool", lambda e: e.affine_select(out=identb[:], in_=identb[:], pattern=[[-1, 128]], compare_op=ALU.is_equal,
                                                   fill=0.0, base=0, channel_multiplier=1), reads=[Bid], writes=[Bid])
            P.op("pool", lambda e: e.memset(tri[:], 1.0), writes=[Btri])
            P.op("pool", lambda e: e.affine_select(out=tri[:], in_=tri[:], pattern=[[1, 128]], compare_op=ALU.is_ge,
                                                   fill=0.0, base=0, channel_multiplier=-1), reads=[Btri], writes=[Btri])
            P.op("dve", lambda e: e.tensor_copy(out=sguwb[:], in_=sguw[:]), reads=[Bsw], writes=[Bswb])
            for g4 in range(4):
                def ft(e, g4=g4):
                    for gi in range(4):
                        g = g4 * 4 + gi
                        ins = e.transpose(out=self.psT[:, gi * 128:(gi + 1) * 128], in_=sguwb[:, g, :], identity=identb[:])
                    return ins
                P.op("pe", ft, reads=[Bswb, Bid], writes=[self.BpsT])
                for gi in range(4):
                    g = g4 * 4 + gi
                    P.op("dve", lambda e, g=g, gi=gi: e.tensor_tensor(
                        out=wmT[:, g, :], in0=self.psT[:, gi * 128:(gi + 1) * 128], in1=tri[:], op=ALU.mult),
                        reads=[self.BpsT, Btri], writes=[B["wmT"]])
            lb = COLS["a_ln_b"][0]
            for q in range(4):
                bank = q % 2
                P.op("pe", lambda e, q=q, bank=bank: e.matmul(
                    self.ps[bank][:], lhsT=self.ones[:], rhs=wmT[:, 4 * q:4 * q + 4, :], start=True, stop=True),
                    reads=[B["wmT"], B["ones"]], writes=[self.Bps[bank]])
                for gi in range(4):
                    g = 4 * q + gi
                    P.op("dve", lambda e, g=g, gi=gi, bank=bank: e.scalar_tensor_tensor(
                        out=B2[:, g, :], in0=self.ps[bank][:, gi * 128:(gi + 1) * 128], scalar=self.cols[:, lb + g:lb + g + 1],
                        in1=sgub[:, g, :], op0=ALU.mult, op1=ALU.add),
                        reads=[self.Bps[bank], Bsb, B["cols"]], writes=[B["B2"]])
        P.barrier()
        stage(1)

        self.norm_stats()
        stage(2)
        self.ada_group("a_ada", 0, adaA, B["adaA"], "a_ada_b")
        self.ada_group("a_ada", 1, adaA, B["adaA"], "a_ada_b")
        self.mk_gsc(0, "a_norm1_g", adaA[:, 16:32], B["adaA"])
        self.modulate(0, lambda c: adaA[:, c:c + 1], B["adaA"])
        stage(3)

        with ExitStack() as ph:
            VS = ph.enter_context(self.sbt("VS", [128, 8, 16, 128], BF16))
            BVS = [Buf("VS%d" % tb) for tb in range(8)]
            st1 = ph.enter_context(self.sbt("st1", [128, 8, 8], F32))
            st2 = ph.enter_context(self.sbt("st2", [128, 8, 8], F32))
            stt_ = ph.enter_context(self.sbt("stt", [128, 8, 4], F32))
            junk = ph.enter_context(self.sbt("junk", [128, 256], BF16))
            ug = [ph.enter_context(self.sbt("ug%d" % i, [128, 512], BF16)) for i in range(2)]
            Bst = [Buf("st%d" % tb) for tb in range(8)]
            Bjunk = Buf("junk")
            Bug = [Buf("ug0"), Buf("ug1")]
            P.op("dve", lambda e: e.memset(st1[:], 0.0), writes=Bst)
            P.op("dve", lambda e: e.memset(st2[:], 0.0), writes=Bst)
            nb = 0
            for cb in range(8):
                wv, wb = self.ws.take(("a_in", 8 + cb))
                for tb in range(8):
                    bank = nb % 4
                    nb += 1

                    def fv(e, wv=wv, tb=tb, bank=bank, cb=cb):
                        o = self.ps[bank][:, 0:256]
                        for k in range(NCH):
                            e.matmul(o, lhsT=self.hT[:, k, tb * 128:(tb + 1) * 128], rhs=wv[:, k, :], start=(k == 0), stop=False)
                        return e.matmul(o, lhsT=self.ones[0:1, :], rhs=inbv[0:1, cb * 256:(cb + 1) * 256], start=False, stop=True)
                    P.op("pe", fv, reads=[wb, B["hT%d" % (tb // 4)], B["inbv"], B["ones"]], writes=[self.Bps[bank]])
                    vdst = VS[:, tb, 2 * cb:2 * cb + 2, :]
                    P.op("act", lambda e, vdst=vdst, bank=bank, tb=tb, cb=cb: e.activation(
                        out=vdst, in_=self.ps[bank][:, 0:256].rearrange("p (a b) -> p a b", a=2), func=AF.Gelu,
                        accum_out=st1[:, tb, cb:cb + 1]),
                        reads=[self.Bps[bank]], writes=[BVS[tb], Bst[tb]])
                    P.op("act", lambda e, vdst=vdst, tb=tb, cb=cb: e.activation(
                        out=junk[:].rearrange("p (a b) -> p a b", a=2), in_=vdst, func=AF.Square,
                        accum_out=st2[:, tb, cb:cb + 1]),
                        reads=[BVS[tb]], writes=[Bjunk, Bst[tb]])
                self.ws.release()
                self.ada_unit("a_ada", 16 + 2 * cb, adaA, B["adaA"], "a_ada_b")
                self.ada_unit("a_ada", 17 + 2 * cb, adaA, B["adaA"], "a_ada_b")
            stage(4)
            X = mybir.AxisListType.X
            for tb in range(8):
                P.chain("dve", [
                    lambda e, tb=tb: e.tensor_reduce(out=stt_[:, tb, 0:1], in_=st1[:, tb, :], axis=X, op=ALU.add),
                    lambda e, tb=tb: e.tensor_reduce(out=stt_[:, tb, 1:2], in_=st2[:, tb, :], axis=X, op=ALU.add),
                    lambda e, tb=tb: e.tensor_scalar(out=stt_[:, tb, 0:1], in0=stt_[:, tb, 0:1], scalar1=1.0 / D, scalar2=None, op0=ALU.mult),
                    lambda e, tb=tb: e.tensor_tensor(out=stt_[:, tb, 2:3], in0=stt_[:, tb, 0:1], in1=stt_[:, tb, 0:1], op=ALU.mult),
                    lambda e, tb=tb: e.scalar_tensor_tensor(out=stt_[:, tb, 1:2], in0=stt_[:, tb, 1:2], scalar=1.0 / D,
                                                            in1=stt_[:, tb, 2:3], op0=ALU.mult, op1=ALU.subtract),
                ], reads=[Bst[tb]], writes=[Bst[tb]])
                P.op("act", lambda e, tb=tb: e.activation(out=stt_[:, tb, 2:3], in_=stt_[:, tb, 1:2], func=AF.Sqrt,
                                                          bias=self.eps_col[:, 0:1], scale=1.0),
                     reads=[Bst[tb], B["consts"]], writes=[Bst[tb]])
                P.chain("dve", [
                    lambda e, tb=tb: e.reciprocal(out=stt_[:, tb, 2:3], in_=stt_[:, tb, 2:3]),
                    lambda e, tb=tb: e.scalar_tensor_tensor(out=stt_[:, tb, 3:4], in0=stt_[:, tb, 0:1], scalar=-1.0,
                                                            in1=stt_[:, tb, 2:3], op0=ALU.mult, op1=ALU.mult),
                ], reads=[Bst[tb]], writes=[Bst[tb]])
                P.op("dve", lambda e, tb=tb: e.tensor_scalar(
                    out=VS[:, tb, :, :], in0=VS[:, tb, :, :], scalar1=stt_[:, tb, 2:3], scalar2=stt_[:, tb, 3:4],
                    op0=ALU.mult, op1=ALU.add),
                    reads=[BVS[tb], Bst[tb]], writes=[BVS[tb]])
            lg = COLS["a_ln_g"][0]
            ns = 0
            for tb in range(8):
                for g4 in range(4):
                    bank = 4 + ns % 2
                    ns += 1

                    def fsg(e, tb=tb, g4=g4, bank=bank):
                        for gi in range(4):
                            g = g4 * 4 + gi
                            ins = e.matmul(self.ps[bank][:, gi * 128:(gi + 1) * 128], lhsT=VS[:, tb, g, :], rhs=wmT[:, g, :],
                                           start=True, stop=True)
                        return ins
                    P.op("pe", fsg, reads=[BVS[tb], B["wmT"]], writes=[self.Bps[bank]])
                    for gi in range(4):
                        g = g4 * 4 + gi
                        P.op("dve", lambda e, tb=tb, g=g, gi=gi, bank=bank: e.scalar_tensor_tensor(
                            out=VS[:, tb, g, :], in0=self.ps[bank][:, gi * 128:(gi + 1) * 128],
                            scalar=self.cols[:, lg + g:lg + g + 1], in1=B2[:, g, :], op0=ALU.mult, op1=ALU.add),
                            reads=[self.Bps[bank], B["B2"], B["cols"]], writes=[BVS[tb]])
            stage(5)
            bo = COLS["a_in_b_u"][0]
            nu = 0
            for u in range(8):
                wv, wb = self.ws.take(("a_in", u))
                for fc in range(2):
                    g = u * 2 + fc
                    for th in range(2):
                        bank = nu % 4
                        i = nu % 2
                        nu += 1

                        def fu(e, wv=wv, fc=fc, th=th, bank=bank):
                            for k in range(NCH):
                                ins = e.matmul(self.ps[bank][:], lhsT=wv[:, k, fc * 128:(fc + 1) * 128],
                                               rhs=self.hT[:, k, th * 512:(th + 1) * 512], start=(k == 0), stop=(k == NCH - 1))
                            return ins
                        P.op("pe", fu, reads=[wb, B["hT%d" % th]], writes=[self.Bps[bank]])
                        P.op("act", lambda e, i=i, bank=bank, g=g: e.activation(
                            out=ug[i][:], in_=self.ps[bank][:], func=AF.Gelu, bias=self.cols[:, bo + g:bo + g + 1], scale=1.0),
                            reads=[self.Bps[bank], B["cols"]], writes=[Bug[i]])
                        P.op("dve", lambda e, i=i, g=g, th=th: e.tensor_tensor(
                            out=VS[:, 4 * th:4 * th + 4, g, :], in0=VS[:, 4 * th:4 * th + 4, g, :],
                            in1=ug[i][:].rearrange("p (a b) -> p a b", a=4), op=ALU.mult),
                            reads=[Bug[i]] + BVS[4 * th:4 * th + 4], writes=BVS[4 * th:4 * th + 4])
                self.ws.release()
                self.ada_unit("a_ada", 32 + u, adaA, B["adaA"], "a_ada_b")
            self.proj_residual("a_out", lambda k, th: VS[:, 4 * th:4 * th + 4, k, :],
                               lambda th: BVS[4 * th:4 * th + 4], lambda m: adaA[:, 32 + m:33 + m], B["adaA"],
                               after_unit=lambda u: self.ada_unit("a_ada", 40 + u, adaA, B["adaA"], "a_ada_b"))
        phA.close()
        P.barrier()
        stage(6)
        self.norm_stats()
        self.mk_gsc(1, "a_norm2_g", adaA[:, 64:80], B["adaA"])
        self.modulate(1, lambda c: adaA[:, 48 + c:49 + c], B["adaA"])
        self.ffn("a", lambda m: adaA[:, 80 + m:81 + m], B["adaA"],
                 between=lambda grp: (self.ada_unit("kv_ada", 2 * grp, adaK, B["adaK"], "kv_ada_b"),
                                      self.ada_unit("kv_ada", 2 * grp + 1, adaK, B["adaK"], "kv_ada_b")) if grp < 8 else None)
        stage(7)
        self.norm_stats()
        self.mk_gsc(2, "kv_norm_g", adaK[:, 16:32], B["adaK"])
        self.modulate(2, lambda c: adaK[:, c:c + 1], B["adaK"])
        with ExitStack() as ph:
            self.cosF = ph.enter_context(self.sbt("cosF", [128, T], F32))
            self.sinS = ph.enter_context(self.sbt("sinS", [128, T], F32))
            self.rope_tables(pos_d)
            self.qn = ph.enter_context(self.sbt("qn", [128, 512], F32))
            B["qn"] = Buf("qn")
            kt = [ph.enter_context(self.sbt("kt%d" % i, [128, 512], BF16)) for i in range(2)]
            Bkt = [Buf("kt0"), Buf("kt1")]
            vt = [ph.enter_context(self.sbt("vt%d" % i, [128, 256], BF16)) for i in range(2)]
            Bvt = [Buf("vt0"), Buf("vt1")]
            kg = self.col("k_norm_g")
            dks = [P.dma_sem(), P.dma_sem()]
            dvs = [P.dma_sem(), P.dma_sem()]
            self.kv_out_sems = dks + dvs
            nk = 0
            for u in range(8):
                wv, wb = self.ws.take(("kv", u))
                for fc in range(2):
                    hh = u * 2 + fc
                    for th in range(2):
                        bank = th
                        i = nk % 2
                        nk += 1

                        def fk(e, wv=wv, fc=fc, th=th, bank=bank):
                            for k in range(NCH):
                                ins = e.matmul(self.ps[bank][:], lhsT=wv[:, k, fc * 128:(fc + 1) * 128],
                                               rhs=self.hT[:, k, th * 512:(th + 1) * 512], start=(k == 0), stop=(k == NCH - 1))
                            return ins
                        P.op("pe", fk, reads=[wb, B["hT%d" % th]], writes=[self.Bps[bank]])
                        self.qk_head(bank, th, kg, kt[i][:], Bkt[i])
                        P.op("sp", lambda e, i=i, hh=hh, th=th: e.dma_start(out=KT_d[hh][:, th * 512:(th + 1) * 512], in_=kt[i][:]),
                             reads=[Bkt[i]], dma=dks[i])
                self.ws.release()
                if self.mode == "F":
                    self.ada_unit("b_ada", u, adaA, B["adaA"], "b_ada_b")
                if self.mode == "F":
                    P.op("pool", lambda e, u=u: e.collective_compute("AllGather", ALU.bypass, replica_groups=[[0, 1, 2, 3], [4, 5, 6, 7]],
                                                                     ins=[self.kt_loc[u]], outs=[self.kt_all[u]]),
                         extra_waits=[(d_, P.cnt[d_]) for d_ in dks], writes=[self.Bktall[u]], dma=P.dma_sem(), dma_inc=1)
            nv = 0
            for u in range(8):
                wv, wb = self.ws.take(("kv", 8 + u))
                for tb in range(8):
                    bank = 4 + nv % 3
                    i = nv % 2
                    nv += 1

                    def fvv(e, wv=wv, tb=tb, bank=bank):
                        for k in range(NCH):
                            ins = e.matmul(self.ps[bank][:, 0:256], lhsT=self.hT[:, k, tb * 128:(tb + 1) * 128], rhs=wv[:, k, :],
                                           start=(k == 0), stop=(k == NCH - 1))
                        return ins
                    P.op("pe", fvv, reads=[wb, B["hT%d" % (tb // 4)]], writes=[self.Bps[bank]])
                    P.op("act", lambda e, i=i, bank=bank: e.activation(out=vt[i][:], in_=self.ps[bank][:, 0:256], func=AF.Copy),
                         reads=[self.Bps[bank]], writes=[Bvt[i]])
                    P.op("sp", lambda e, i=i, u=u, tb=tb: e.dma_start(out=V_d[u][tb * 128:(tb + 1) * 128, :], in_=vt[i][:]),
                         reads=[Bvt[i]], dma=dvs[i])
                self.ws.release()
                if self.mode == "F":
                    self.ada_unit("b_ada", 8 + u, adaA, B["adaA"], "b_ada_b")
                if self.mode == "F":
                    P.op("pool", lambda e, u=u: e.collective_compute("AllGather", ALU.bypass, replica_groups=[[0, 1, 2, 3], [4, 5, 6, 7]],
                                                                     ins=[self.v_loc[u]], outs=[self.v_all[u]]),
                         extra_waits=[(d_, P.cnt[d_]) for d_ in dvs], writes=[self.Bvall[u]], dma=P.dma_sem(), dma_inc=1)
            if self.mode == "F":
                Bdl = Buf("dmloc")
                P.op("sp", lambda e: e.dma_start(out=self.dm_loc, in_=self.ones[:]), reads=[B["ones"]], writes=[Bdl], dma=P.dma_sem())
                P.op("pool", lambda e: e.collective_compute("AllGather", ALU.bypass, replica_groups=[[0, 1, 2, 3], [4, 5, 6, 7]],
                                                            ins=[self.dm_loc], outs=[self.dm_all]),
                     reads=[Bdl], writes=[self.Bdummy], dma=P.dma_sem(), dma_inc=1)
        P.barrier()

    def layer_b(self, KTf_d, Vf_d, mask_d, pos_d):
        P, B, nc, sb = self.P, self.B, self.nc, self.sb
        adaA = self.adaA
        QT = sb("QT", [128, NCH, T], BF16)
        BQ = [[Buf("QT%d_%d" % (hd, m)) for m in range(4)] for hd in range(8)]
        mask = sb("mask", [128, 8, 128], BF16)
        lamt = sb("lamt", [128, 8], F32)
        lamb = sb("lamb", [128, 2], BF16)
        B["mask"] = Buf("mask")
        B["lamt"] = Buf("lamt")
        P.op("pool", lambda e: e.dma_start(out=mask[:], in_=mask_d), writes=[B["mask"]], dma=P.dma_sem())
        stage(10)
        lo = COLS["lam"][0]
        so = COLS["subln_g"][0]
        P.op("dve", lambda e: e.tensor_tensor(out=lamb[:, 0:1], in0=self.cols[:, lo:lo + 1], in1=self.cols[:, lo + 1:lo + 2], op=ALU.mult),
             reads=[B["cols"]], writes=[B["lamt"]])
        P.op("dve", lambda e: e.tensor_tensor(out=lamb[:, 1:2], in0=self.cols[:, lo + 2:lo + 3], in1=self.cols[:, lo + 3:lo + 4], op=ALU.mult),
             reads=[B["cols"]], writes=[B["lamt"]])
        P.op("pe", lambda e: e.matmul(self.ps[5][:, 0:2], lhsT=self.ones[:], rhs=lamb[:, 0:2], start=True, stop=True),
             reads=[B["lamt"], B["ones"]], writes=[self.Bps[5]])
        P.op("act", lambda e: e.activation(out=lamt[:, 2:4], in_=self.ps[5][:, 0:2], func=AF.Exp),
             reads=[self.Bps[5]], writes=[B["lamt"]])
        P.chain("dve", [
            lambda e: e.scalar_tensor_tensor(out=lamt[:, 4:5], in0=lamt[:, 2:3], scalar=LAM_INIT, in1=lamt[:, 3:4],
                                             op0=ALU.add, op1=ALU.subtract),
            lambda e: e.tensor_scalar(out=lamt[:, 5:6], in0=lamt[:, 4:5], scalar1=-1.0, scalar2=None, op0=ALU.mult),
            lambda e: e.tensor_scalar(out=lamt[:, 6:8], in0=self.cols[:, so:so + 2], scalar1=1.0 - LAM_INIT, scalar2=None, op0=ALU.mult),
        ], reads=[B["lamt"], B["cols"]], writes=[B["lamt"]])

        stage(11)
        self.norm_stats()
        if self.mode == "B":
            self.ada_group("b_ada", 0, adaA, B["adaA"], "b_ada_b")
            self.ada_group("b_ada", 1, adaA, B["adaA"], "b_ada_b")
        self.mk_gsc(0, "b_norm1_g", adaA[:, 16:32], B["adaA"])
        self.modulate(0, lambda c: adaA[:, c:c + 1], B["adaA"])
        stage(12)
        with ExitStack() as ph:
            self.cosF = ph.enter_context(self.sbt("cosF", [128, T], F32))
            self.sinS = ph.enter_context(self.sbt("sinS", [128, T], F32))
            self.rope_tables(pos_d)
            self.qn = ph.enter_context(self.sbt("qn", [128, 512], F32))
            B["qn"] = Buf("qn")
            qg = self.col("q_norm_g")
            for u in range(8):
                wv, wb = self.ws.take(("b_q", u))
                for fc in range(2):
                    hh = u * 2 + fc
                    for th in range(2):
                        bank = th

                        def fk(e, wv=wv, fc=fc, th=th, bank=bank):
                            for k in range(NCH):
                                ins = e.matmul(self.ps[bank][:], lhsT=wv[:, k, fc * 128:(fc + 1) * 128],
                                               rhs=self.hT[:, k, th * 512:(th + 1) * 512], start=(k == 0), stop=(k == NCH - 1))
                            return ins
                        P.op("pe", fk, reads=[wb, B["hT%d" % th]], writes=[self.Bps[bank]])
                        self.qk_head(bank, th, qg, QT[:, hh, th * 512:(th + 1) * 512], BQ[u][2 * th])
                        BQ[u][2 * th + 1].w = BQ[u][2 * th].w
                self.ws.release()
        P.barrier()
        stage(13)

        with ExitStack() as ph:
            ex = [ph.enter_context(self.sbt("kvx%d" % i, [128, 4096], BF16)) for i in range(1)]
            slots = [self.hT[:, 4 * i:4 * i + 4, :].rearrange("p c t -> p (c t)") for i in range(4)] + [e_[:] for e_ in ex]
            NS = len(slots)
            sbufs = [Buf("kvs%d" % i) for i in range(NS)]
            ssems = [P.dma_sem() for _ in range(NS)]
            plan = []
            fused = self.mode == "F"
            for hd in range(8):
                if fused:
                    plan.append(("ktf", KTf_d[2 * hd], hd))
                    plan.append(("ktf", KTf_d[2 * hd + 1], hd))
                    plan.append(("vf", (Vf_d[hd][0], Vf_d[hd][1]), hd))
                    plan.append(("vf", (Vf_d[hd][2], Vf_d[hd][3]), hd))
                else:
                    plan.append(("kt", KTf_d[2 * hd], hd))
                    plan.append(("kt", KTf_d[2 * hd + 1], hd))
                    plan.append(("v", Vf_d[hd, 0], hd))
                    plan.append(("v", Vf_d[hd, 1], hd))

            def kmap(kb):
                if not fused:
                    return kb * 128, kb // 16, kb % 16
                m8, r8 = kb // 8, kb % 8
                if r8 < 4:
                    j, sl_ = r8, 2 * m8
                else:
                    j, sl_ = 7 - r8, 2 * m8 + 1
                return j * 1024 + sl_ * 128, j // 2, (j % 2) * 8 + sl_
            st = {"issued": 0, "closed": 0}

            def kview(s, kind):
                if kind == "kt":
                    return slots[s]
                if kind == "ktf":
                    return slots[s].rearrange("p (j t) -> p j t", j=4)
                if kind == "vf":
                    return slots[s].rearrange("p (j s f) -> p j s f", j=2, s=8)
                return slots[s].rearrange("p (k f) -> p k f", k=16)

            def pump():
                while st["issued"] < len(plan) and st["issued"] - NS < st["closed"]:
                    i = st["issued"]
                    kind, src, phd = plan[i]
                    s = i % NS
                    dst = kview(s, kind)
                    if kind == "vf":
                        for jj in range(2):
                            P.op("sp", (lambda dst, src: (lambda e: e.dma_start(out=dst, in_=src)))(dst[:, jj], src[jj]),
                                 reads=[self.Bvall[phd], self.Bdummy], writes=[sbufs[s]], dma=ssems[s])
                    else:
                        P.op("sp", (lambda dst, src: (lambda e: e.dma_start(out=dst, in_=src)))(dst, src),
                             reads=([self.Bktall[phd], self.Bdummy] if kind == "ktf" else []),
                             writes=[sbufs[s]], dma=ssems[s])
                    st["issued"] += 1

            pT = [self.sqb[0][:, 0, :], self.sqb[0][:, 1, :], self.sqb[1][:, 0, :], self.sqb[1][:, 1, :]]
            BpT = [Buf("pT%d" % i) for i in range(4)]
            accs = [ph.enter_context(self.sbt("accs%d" % i, [128, 260], F32)) for i in range(4)]
            Baccs = [Buf("accs%d" % i) for i in range(4)]
            fin = ph.enter_context(self.sbt("fin", [128, 8], F32))
            Bfin = Buf("fin")
            onbs = [ph.enter_context(self.sbt("onb%d" % i, [128, 256], BF16)) for i in range(2)]
            Bonbs = [Buf("onb0"), Buf("onb1")]
            identb = ph.enter_context(self.sbt("identb", [128, 128], BF16))
            Bidb = Buf("identb")
            P.op("pool", lambda e: e.memset(identb[:], 1.0), writes=[Bidb])
            P.op("pool", lambda e: e.affine_select(out=identb[:], in_=identb[:], pattern=[[-1, 128]], compare_op=ALU.is_equal,
                                                   fill=0.0, base=0, channel_multiplier=1), reads=[Bidb], writes=[Bidb])
            t0, t1, t2 = self.tmpf
            scale = 128.0 ** -0.5
            npt = 0
            nst = 0
            ada_next = 16
            for hd in range(8):
                base = hd * 4
                pump()
                if hd == 0 and self.DBG:
                    if fused:
                        P.op("sp", lambda e: e.dma_start(out=self.dbgA[1], in_=self.kt_all[0]), reads=[self.Bktall[0]], dma=self.do)
                    P.op("sp", lambda e: e.dma_start(out=self.dbgK, in_=slots[0]), reads=[sbufs[0]], dma=self.do)
                    P.op("sp", lambda e: e.dma_start(out=self.dbgV, in_=slots[2]), reads=[sbufs[2]], dma=self.do)
                kts = [kview((base + i) % NS, "kt") for i in range(2)]
                vs = [kview((base + 2 + i) % NS, "v") for i in range(2)]
                kb_ = [sbufs[(base + i) % NS] for i in range(4)]
                steps = [(m, kb) for m in range(4) for kb in range(8 * m + 8)]
                info = {}
                SB = (0, 1, 6)

                def emit_S(idx, hd=hd, kts=kts, kb_=kb_):
                    nonlocal nst, npt
                    m, kb = steps[idx]
                    kc, vh, vb = kmap(kb)
                    full = kb < 8 * m + 4
                    off = 0 if full else 128
                    sbank = SB[nst % 3]
                    nst += 1
                    pi = npt % 4
                    npt += 1
                    info[idx] = (pi, full, vh, vb)
                    psS = self.ps[sbank]

                    def fs(e):
                        for sub in range(2):
                            ins = e.matmul(psS[:, sub * 256 + off:sub * 256 + 256],
                                           lhsT=kts[sub][:, kc:kc + 128],
                                           rhs=QT[:, 2 * hd + sub, 256 * m + off:256 * m + 256], start=True, stop=True)
                        return ins
                    P.op("pe", fs, reads=[kb_[0], kb_[1], BQ[hd][m]], writes=[self.Bps[sbank]])
                    src3 = psS[:].rearrange("p (s q) -> p s q", s=2)[:, :, off:256]
                    dst3 = pT[pi].rearrange("p (s q) -> p s q", s=2)[:, :, off:256]
                    P.op("act", lambda e: e.activation(out=dst3, in_=src3, func=AF.Exp, scale=scale),
                         reads=[self.Bps[sbank]], writes=[BpT[pi]])
                    if kb >= 8 * m:
                        par = 0 if full else 1
                        r = kb - 8 * m - 4 * par
                        moff = 0 if par == 0 else 128
                        for sub in range(2):
                            dm = pT[pi][:, sub * 256 + moff:sub * 256 + moff + 128]
                            P.op("dve", lambda e, dm=dm: e.tensor_tensor(
                                out=dm, in0=dm, in1=mask[:, par * 4 + r, :], op=ALU.mult),
                                reads=[BpT[pi], B["mask"]], writes=[BpT[pi]])

                def emit_PV(idx, hd=hd, vs=vs, kb_=kb_):
                    m, kb = steps[idx]
                    pi, full, vh, vb = info[idx]
                    vv = vs[vh][:, vb, :]

                    def fpv(e):
                        for sl in ((0, 1) if full else (1,)):
                            last = (8 * m + 3) if sl == 0 else (8 * m + 7)
                            for sub in range(2):
                                acc = self.ps[2 + sl * 2 + sub]
                                lh = pT[pi][:, sub * 256 + sl * 128:sub * 256 + sl * 128 + 128]
                                e.matmul(acc[:, 0:256], lhsT=lh, rhs=vv, start=(kb == 0), stop=(kb == last),
                                         skip_group_check=True)
                                ins = e.matmul(acc[:, 256:257], lhsT=lh, rhs=self.ones[:, 0:1], start=False, stop=(kb == last),
                                               skip_group_check=True)
                        return ins
                    P.op("pe", fpv, reads=[BpT[pi], kb_[2 + vh], B["ones"]],
                         writes=[self.Bps[2 + sl * 2 + sub] for sl in ((0, 1) if full else (1,)) for sub in range(2)])
                    if kb == 8 * m + 7:
                        finalize(m)
                        pending.append((idx + 6, m))

                def finalize(m, hd=hd):
                    for a in range(4):
                        P.op("act", lambda e, a=a: e.activation(out=accs[a][:, 0:257], in_=self.ps[2 + a][:, 0:257], func=AF.Copy),
                             reads=[self.Bps[2 + a]], writes=[Baccs[a]])
                    for sl in range(2):
                        aA, aB = accs[2 * sl], accs[2 * sl + 1]
                        onb, Bonb = onbs[sl], Bonbs[sl]
                        P.chain("dve", [
                            lambda e, aA=aA: e.reciprocal(out=fin[:, 0:1], in_=aA[:, 256:257]),
                            lambda e, aB=aB: e.reciprocal(out=fin[:, 1:2], in_=aB[:, 256:257]),
                            lambda e: e.tensor_tensor(out=fin[:, 2:3], in0=fin[:, 1:2], in1=lamt[:, 5:6], op=ALU.mult),
                        ], reads=[Baccs[2 * sl], Baccs[2 * sl + 1], Bfin, B["lamt"]], writes=[Bfin])
                        P.op("dve", lambda e, aB=aB: e.tensor_scalar(out=t0[:, 0:256], in0=aB[:, 0:256], scalar1=fin[:, 2:3], scalar2=None,
                                                                     op0=ALU.mult),
                             reads=[Baccs[2 * sl + 1], Bfin], writes=[B["tmpf0"]])
                        P.op("dve", lambda e, aA=aA: e.scalar_tensor_tensor(out=t1[:, 0:256], in0=aA[:, 0:256], scalar=fin[:, 0:1],
                                                                            in1=t0[:, 0:256], op0=ALU.mult, op1=ALU.add),
                             reads=[Baccs[2 * sl], Bfin, B["tmpf0"]], writes=[B["tmpf1"]])
                        P.op("act", lambda e: e.activation(out=t2[:, 0:256], in_=t1[:, 0:256], func=AF.Square, accum_out=fin[:, 3:4]),
                             reads=[B["tmpf1"], Bfin], writes=[B["tmpf2"], Bfin])
                        P.op("act", lambda e: e.activation(out=fin[:, 4:5], in_=fin[:, 3:4], func=AF.Sqrt, bias=self.eps_col[:, 0:1],
                                                           scale=1.0 / 256),
                             reads=[Bfin, B["consts"]], writes=[Bfin])
                        P.op("dve", lambda e: e.reciprocal(out=fin[:, 5:6], in_=fin[:, 4:5]), reads=[Bfin], writes=[Bfin])
                        P.op("dve", lambda e, onb=onb: e.tensor_scalar(out=onb[:], in0=t1[:, 0:256], scalar1=fin[:, 5:6], scalar2=None, op0=ALU.mult),
                             reads=[B["tmpf1"], Bfin], writes=[Bonb])

                def finalize_b(m, hd=hd):
                    def ftr(e):
                        for sl in range(2):
                            for c2 in range(2):
                                ins = e.transpose(out=self.psT[:, sl * 256 + c2 * 128:sl * 256 + (c2 + 1) * 128],
                                                  in_=onbs[sl][:, c2 * 128:(c2 + 1) * 128], identity=identb[:])
                        return ins
                    P.op("pe", ftr, reads=[Bonbs[0], Bonbs[1], Bidb], writes=[self.BpsT])
                    for c2 in range(2):
                        P.op("dve", lambda e, c2=c2: e.tensor_scalar(
                            out=QT[:, 2 * hd + c2, 256 * m:256 * m + 256].rearrange("p (s q) -> p s q", s=2),
                            in0=self.psT[:, 0:512].rearrange("p (s c q) -> p s c q", s=2, c=2)[:, :, c2, :],
                            scalar1=lamt[:, 6 + c2:7 + c2], scalar2=None, op0=ALU.mult),
                            reads=[self.BpsT, B["lamt"]], writes=[BQ[hd][m]])

                LA = 2
                pending = []
                for idx in range(len(steps) + LA):
                    if idx < len(steps):
                        emit_S(idx)
                    if idx >= LA:
                        emit_PV(idx - LA)
                    while pending and pending[0][0] <= idx - LA:
                        finalize_b(pending.pop(0)[1])
                while pending:
                    finalize_b(pending.pop(0)[1])
                st["closed"] = base + 4
                if STOP <= 14 + hd:
                    ada_next = 48
                    break
                for _ in range(4):
                    self.ada_unit("b_ada", ada_next, adaA, B["adaA"], "b_ada_b")
                    ada_next += 1
            assert ada_next == 48
        P.barrier()
        stage(21)
        allQ = [BQ[hd][m] for hd in range(8) for m in range(4)]
        self.proj_residual("b_o", lambda k, th: QT[:, k, th * 512:(th + 1) * 512], lambda th: allQ,
                           lambda m: adaA[:, 32 + m:33 + m], B["adaA"])
        stage(22)
        self.norm_stats()
        self.mk_gsc(1, "b_norm2_g", adaA[:, 64:80], B["adaA"])
        self.modulate(1, lambda c: adaA[:, 48 + c:49 + c], B["adaA"])
        self.ffn("b", lambda m: adaA[:, 80 + m:81 + m], B["adaA"])


_NC_CACHE = {}


def get_nc(mode):
    if mode not in _NC_CACHE:
        _NC_CACHE[mode] = Builder(mode).build()
    return _NC_CACHE[mode]


def host_cols(inp, b):
    cols = np.zeros((128, NCOL), np.float32)

    def put(name, arr):
        o, w = COLS[name]
        cols[:, o:o + w] = arr
    put("a_norm1_g", colv(inp["a_norm1_g"][0]))
    put("a_norm2_g", colv(inp["a_norm2_g"][0]))
    put("a_in_b_u", colv(inp["a_in_b"][0][:D]))
    put("a_ln_g", colv(inp["a_sgu_ln_g"][0]))
    put("a_ln_b", colv(inp["a_sgu_ln_b"][0]))
    put("kv_norm_g", colv(inp["kv_norm_g"]))
    put("b_norm1_g", colv(inp["b_norm1_g"][0]))
    put("b_norm2_g", colv(inp["b_norm2_g"][0]))
    put("a_ada_b", colv(inp["a_ada_b"][0]))
    put("kv_ada_b", colv(inp["kv_ada_b"]))
    put("b_ada_b", colv(inp["b_ada_b"][0]))
    put("k_norm_g", colv(inp["k_norm_g"]))
    put("q_norm_g", colv(inp["b_q_norm_g"][0]))
    put("subln_g", colv(inp["b_subln_g"][0]))
    put("lam", np.stack([inp["b_lambda_q1"][0], inp["b_lambda_k1"][0], inp["b_lambda_q2"][0], inp["b_lambda_k2"][0]], 1))
    inv_freq = 1.0 / (10000.0 ** (np.arange(0, 128, 2, dtype=np.float32) / 128.0))
    put("invf", (np.concatenate([inv_freq, inv_freq]) / (2 * np.pi)).astype(np.float32)[:, None])
    put("cT", colv(inp["c"][b]))
    return cols


def host_mask(j):
    m = np.zeros((128, 8, 128), np.float32)
    tri = (np.arange(128)[:, None] <= np.arange(128)[None, :]).astype(np.float32)
    for par in range(2):
        p = j if par == 0 else 7 - j
        for r in range(4):
            kbk = r if par == 0 else 4 + r
            if kbk < p:
                m[:, par * 4 + r, :] = 1.0
            elif kbk == p:
                m[:, par * 4 + r, :] = tri
    return m


def run_a(inp, ncores=8):
    nc = get_nc("A")
    shared = {
        "a_ada_w": tile_w(inp["a_ada_w"][0]), "a_in_w": tile_w(inp["a_in_w"][0]), "a_out_w": tile_w(inp["a_out_w"][0]),
        "a_ffn_wi": tile_w(inp["a_ffn_wi"][0]), "a_ffn_wo": tile_wo(inp["a_ffn_wo"][0]),
        "kv_ada_w": tile_w(inp["kv_ada_w"]), "kv_w": tile_w(inp["kv_w"]),
        "sgu_w": np.ascontiguousarray(inp["a_sgu_w"][0]),
        "rows": np.ascontiguousarray(np.stack([inp["a_in_b"][0][D:], inp["a_sgu_ln_b"][0], inp["a_sgu_b"][0].reshape(-1)]).astype(np.float32)),
    }
    in_maps = []
    for i in range(8):
        b, j = i // 4, i % 4
        idx = tok_index(j)
        m = dict(shared)
        m["xT"] = np.ascontiguousarray(inp["x"][b][idx].T)
        m["pos"] = np.ascontiguousarray(inp["positions"][b][idx][None, :]).astype(np.int32)
        m["cols"] = host_cols(inp, b)
        in_maps.append(m)
    in_maps = in_maps[:ncores]
    res = run_bass_kernel_spmd(nc, in_maps, core_ids=list(range(ncores)))
    return res.results


def run_b(inp, ra, ncores=8):
    nc = get_nc("B")
    shared = {
        "b_ada_w": tile_w(inp["b_ada_w"][0]), "b_q_w": tile_w(inp["b_q_w"][0]), "b_o_w": tile_w(inp["b_o_w"][0]),
        "b_ffn_wi": tile_w(inp["b_ffn_wi"][0]), "b_ffn_wo": tile_wo(inp["b_ffn_wo"][0]),
    }
    KTf, Vf = [], []
    for b in range(2):
        kt = np.zeros((16, 128, 4096), ml_dtypes.bfloat16)
        v = np.zeros((8, 4096, 256), ml_dtypes.bfloat16)
        for j in range(4):
            idx = tok_index(j)
            kt[:, :, idx] = ra[4 * b + j]["KT"]
            v[:, idx, :] = ra[4 * b + j]["V"]
        KTf.append(kt)
        Vf.append(np.ascontiguousarray(v.reshape(8, 2, 16, 128, 256).transpose(0, 1, 3, 2, 4)))
    in_maps = []
    for i in range(8):
        b, j = i // 4, i % 4
        idx = tok_index(j)
        m = dict(shared)
        m["xT"] = np.ascontiguousarray(ra[i]["outT"])
        m["pos"] = np.ascontiguousarray(inp["positions"][b][idx][None, :]).astype(np.int32)
        m["cols"] = host_cols(inp, b)
        m["KTf"] = KTf[b]
        m["Vf"] = Vf[b]
        m["mask"] = host_mask(j)
        in_maps.append(m)
    in_maps = in_maps[:ncores]
    res = run_bass_kernel_spmd(nc, in_maps, core_ids=list(range(ncores)))
    return res.results


def run_f(inp, ncores=8):
    nc = get_nc("F")
    shared = {
        "a_ada_w": tile_w(inp["a_ada_w"][0]), "a_in_w": tile_w(inp["a_in_w"][0]), "a_out_w": tile_w(inp["a_out_w"][0]),
        "a_ffn_wi": tile_w(inp["a_ffn_wi"][0]), "a_ffn_wo": tile_wo(inp["a_ffn_wo"][0]),
        "kv_ada_w": tile_w(inp["kv_ada_w"]), "kv_w": tile_w(inp["kv_w"]),
        "sgu_w": np.ascontiguousarray(inp["a_sgu_w"][0]),
        "rows": np.ascontiguousarray(np.stack([inp["a_in_b"][0][D:], inp["a_sgu_ln_b"][0], inp["a_sgu_b"][0].reshape(-1)]).astype(np.float32)),
        "b_ada_w": tile_w(inp["b_ada_w"][0]), "b_q_w": tile_w(inp["b_q_w"][0]), "b_o_w": tile_w(inp["b_o_w"][0]),
        "b_ffn_wi": tile_w(inp["b_ffn_wi"][0]), "b_ffn_wo": tile_wo(inp["b_ffn_wo"][0]),
    }
    in_maps = []
    for i in range(8):
        b, j = i // 4, i % 4
        idx = tok_index(j)
        m = dict(shared)
        m["xT"] = np.ascontiguousarray(inp["x"][b][idx].T)
        m["pos"] = np.ascontiguousarray(inp["positions"][b][idx][None, :]).astype(np.int32)
        m["cols"] = host_cols(inp, b)
        m["mask"] = host_mask(j)
        in_maps.append(m)
    in_maps = in_maps[:ncores]
    res = run_bass_kernel_spmd(nc, in_maps, core_ids=list(range(ncores)))
    return res.results


def kernel(**inp):
    inp = {k: np.asarray(v) for k, v in inp.items()}
    rf = run_f(inp)
    out = np.zeros((2, 4096, D), np.float32)
    for i in range(8):
        b, j = i // 4, i % 4
        out[b, tok_index(j), :] = rf[i]["outT"].T
    return out
```
